# Optimizing a Trainium2 kernel written in Bass

```python
import math
import jax, jax.numpy as jnp
from jax import lax
import numpy as np

D_MODEL = 2048
BATCH = 8
SEQ = 4096
DEPTH = 1

N_Q_HEADS = 32
N_KV_HEADS = 4
GROUP = N_Q_HEADS // N_KV_HEADS
HEAD_DIM = 64
ATTN_WIDTH = N_Q_HEADS * HEAD_DIM
KV_WIDTH = N_KV_HEADS * HEAD_DIM
WINDOW = 128
BLOCK = 128
NEG_INF = -1e30
N_BUCKETS = 32
MAX_DISTANCE = 128
LRU_WIDTH = D_MODEL
LRU_BLOCKS = 16
LRU_BLOCK_W = LRU_WIDTH // LRU_BLOCKS
CONV_WIDTH = 4
LRU_C = 8.0
D_FF = 4 * D_MODEL
EPS = 1e-6
IN_SPLITS = (LRU_WIDTH, LRU_WIDTH, ATTN_WIDTH, KV_WIDTH, KV_WIDTH, D_MODEL, D_MODEL)
IN_WIDTH = sum(IN_SPLITS)

kernel_name = "hybrid_rglru_swa_sink_sqrelu_adaln"


def _rms_norm(x, g):
    xf = x.astype(jnp.float32)
    y = xf * lax.rsqrt(jnp.mean(xf * xf, axis=-1, keepdims=True) + EPS)
    return (y * g.astype(jnp.float32)).astype(x.dtype)


def _modulate(h, shift, scale):
    return h * (1.0 + scale[:, None, :]) + shift[:, None, :]


def _causal_depthwise_conv(x, w, b):
    s = x.shape[1]
    xp = jnp.pad(x, ((0, 0), (CONV_WIDTH - 1, 0), (0, 0)))
    y = b
    for k in range(CONV_WIDTH):
        y = y + xp[:, k:k + s] * w[k]
    return y


def _rg_lru(x, wa, ba, wx, bx, lam):
    b_, s, w = x.shape
    xb = x.reshape(b_, s, LRU_BLOCKS, LRU_BLOCK_W)
    r = jax.nn.sigmoid(jnp.einsum("bshi,hij->bshj", xb, wa).reshape(b_, s, w) + ba)
    i = jax.nn.sigmoid(jnp.einsum("bshi,hij->bshj", xb, wx).reshape(b_, s, w) + bx)
    log_a = -LRU_C * r.astype(jnp.float32) * jax.nn.softplus(-lam.astype(jnp.float32))
    a = jnp.exp(log_a)
    mult = jnp.sqrt(-jnp.expm1(2.0 * log_a))
    mult = jnp.where(jnp.arange(s)[None, :, None] == 0, 1.0, mult)
    u = mult * (i * x).astype(jnp.float32)

    def step(h, au):
        a_t, u_t = au
        h = a_t * h + u_t
        return h, h

    h0 = jnp.zeros((b_, w), jnp.float32)
    _, hs = lax.scan(step, h0, (jnp.swapaxes(a, 0, 1), jnp.swapaxes(u, 0, 1)))
    return jnp.swapaxes(hs, 0, 1).astype(x.dtype)


def _t5_causal_bucket(rel):
    max_exact = N_BUCKETS // 2
    relf = jnp.maximum(rel, 1).astype(jnp.float32)
    large = max_exact + (jnp.log(relf / max_exact) / math.log(MAX_DISTANCE / max_exact)
                         * (N_BUCKETS - max_exact)).astype(jnp.int32)
    large = jnp.minimum(large, N_BUCKETS - 1)
    return jnp.where(rel < max_exact, rel, large)


def _band_bias_and_mask(rel_bias, n_blocks):
    qi = jnp.arange(BLOCK)[:, None]
    ki = jnp.arange(2 * BLOCK)[None, :]
    rel = qi + BLOCK - ki
    bucket = _t5_causal_bucket(jnp.maximum(rel, 0))
    bias = jnp.transpose(rel_bias[bucket], (2, 0, 1)).astype(jnp.float32)
    bias = bias.reshape(N_KV_HEADS, GROUP, BLOCK, 2 * BLOCK)
    kpos = jnp.arange(n_blocks)[:, None, None] * BLOCK - BLOCK + ki[None]
    valid = (kpos >= 0) & (rel[None] >= 0) & (rel[None] < WINDOW)
    return bias, valid


def _swa_sink_attention(q, k, v, sinks, rel_bias):
    b_, s, _ = q.shape
    n = s // BLOCK
    bias, valid = _band_bias_and_mask(rel_bias, n)
    q = q.reshape(b_, n, BLOCK, N_KV_HEADS, GROUP, HEAD_DIM)

    def band(t):
        t = t.reshape(b_, s, N_KV_HEADS, HEAD_DIM)
        tp = jnp.pad(t, ((0, 0), (BLOCK, 0), (0, 0), (0, 0)))
        tp = tp.reshape(b_, n + 1, BLOCK, N_KV_HEADS, HEAD_DIM)
        return jnp.concatenate([tp[:, :-1], tp[:, 1:]], axis=2)

    kb, vb = band(k), band(v)
    logits = jnp.einsum("bnqkgd,bnskd->bnkgqs", q, kb,
                        preferred_element_type=jnp.float32) * (HEAD_DIM ** -0.5)
    logits = jnp.where(valid[None, :, None, None], logits + bias[None, None], NEG_INF)
    sink = sinks.astype(jnp.float32).reshape(N_KV_HEADS, GROUP)[None, None, :, :, None, None]
    m = jnp.maximum(jnp.max(logits, axis=-1, keepdims=True), sink)
    e = jnp.exp(logits - m)
    p = e / (jnp.sum(e, axis=-1, keepdims=True) + jnp.exp(sink - m))
    o = jnp.einsum("bnkgqs,bnskd->bnqkgd", p.astype(vb.dtype), vb)
    return o.reshape(b_, s, ATTN_WIDTH)


def setup_inputs(seed: int = 0) -> dict:
    key = jax.random.key(seed)
    ks = jax.random.split(key, 24)
    f32 = jnp.float32
    L, D = DEPTH, D_MODEL

    def nrm(k, shape, scale):
        return jax.random.normal(k, shape, f32) * scale

    a_c = jax.random.uniform(ks[12], (L, LRU_WIDTH), f32, 0.9, 0.999)
    a0 = a_c ** (1.0 / LRU_C)
    lam = jnp.log(a0) - jnp.log1p(-a0)
    return {
        "x": nrm(ks[0], (BATCH, SEQ, D), 1.0),
        "c": nrm(ks[1], (BATCH, D), 1.0),
        "w_ada": nrm(ks[2], (L, D, 6 * D), 0.5 * D ** -0.5),
        "b_ada": nrm(ks[3], (L, 6 * D), 0.02),
        "norm1_g": 1.0 + nrm(ks[4], (L, D), 0.02),
        "w_in": nrm(ks[5], (L, D, IN_WIDTH), D ** -0.5),
        "conv_w": nrm(ks[6], (L, CONV_WIDTH, LRU_WIDTH), CONV_WIDTH ** -0.5),
        "conv_b": nrm(ks[7], (L, LRU_WIDTH), 0.02),
        "lru_wa": nrm(ks[8], (L, LRU_BLOCKS, LRU_BLOCK_W, LRU_BLOCK_W), LRU_BLOCK_W ** -0.5),
        "lru_ba": nrm(ks[9], (L, LRU_WIDTH), 0.02),
        "lru_wx": nrm(ks[10], (L, LRU_BLOCKS, LRU_BLOCK_W, LRU_BLOCK_W), LRU_BLOCK_W ** -0.5),
        "lru_bx": nrm(ks[11], (L, LRU_WIDTH), 0.02),
        "lru_lambda": lam,
        "w_lru_out": nrm(ks[13], (L, LRU_WIDTH, D), LRU_WIDTH ** -0.5),
        "w_attn_out": nrm(ks[14], (L, ATTN_WIDTH, D), ATTN_WIDTH ** -0.5),
        "attn_sinks": nrm(ks[15], (L, N_Q_HEADS), 1.0),
        "rel_bias": nrm(ks[16], (N_BUCKETS, N_Q_HEADS), 0.5),
        "w_out": nrm(ks[17], (L, D, D), D ** -0.5),
        "norm2_g": 1.0 + nrm(ks[18], (L, D), 0.02),
        "w_ff1": nrm(ks[19], (L, D, D_FF), D ** -0.5),
        "w_ff2": nrm(ks[20], (L, D_FF, D), D_FF ** -0.5),
        "final_g": 1.0 + nrm(ks[21], (D,), 0.02),
    }


def reference(x, c, w_ada, b_ada, norm1_g, w_in, conv_w, conv_b, lru_wa, lru_ba, lru_wx,
              lru_bx, lru_lambda, w_lru_out, w_attn_out, attn_sinks, rel_bias, w_out,
              norm2_g, w_ff1, w_ff2, final_g):
    split_idx = list(np.cumsum(IN_SPLITS)[:-1])
    c_act = jax.nn.silu(c)
    for l in range(DEPTH):
        mod = jnp.dot(c_act, w_ada[l]) + b_ada[l]
        shift1, scale1, gate1, shift2, scale2, gate2 = jnp.split(mod, 6, axis=-1)

        h = _modulate(_rms_norm(x, norm1_g[l]), shift1, scale1)
        proj = jnp.einsum("bsd,de->bse", h, w_in[l])
        lru_x, lru_gate, q, k, v, g_a, g_b = jnp.split(proj, split_idx, axis=-1)

        xc = _causal_depthwise_conv(lru_x, conv_w[l], conv_b[l])
        rec = _rg_lru(xc, lru_wa[l], lru_ba[l], lru_wx[l], lru_bx[l], lru_lambda[l])
        y_a = jnp.einsum("bsw,wd->bsd", rec * jax.nn.gelu(lru_gate, approximate=True),
                         w_lru_out[l])

        att = _swa_sink_attention(q, k, v, attn_sinks[l], rel_bias)
        y_b = jnp.einsum("bsw,wd->bsd", att, w_attn_out[l])

        merged = jax.nn.sigmoid(g_a) * y_a + jax.nn.sigmoid(g_b) * y_b
        x = x + gate1[:, None, :] * jnp.einsum("bsd,de->bse", merged, w_out[l])

        h2 = _modulate(_rms_norm(x, norm2_g[l]), shift2, scale2)
        ff = jnp.square(jax.nn.relu(jnp.einsum("bsd,df->bsf", h2, w_ff1[l])))
        x = x + gate2[:, None, :] * jnp.einsum("bsf,fd->bsd", ff, w_ff2[l])

    return _rms_norm(x, final_g)
```

```python
import contextlib
import math
import numpy as np
import concourse.bass as bass
import concourse.mybir as mybir
from concourse.bass_utils import run_bass_kernel_spmd

F32 = mybir.dt.float32
BF16 = mybir.dt.bfloat16
ALU = mybir.AluOpType
ACT = mybir.ActivationFunctionType
AX = mybir.AxisListType

D = 2048
S = 4096
T = 512
NCH = 16
NHEAD = 32
EPS = 1e-6
NADA = 48
NEG = -30000.0
PC_C, PC_BADA, PC_G1, PC_G2, PC_FG, PC_CW, PC_CB, PC_BA, PC_BX, PC_LAM, PC_N = 0, 16, 112, 128, 144, 160, 224, 240, 256, 272, 288
DV_SH1, DV_SC1, DV_GT1, DV_SH2, DV_SC2, DV_GT2, DV_A1, DV_A2, DV_CA, DV_CA2, DV_NBA, DV_NBX, DV_N = (
    0, 16, 32, 48, 64, 80, 96, 112, 128, 144, 160, 176, 192)
OFF_LX, OFF_LG, OFF_Q, OFF_K, OFF_V, OFF_GA, OFF_GB = 0, 2048, 4096, 6144, 6400, 6656, 8704


def block_specs():
    sp = []
    for ob in range(NADA):
        sp.append(("w_ada", 0, list(range(ob * 256, ob * 256 + 256))))
    for c in range(16):
        sp.append(("w_in", 0, list(range(OFF_LX + c * 128, OFF_LX + c * 128 + 128)) +
                   list(range(OFF_LG + c * 128, OFF_LG + c * 128 + 128))))
    for g in range(4):
        sp.append(("w_in", 0, list(range(OFF_Q + g * 512, OFF_Q + g * 512 + 256))))
        sp.append(("w_in", 0, list(range(OFF_Q + g * 512 + 256, OFF_Q + g * 512 + 512))))
        kc = list(range(OFF_K + g * 64, OFF_K + g * 64 + 64))
        sp.append(("w_in", 0, kc + kc + list(range(OFF_V + g * 64, OFF_V + g * 64 + 64)) + [-1] * 64))
    for j2 in range(8):
        sp.append(("w_in", 0, list(range(OFF_GA + j2 * 256, OFF_GA + j2 * 256 + 256))))
        sp.append(("w_lru_out", 0, list(range(j2 * 256, j2 * 256 + 256))))
        sp.append(("w_in", 0, list(range(OFF_GB + j2 * 256, OFF_GB + j2 * 256 + 256))))
        sp.append(("w_attn_out", 0, list(range(j2 * 256, j2 * 256 + 256))))
    for j2 in range(8):
        sp.append(("w_out", 0, list(range(j2 * 256, j2 * 256 + 256))))
    for qf in range(4):
        for b in range(8):
            sp.append(("w_ff1", 0, list(range(qf * 2048 + b * 256, qf * 2048 + b * 256 + 256))))
        for j2 in range(8):
            sp.append(("w_ff2", qf * 2048, list(range(j2 * 256, j2 * 256 + 256))))
    return sp


NBLK = NADA + 132


class Buf:
    __slots__ = ("name", "lw", "rd")

    def __init__(self, name):
        self.name = name
        self.lw = None
        self.rd = {}


class Stream:
    def __init__(self, name):
        self.name = name
        self.ops = []
        self.count = 0
        self.waited = {}


class KB:
    ENGS = ("pe", "act", "dve", "pool", "sp")

    def __init__(self, nc):
        self.nc = nc
        self.st = {n: Stream(n) for n in self.ENGS}
        self.dma_sems = {}
        self.sem_handles = {}

    def _need(self, s, ev, waits):
        if ev is None:
            return
        key, val = ev
        if s.waited.get(key, 0) >= val:
            return
        s.waited[key] = val
        waits.append((key, val))

    def op(self, eng, fn, reads=(), writes=(), signal=True):
        s = self.st[eng]
        waits = []
        for b in reads:
            if b.lw is not None:
                self._need(s, b.lw, waits)
        for b in writes:
            if b.lw is not None and b.lw[0] != eng:
                self._need(s, b.lw, waits)
            for k, v in b.rd.items():
                if k != eng:
                    self._need(s, (k, v), waits)
        keep = []
        for (k, v) in waits:
            if k == eng and v > s.count:
                s.waited[k] = s.count
                continue
            keep.append((k, v))
        nxt = s.count + 1
        ev = (eng, nxt)
        for b in writes:
            b.lw = ev
            b.rd = {}
        for b in reads:
            if b.rd.get(eng, 0) < nxt:
                b.rd[eng] = nxt
        if signal:
            s.count = nxt
        s.ops.append((keep, fn, eng if signal else None, 1))

    def dma(self, eng, slot, out, in_, reads=(), writes=()):
        s = self.st[eng]
        waits = []
        for b in reads:
            self._need(s, b.lw, waits)
        for b in writes:
            self._need(s, b.lw, waits)
            for k, v in b.rd.items():
                self._need(s, (k, v), waits)
        key = "dma_" + slot
        val = self.dma_sems.get(key, 0) + 16
        self.dma_sems[key] = val
        ev = (key, val)
        for b in writes:
            b.lw = ev
            b.rd = {}
        for b in reads:
            b.rd[key] = val

        def fn(e, out=out, in_=in_):
            return e.dma_start(out=out, in_=in_)
        s.ops.append((waits, fn, key, 16))

    def final_wait(self, eng):
        s = self.st[eng]
        waits = []
        for n in self.ENGS:
            if n != eng and self.st[n].count > 0:
                self._need(s, (n, self.st[n].count), waits)
        for k, v in self.dma_sems.items():
            self._need(s, (k, v), waits)
        s.ops.append((waits, None, None, 0))

    def emit(self):
        nc = self.nc
        with contextlib.ExitStack() as es:
            keys = list(self.ENGS) + list(self.dma_sems.keys())
            for k in keys:
                self.sem_handles[k] = es.enter_context(nc.semaphore("s_" + k))
            block = es.enter_context(nc.Block())
            H = self.sem_handles

            def run(stream):
                def body(e):
                    for waits, fn, inc, amt in stream.ops:
                        for k, v in waits:
                            e.wait_ge(H[k], v)
                        if fn is None:
                            continue
                        ins = fn(e)
                        if inc is not None:
                            ins.then_inc(H[inc], amt)
                return body

            block.tensor(run(self.st["pe"]))
            block.scalar(run(self.st["act"]))
            block.vector(run(self.st["dve"]))
            block.gpsimd(run(self.st["pool"]))
            block.sync(run(self.st["sp"]))


def build(NT=8):
    nc = bass.Bass("TRN2", target_bir_lowering=False)
    xT_d = nc.dram_tensor("xT", [D, S], F32, kind="ExternalInput").ap()
    wblk_d = nc.dram_tensor("wblk", [NBLK, 128, 4096], F32, kind="ExternalInput").ap()
    pp_d = nc.dram_tensor("pp", [128, PC_N], F32, kind="ExternalInput").ap()
    lruw_d = nc.dram_tensor("lruw", [128, 16 * 256], F32, kind="ExternalInput").ap()
    biasg_d = nc.dram_tensor("biasg", [128, NHEAD * 256], F32, kind="ExternalInput").ap()
    maskc_d = nc.dram_tensor("maskc", [128, 256], F32, kind="ExternalInput").ap()
    sinks_d = nc.dram_tensor("sinks", [1, NHEAD], F32, kind="ExternalInput").ap()
    idn_d = nc.dram_tensor("idn", [128, 128], F32, kind="ExternalInput").ap()
    out_d = nc.dram_tensor("outT", [D, S], F32, kind="ExternalOutput").ap()
    wscr_d = nc.dram_tensor("wscr", [NBLK, 128, 4096], BF16).ap()

    with contextlib.ExitStack() as es:
        def sb(n, s, d):
            return es.enter_context(nc.sbuf_tensor(n, s, d))

        def ps(n, s, d):
            return es.enter_context(nc.psum_tensor(n, s, d))

        wring = sb("wring", [128, 4, 4096], BF16)
        xT = sb("xTs", [128, 16, 512], F32)
        hT = sb("hTs", [128, 16, 512], BF16)
        big = sb("big", [128, 48, 512], BF16)
        bm = sb("bm", [128, NHEAD, 256], BF16)
        lruw = sb("lruws", [128, 16, 256], BF16)
        pp = sb("pps", [128, PC_N], F32)
        dv = sb("dvs", [128, DV_N], F32)
        cact = sb("cact", [128, 16], BF16)
        ctmp = sb("ctmp", [128, 16], F32)
        ident_b = sb("ident_b", [128, 128], BF16)
        ones_f = sb("ones_f", [128, 128], F32)
        sinks_bc = sb("sinks_bc", [128, NHEAD], F32)
        nsink = sb("nsink", [128, NHEAD], F32)
        maskc = sb("maskcs", [128, 256], F32)
        hist = sb("hist", [128, 16, 4], F32)
        hstate = sb("hstate", [128, 16], F32)
        kbuf = sb("kbuf", [128, 4, 640], BF16)
        vbuf = sb("vbuf", [128, 4, 5, 64], BF16)
        sqt = sb("sqt", [128, 2, 512], F32)
        lnv = sb("lnv", [128, 512], F32)
        rstd = sb("rstd", [128, 512], F32)
        lxb = sb("lxb", [128, 516], F32)
        xc = sb("xc", [128, 512], F32)
        xcb = sb("xcb", [128, 512], BF16)
        tr = sb("tr", [128, 512], F32)
        ta = sb("ta", [128, 512], F32)
        ti = sb("ti", [128, 512], F32)
        rec = sb("rec", [128, 512], F32)
        gxs = sb("gxs", [128, 512], F32)
        gsq = sb("gsq", [128, 512], F32)
        gt = sb("gt", [128, 512], F32)
        sga = sb("sga", [128, 2, 512], F32)
        sgb = sb("sgb", [128, 2, 512], F32)
        ntmp = sgb
        ost = sqt
        rtmp = sga
        pbuf = sb("pbuf", [128, 2, 2, 256], BF16)
        pTs = sb("pTs", [128, 2, 512], BF16)
        mx = sb("mx", [128, 2, 2], F32)
        negm = sb("negm", [128, 8], F32)
        rsum = sb("rsum", [128, 8], F32)
        etmp = sb("etmp", [128, 8], F32)
        rden = sb("rden", [128, 8], F32)
        atok = sb("atok", [128, 512], BF16)

        pmm = [ps("pmm%d" % i, [128, 512], F32) for i in range(4)]
        pS = [ps("pS%d" % i, [128, 512], F32) for i in range(2)]
        pT = ps("pT", [128, 1024], BF16)
        pO = ps("pO", [128, 512], F32)

        k = KB(nc)
        b_wr = [Buf("wr%d" % i) for i in range(4)]
        b_scr = [Buf("scr%d" % i) for i in range(NBLK)]
        b_x = [Buf("x%d" % i) for i in range(16)]
        b_h = [Buf("h%d" % i) for i in range(16)]
        b_big = [Buf("big%d" % i) for i in range(48)]
        b_pmm = [Buf("pmm%d" % i) for i in range(4)]
        b_pS = [Buf("pS%d" % i) for i in range(2)]
        b_pT = [Buf("pT%d" % i) for i in range(2)]
        b_pO = Buf("pO")
        names = ["bm", "lruw", "pp", "dv", "cact", "ctmp", "ident", "ones", "sinks", "nsink", "maskc", "hist",
                 "hstate", "lnv", "rstd", "lxb", "xc", "xcb", "tr", "ta", "ti", "rec", "gxs", "gsq", "gt",
                 "negm", "rsum", "etmp", "rden", "atok"]
        B = {n: Buf(n) for n in names}
        b_kb = [Buf("kb%d" % i) for i in range(4)]
        b_vb = [Buf("vb%d" % i) for i in range(4)]
        b_sqt = [Buf("sqt%d" % i) for i in range(2)]
        b_sga = [Buf("sga%d" % i) for i in range(2)]
        b_sgb = [Buf("sgb%d" % i) for i in range(2)]
        b_ntmp = b_sgb
        b_ost = b_sqt
        b_rtmp = b_sga
        b_pb = [Buf("pb%d" % i) for i in range(2)]
        b_pTs = [Buf("pTs%d" % i) for i in range(2)]
        b_mx = [Buf("mx%d" % i) for i in range(2)]

        def dvc(col, c=0, n=1):
            return dv[:, col + c: col + c + n]

        def ppc(col, c=0, n=1):
            return pp[:, col + c: col + c + n]

        state = {"seq": 0, "pm": 0}
        sched = []

        def issue_load(seq, blk, first):
            s = seq % 4
            if first:
                k.dma("pool", "w%d" % s, wring[:, s, :], wblk_d[blk], writes=[b_wr[s]])
                if blk >= NADA and NT > 1:
                    k.dma("sp", "ws%d" % s, wscr_d[blk], wring[:, s, :], reads=[b_wr[s]], writes=[b_scr[blk]])
            else:
                k.dma("sp", "w%d" % s, wring[:, s, :], wscr_d[blk], reads=[b_scr[blk]], writes=[b_wr[s]])

        full = [(b, True) for b in range(NADA)]
        for tt in range(NT):
            full += [(b, tt == 0) for b in range(NADA, NBLK)]
        PRE = 3
        issued = {"n": 0}

        def next_block():
            seq = state["seq"]
            while issued["n"] < min(len(full), seq + PRE + 1):
                n = issued["n"]
                issue_load(n, full[n][0], full[n][1])
                issued["n"] += 1
            state["seq"] = seq + 1
            return seq % 4

        def next_pmm():
            i = state["pm"] % 4
            state["pm"] += 1
            return i

        def mm_group(pi, pairs, reads, ncols=512, col0=0, sig_all=False):
            n = len(pairs)
            for idx, (l, r) in enumerate(pairs):
                last = idx == n - 1
                k.op("pe", lambda e, l=l, r=r, idx=idx, last=last: e.matmul(
                    pmm[pi][:, col0:col0 + ncols], l, r, start=(idx == 0), stop=last),
                    reads=reads[idx] if isinstance(reads, dict) else reads,
                    writes=[b_pmm[pi]], signal=(last or sig_all))

        k.dma("sp", "c0", pp[:], pp_d, writes=[B["pp"]])
        k.dma("sp", "c1", maskc[:], maskc_d, writes=[B["maskc"]])
        k.dma("sp", "c2", sinks_bc[:], sinks_d.broadcast_to([128, NHEAD]), writes=[B["sinks"]])
        k.dma("pool", "c3", ident_b[:], idn_d, writes=[B["ident"]])
        k.dma("pool", "c4", lruw[:].rearrange("p a b -> p (a b)"), lruw_d, writes=[B["lruw"]])
        xflat = xT[:].rearrange("p a b -> p (a b)")
        k.dma("sp", "c5", xflat, biasg_d, writes=b_x)
        k.op("dve", lambda e: e.memset(ones_f[:], 1.0), writes=[B["ones"]])
        k.op("dve", lambda e: e.memset(hist[:].rearrange("p a b -> p (a b)"), 0.0), writes=[B["hist"]])
        k.op("dve", lambda e: e.memset(hstate[:], 0.0), writes=[B["hstate"]])
        k.op("dve", lambda e: e.memset(kbuf[:].rearrange("p a b -> p (a b)"), 0.0), writes=b_kb)
        k.op("dve", lambda e: e.memset(vbuf[:].rearrange("p a b c -> p (a b c)"), 0.0), writes=b_vb)
        for h in range(NHEAD):
            k.op("dve", lambda e, h=h: e.tensor_tensor(bm[:, h, :], xflat[:, h * 256:(h + 1) * 256], maskc[:], ALU.add),
                 reads=b_x + [B["maskc"]], writes=[B["bm"]])
        k.op("dve", lambda e: e.tensor_scalar(nsink[:], sinks_bc[:], -1.0, None, ALU.mult),
             reads=[B["sinks"]], writes=[B["nsink"]])
        k.op("act", lambda e: e.activation(ctmp[:], ppc(PC_C, 0, 16), ACT.Exp, scale=-1.0),
             reads=[B["pp"]], writes=[B["ctmp"]])
        k.op("dve", lambda e: e.tensor_scalar(ctmp[:], ctmp[:], 1.0, None, ALU.add), reads=[B["ctmp"]], writes=[B["ctmp"]])
        k.op("dve", lambda e: e.reciprocal(ctmp[:], ctmp[:]), reads=[B["ctmp"]], writes=[B["ctmp"]])
        k.op("dve", lambda e: e.tensor_tensor(cact[:], ctmp[:], ppc(PC_C, 0, 16), ALU.mult),
             reads=[B["ctmp"], B["pp"]], writes=[B["cact"]])
        for ob in range(NADA):
            s = next_block()
            for i in range(2):
                oc = ob * 2 + i
                for kc in range(16):
                    k.op("pe", lambda e, s=s, i=i, kc=kc, oc=oc: e.matmul(
                        pO[:, oc:oc + 1], wring[:, s, kc * 256 + i * 128: kc * 256 + i * 128 + 128],
                        cact[:, kc:kc + 1], start=(kc == 0), stop=(kc == 15)),
                        reads=[b_wr[s], B["cact"]], writes=[b_pO], signal=(kc == 15))
        k.op("dve", lambda e: e.tensor_tensor(dv[:, 0:96], pO[:, 0:96], ppc(PC_BADA, 0, 96), ALU.add),
             reads=[b_pO, B["pp"]], writes=[B["dv"]])
        k.op("dve", lambda e: e.scalar_tensor_tensor(dvc(DV_A1, 0, 16), dvc(DV_SC1, 0, 16), 1.0, ppc(PC_G1, 0, 16),
                                                     ALU.add, ALU.mult), reads=[B["dv"], B["pp"]], writes=[B["dv"]])
        k.op("dve", lambda e: e.scalar_tensor_tensor(dvc(DV_A2, 0, 16), dvc(DV_SC2, 0, 16), 1.0, ppc(PC_G2, 0, 16),
                                                     ALU.add, ALU.mult), reads=[B["dv"], B["pp"]], writes=[B["dv"]])
        k.op("act", lambda e: e.activation(ctmp[:], ppc(PC_LAM, 0, 16), ACT.Exp, scale=-1.0),
             reads=[B["pp"], B["cact"]], writes=[B["ctmp"]])
        k.op("act", lambda e: e.activation(ctmp[:], ctmp[:], ACT.Ln, bias=1.0), reads=[B["ctmp"]], writes=[B["ctmp"]])
        k.op("dve", lambda e: e.tensor_scalar(dvc(DV_CA, 0, 16), ctmp[:], -8.0, None, ALU.mult),
             reads=[B["ctmp"]], writes=[B["dv"]])
        k.op("dve", lambda e: e.tensor_scalar(dvc(DV_CA2, 0, 16), ctmp[:], -16.0, None, ALU.mult),
             reads=[B["ctmp"]], writes=[B["dv"]])
        k.op("dve", lambda e: e.tensor_scalar(dvc(DV_NBA, 0, 16), ppc(PC_BA, 0, 16), -1.0, None, ALU.mult),
             reads=[B["pp"]], writes=[B["dv"]])
        k.op("dve", lambda e: e.tensor_scalar(dvc(DV_NBX, 0, 16), ppc(PC_BX, 0, 16), -1.0, None, ALU.mult),
             reads=[B["pp"]], writes=[B["dv"]])

        def rmsnorm_stats():
            pi = next_pmm()
            for c in range(16):
                q = c % 2
                k.op("act", lambda e, c=c, q=q: e.activation(sqt[:, q, :], xT[:, c, :], ACT.Square),
                     reads=[b_x[c]], writes=[b_sqt[q]])
                k.op("pe", lambda e, c=c, q=q, pi=pi: e.matmul(pmm[pi][:], ones_f[:], sqt[:, q, :],
                                                              start=(c == 0), stop=(c == 15)),
                     reads=[b_sqt[q], B["ones"]], writes=[b_pmm[pi]], signal=True)
            k.op("act", lambda e, pi=pi: e.activation(lnv[:], pmm[pi][:], ACT.Ln, bias=EPS, scale=1.0 / D),
                 reads=[b_pmm[pi]], writes=[B["lnv"]])
            k.op("act", lambda e: e.activation(rstd[:], lnv[:], ACT.Exp, scale=-0.5),
                 reads=[B["lnv"]], writes=[B["rstd"]])

        def modulate(acol, scol):
            for c in range(16):
                q = c % 2
                k.op("dve", lambda e, c=c, q=q: e.tensor_tensor(ntmp[:, q, :], xT[:, c, :], rstd[:], ALU.mult),
                     reads=[b_x[c], B["rstd"]], writes=[b_ntmp[q]])
                k.op("act", lambda e, c=c, q=q: e.activation(hT[:, c, :], ntmp[:, q, :], ACT.Identity,
                                                             bias=dvc(scol, c), scale=dvc(acol, c)),
                     reads=[b_ntmp[q], B["dv"]], writes=[b_h[c]])

        def proj_pairs(s, i, src, srcbufs=None):
            return [(wring[:, s, kc * 256 + i * 128: kc * 256 + i * 128 + 128], src(kc)) for kc in range(16)]

        C0 = math.sqrt(2.0 / math.pi)
        C1 = 0.044715

        for tt in range(NT):
            t0 = tt * T
            for c in range(16):
                k.dma("sp", "x%d" % c, xT[:, c, :], xT_d[c * 128:(c + 1) * 128, t0:t0 + T], writes=[b_x[c]])
            rmsnorm_stats()
            modulate(DV_A1, DV_SH1)

            for c in range(16):
                s = next_block()
                p_lx = next_pmm()
                mm_group(p_lx, proj_pairs(s, 0, lambda kc: hT[:, kc, :]), reads=[b_wr[s]] + b_h)
                p_lg = next_pmm()
                mm_group(p_lg, proj_pairs(s, 1, lambda kc: hT[:, kc, :]), reads=[b_wr[s]] + b_h)
                k.op("dve", lambda e, c=c: e.tensor_copy(lxb[:, 0:3], hist[:, c, 0:3]),
                     reads=[B["hist"]], writes=[B["lxb"]])
                k.op("act", lambda e, p=p_lx: e.activation(lxb[:, 3:515], pmm[p][:], ACT.Copy),
                     reads=[b_pmm[p_lx]], writes=[B["lxb"]])
                k.op("dve", lambda e, c=c: e.tensor_copy(hist[:, c, 0:3], lxb[:, 512:515]),
                     reads=[B["lxb"]], writes=[B["hist"]])
                k.op("dve", lambda e, c=c: e.tensor_scalar(xc[:], lxb[:, 3:515], ppc(PC_CW, 3 * 16 + c), ppc(PC_CB, c),
                                                          ALU.mult, ALU.add),
                     reads=[B["lxb"], B["pp"]], writes=[B["xc"]])
                for kk in (2, 1, 0):
                    k.op("dve", lambda e, c=c, kk=kk: e.scalar_tensor_tensor(
                        xc[:], lxb[:, kk:kk + 512], ppc(PC_CW, kk * 16 + c), xc[:], ALU.mult, ALU.add),
                        reads=[B["lxb"], B["pp"], B["xc"]], writes=[B["xc"]])
                k.op("act", lambda e: e.activation(xcb[:], xc[:], ACT.Copy), reads=[B["xc"]], writes=[B["xcb"]])
                p_r = next_pmm()
                k.op("pe", lambda e, c=c, p=p_r: e.matmul(pmm[p][:], lruw[:, c, 0:128], xcb[:], start=True, stop=True),
                     reads=[B["lruw"], B["xcb"]], writes=[b_pmm[p_r]])
                p_i = next_pmm()
                k.op("pe", lambda e, c=c, p=p_i: e.matmul(pmm[p][:], lruw[:, c, 128:256], xcb[:], start=True, stop=True),
                     reads=[B["lruw"], B["xcb"]], writes=[b_pmm[p_i]])
                k.op("act", lambda e, c=c, p=p_r: e.activation(tr[:], pmm[p][:], ACT.Exp, bias=dvc(DV_NBA, c), scale=-1.0),
                     reads=[b_pmm[p_r], B["dv"]], writes=[B["tr"]])
                k.op("act", lambda e, c=c, p=p_i: e.activation(ti[:], pmm[p][:], ACT.Exp, bias=dvc(DV_NBX, c), scale=-1.0),
                     reads=[b_pmm[p_i], B["dv"]], writes=[B["ti"]])
                k.op("act", lambda e: e.activation(tr[:], tr[:], ACT.Ln, bias=1.0), reads=[B["tr"]], writes=[B["tr"]])
                k.op("act", lambda e: e.activation(tr[:], tr[:], ACT.Exp, scale=-1.0), reads=[B["tr"]], writes=[B["tr"]])
                k.op("act", lambda e, c=c: e.activation(ta[:], tr[:], ACT.Exp, scale=dvc(DV_CA, c)),
                     reads=[B["tr"], B["dv"]], writes=[B["ta"]])
                k.op("act", lambda e, c=c: e.activation(tr[:], tr[:], ACT.Exp, scale=dvc(DV_CA2, c)),
                     reads=[B["tr"], B["dv"]], writes=[B["tr"]])
                k.op("act", lambda e: e.activation(tr[:], tr[:], ACT.Ln, bias=1.0000002, scale=-1.0),
                     reads=[B["tr"]], writes=[B["tr"]])
                k.op("act", lambda e: e.activation(tr[:], tr[:], ACT.Exp, scale=0.5), reads=[B["tr"]], writes=[B["tr"]])
                k.op("dve", lambda e: e.tensor_scalar(ti[:], ti[:], 1.0, None, ALU.add), reads=[B["ti"]], writes=[B["ti"]])
                k.op("dve", lambda e: e.reciprocal(ti[:], ti[:]), reads=[B["ti"]], writes=[B["ti"]])
                k.op("dve", lambda e: e.tensor_tensor(ti[:], ti[:], xc[:], ALU.mult), reads=[B["ti"], B["xc"]], writes=[B["ti"]])
                if tt == 0:
                    k.op("dve", lambda e: e.memset(tr[:, 0:1], 1.0), reads=[B["tr"]], writes=[B["tr"]])
                k.op("dve", lambda e: e.tensor_tensor(ti[:], ti[:], tr[:], ALU.mult), reads=[B["ti"], B["tr"]], writes=[B["ti"]])
                k.op("dve", lambda e, c=c: e.tensor_tensor_scan(rec[:], ta[:], ti[:], hstate[:, c:c + 1], ALU.mult, ALU.add),
                     reads=[B["ta"], B["ti"], B["hstate"]], writes=[B["rec"]])
                k.op("dve", lambda e, c=c: e.tensor_copy(hstate[:, c:c + 1], rec[:, 511:512]),
                     reads=[B["rec"]], writes=[B["hstate"]])
                k.op("act", lambda e, p=p_lg: e.activation(gxs[:], pmm[p][:], ACT.Copy), reads=[b_pmm[p_lg]], writes=[B["gxs"]])
                k.op("act", lambda e, p=p_lg: e.activation(gsq[:], pmm[p][:], ACT.Square), reads=[b_pmm[p_lg]], writes=[B["gsq"]])
                k.op("dve", lambda e: e.tensor_scalar(gt[:], gsq[:], C1, 1.0, ALU.mult, ALU.add), reads=[B["gsq"]], writes=[B["gt"]])
                k.op("dve", lambda e: e.tensor_tensor(gt[:], gt[:], gxs[:], ALU.mult), reads=[B["gt"], B["gxs"]], writes=[B["gt"]])
                k.op("act", lambda e: e.activation(gt[:], gt[:], ACT.Exp, scale=-2.0 * C0), reads=[B["gt"]], writes=[B["gt"]])
                k.op("act", lambda e: e.activation(gt[:], gt[:], ACT.Ln, bias=1.0), reads=[B["gt"]], writes=[B["gt"]])
                k.op("act", lambda e: e.activation(gt[:], gt[:], ACT.Exp, scale=-1.0), reads=[B["gt"]], writes=[B["gt"]])
                k.op("dve", lambda e: e.tensor_tensor(gt[:], gt[:], gxs[:], ALU.mult), reads=[B["gt"], B["gxs"]], writes=[B["gt"]])
                k.op("dve", lambda e, c=c: e.tensor_tensor(big[:, c, :], gt[:], rec[:], ALU.mult),
                     reads=[B["gt"], B["rec"]], writes=[b_big[c]])

            for g in range(4):
                for qb in range(2):
                    s = next_block()
                    for i in range(2):
                        ci = 32 + 4 * g + 2 * qb + i
                        p = next_pmm()
                        mm_group(p, proj_pairs(s, i, lambda kc: hT[:, kc, :]), reads=[b_wr[s]] + b_h)
                        k.op("act", lambda e, p=p, ci=ci: e.activation(big[:, ci, :], pmm[p][:], ACT.Copy, scale=0.125),
                             reads=[b_pmm[p]], writes=[b_big[ci]])
                s = next_block()
                p = next_pmm()
                mm_group(p, proj_pairs(s, 0, lambda kc: hT[:, kc, :]), reads=[b_wr[s]] + b_h)
                k.op("dve", lambda e, p=p, g=g: e.tensor_copy(kbuf[:, g, 128:640], pmm[p][:]),
                     reads=[b_pmm[p]], writes=[b_kb[g]])
                p = next_pmm()
                for tb in range(4):
                    for kc in range(16):
                        k.op("pe", lambda e, p=p, s=s, tb=tb, kc=kc: e.matmul(
                            pmm[p][:, tb * 64:(tb + 1) * 64], hT[:, kc, tb * 128:(tb + 1) * 128],
                            wring[:, s, kc * 256 + 128: kc * 256 + 192], start=(kc == 0), stop=(kc == 15)),
                            reads=[b_wr[s]] + b_h, writes=[b_pmm[p]], signal=(kc == 15))
                k.op("act", lambda e, p=p, g=g: e.activation(
                    vbuf[:, g, 1:5, :].rearrange("p a b -> p (a b)"), pmm[p][:, 0:256], ACT.Copy),
                    reads=[b_pmm[p]], writes=[b_vb[g]])
                for n in range(4):
                    first_blk = (tt == 0 and n == 0)
                    kw = 128 if first_blk else 256
                    kcol0 = n * 128 + (128 if first_blk else 0)
                    bcol0 = 128 if first_blk else 0
                    nkb = 1 if first_blk else 2
                    for rnd in range(4):
                        par = rnd % 2
                        hp0 = 2 * rnd
                        for hh in range(2):
                            j = hp0 + hh
                            h = 8 * g + j
                            ci = 32 + 4 * g + j // 2
                            half = (j % 2) * 64
                            k.op("pe", lambda e, par=par, hh=hh, ci=ci, half=half, n=n, g=g, kcol0=kcol0, kw=kw: e.matmul(
                                pS[par][:, hh * 256: hh * 256 + kw], big[half:half + 64, ci, n * 128:(n + 1) * 128],
                                kbuf[half:half + 64, g, kcol0:kcol0 + kw], start=True, stop=False),
                                reads=[b_big[ci], b_kb[g]], writes=[b_pS[par]], signal=False)
                            k.op("pe", lambda e, par=par, hh=hh, h=h, bcol0=bcol0, kw=kw: e.matmul(
                                pS[par][:, hh * 256: hh * 256 + kw], ident_b[:], bm[:, h, bcol0:bcol0 + kw],
                                start=False, stop=True),
                                reads=[B["ident"], B["bm"]], writes=[b_pS[par]], signal=True)
                        k.op("dve", lambda e, par=par, kw=kw: e.tensor_reduce(
                            mx[:, par, :], pS[par][:].rearrange("p (a b) -> p a b", a=2)[:, :, 0:kw], AX.X, ALU.max),
                            reads=[b_pS[par]], writes=[b_mx[par]])
                        k.op("dve", lambda e, par=par, hp0=hp0, g=g: e.scalar_tensor_tensor(
                            negm[:, hp0:hp0 + 2], mx[:, par, :], -1.0, nsink[:, 8 * g + hp0: 8 * g + hp0 + 2],
                            ALU.mult, ALU.min),
                            reads=[b_mx[par], B["nsink"]], writes=[B["negm"]])
                        for hh in range(2):
                            j = hp0 + hh
                            k.op("act", lambda e, par=par, hh=hh, j=j, kw=kw: e.activation(
                                pbuf[:, par, hh, 0:kw], pS[par][:, hh * 256: hh * 256 + kw], ACT.Exp,
                                bias=negm[:, j:j + 1], scale=1.0, accum_out=rsum[:, j:j + 1]),
                                reads=[b_pS[par], B["negm"]], writes=[b_pb[par], B["rsum"]])
                        for hh in range(2):
                            for kb in range(nkb):
                                col = (hh * 2 + kb) * 128
                                k.op("pe", lambda e, par=par, hh=hh, kb=kb, col=col: e.transpose(
                                    pT[:, par * 512 + col: par * 512 + col + 128],
                                    pbuf[:, par, hh, kb * 128:(kb + 1) * 128], ident_b[:]),
                                    reads=[b_pb[par], B["ident"]], writes=[b_pT[par]],
                                    signal=(hh == 1 and kb == nkb - 1))
                        k.op("dve", lambda e, par=par: e.tensor_copy(pTs[:, par, :], pT[:, par * 512:(par + 1) * 512]),
                             reads=[b_pT[par]], writes=[b_pTs[par]])
                        for hh in range(2):
                            j = hp0 + hh
                            for kb in range(nkb):
                                col = (hh * 2 + kb) * 128
                                vblk = n + kb + (1 if first_blk else 0)
                                k.op("pe", lambda e, par=par, j=j, kb=kb, col=col, vblk=vblk, g=g, nkb=nkb: e.matmul(
                                    pO[:, j * 64:(j + 1) * 64], pTs[:, par, col:col + 128], vbuf[:, g, vblk, :],
                                    start=(kb == 0), stop=(kb == nkb - 1)),
                                    reads=[b_pTs[par], b_vb[g]], writes=[b_pO],
                                    signal=(kb == nkb - 1))
                    k.op("dve", lambda e, g=g: e.tensor_tensor(etmp[:], sinks_bc[:, 8 * g:8 * g + 8], negm[:], ALU.add),
                         reads=[B["sinks"], B["negm"]], writes=[B["etmp"]])
                    k.op("act", lambda e: e.activation(etmp[:], etmp[:], ACT.Exp), reads=[B["etmp"]], writes=[B["etmp"]])
                    k.op("dve", lambda e: e.tensor_tensor(rden[:], rsum[:], etmp[:], ALU.add),
                         reads=[B["rsum"], B["etmp"]], writes=[B["rden"]])
                    k.op("dve", lambda e: e.reciprocal(rden[:], rden[:]), reads=[B["rden"]], writes=[B["rden"]])
                    k.op("dve", lambda e: e.tensor_tensor(
                        atok[:].rearrange("p (a b) -> p a b", a=8), pO[:].rearrange("p (a b) -> p a b", a=8),
                        rden[:].unsqueeze(2).broadcast_to([128, 8, 64]), ALU.mult),
                        reads=[b_pO, B["rden"]], writes=[B["atok"]])
                    par = n % 2
                    for i in range(4):
                        k.op("pe", lambda e, par=par, i=i: e.transpose(
                            pT[:, par * 512 + i * 128: par * 512 + (i + 1) * 128], atok[:, i * 128:(i + 1) * 128], ident_b[:]),
                            reads=[B["atok"], B["ident"]], writes=[b_pT[par]], signal=(i == 3))
                    k.op("act", lambda e, par=par, g=g, n=n: e.activation(
                        big[:, 16 + 4 * g:16 + 4 * g + 4, n * 128:(n + 1) * 128],
                        pT[:, par * 512:(par + 1) * 512].rearrange("p (a b) -> p a b", a=4), ACT.Copy),
                        reads=[b_pT[par]], writes=[b_big[16 + 4 * g + i] for i in range(4)])
                if tt < NT - 1:
                    k.op("dve", lambda e, g=g: e.tensor_copy(kbuf[:, g, 0:128], kbuf[:, g, 512:640]),
                         reads=[b_kb[g]], writes=[b_kb[g]])
                    k.op("dve", lambda e, g=g: e.tensor_copy(vbuf[:, g, 0, :], vbuf[:, g, 4, :]),
                         reads=[b_vb[g]], writes=[b_vb[g]])

            for j2 in range(8):
                s = next_block()
                for i in range(2):
                    p = next_pmm()
                    mm_group(p, proj_pairs(s, i, lambda kc: hT[:, kc, :]), reads=[b_wr[s]] + b_h)
                    k.op("act", lambda e, p=p, i=i: e.activation(sga[:, i, :], pmm[p][:], ACT.Exp, scale=-1.0),
                         reads=[b_pmm[p]], writes=[b_sga[i]])
                    k.op("dve", lambda e, i=i: e.tensor_scalar(sga[:, i, :], sga[:, i, :], 1.0, None, ALU.add),
                         reads=[b_sga[i]], writes=[b_sga[i]])
                    k.op("dve", lambda e, i=i: e.reciprocal(sga[:, i, :], sga[:, i, :]), reads=[b_sga[i]], writes=[b_sga[i]])
                s = next_block()
                for i in range(2):
                    p = next_pmm()
                    mm_group(p, proj_pairs(s, i, lambda kc: big[:, kc, :]), reads=[b_wr[s]] + b_big[0:16])
                    k.op("dve", lambda e, p=p, i=i: e.tensor_tensor(sga[:, i, :], sga[:, i, :], pmm[p][:], ALU.mult),
                         reads=[b_sga[i], b_pmm[p]], writes=[b_sga[i]])
                s = next_block()
                for i in range(2):
                    p = next_pmm()
                    mm_group(p, proj_pairs(s, i, lambda kc: hT[:, kc, :]), reads=[b_wr[s]] + b_h)
                    k.op("act", lambda e, p=p, i=i: e.activation(sgb[:, i, :], pmm[p][:], ACT.Exp, scale=-1.0),
                         reads=[b_pmm[p]], writes=[b_sgb[i]])
                    k.op("dve", lambda e, i=i: e.tensor_scalar(sgb[:, i, :], sgb[:, i, :], 1.0, None, ALU.add),
                         reads=[b_sgb[i]], writes=[b_sgb[i]])
                    k.op("dve", lambda e, i=i: e.reciprocal(sgb[:, i, :], sgb[:, i, :]), reads=[b_sgb[i]], writes=[b_sgb[i]])
                s = next_block()
                for i in range(2):
                    j = 2 * j2 + i
                    p = next_pmm()
                    mm_group(p, proj_pairs(s, i, lambda kc: big[:, 16 + kc, :]), reads=[b_wr[s]] + b_big[16:32])
                    k.op("dve", lambda e, p=p, i=i: e.tensor_tensor(sgb[:, i, :], sgb[:, i, :], pmm[p][:], ALU.mult),
                         reads=[b_sgb[i], b_pmm[p]], writes=[b_sgb[i]])
                    k.op("dve", lambda e, i=i, j=j: e.tensor_tensor(big[:, 32 + j, :], sga[:, i, :], sgb[:, i, :], ALU.add),
                         reads=[b_sga[i], b_sgb[i]], writes=[b_big[32 + j]])

            for j2 in range(8):
                s = next_block()
                for i in range(2):
                    j = 2 * j2 + i
                    p = next_pmm()
                    mm_group(p, proj_pairs(s, i, lambda kc: big[:, 32 + kc, :]), reads=[b_wr[s]] + b_big[32:48])
                    k.op("dve", lambda e, p=p, j=j: e.scalar_tensor_tensor(
                        xT[:, j, :], pmm[p][:], dvc(DV_GT1, j), xT[:, j, :], ALU.mult, ALU.add),
                        reads=[b_pmm[p], B["dv"], b_x[j]], writes=[b_x[j]])

            rmsnorm_stats()
            modulate(DV_A2, DV_SH2)

            for qf in range(4):
                slot = (qf % 3) * 16
                for b8 in range(8):
                    s = next_block()
                    for i in range(2):
                        fi = slot + 2 * b8 + i
                        p = next_pmm()
                        mm_group(p, proj_pairs(s, i, lambda kc: hT[:, kc, :]), reads=[b_wr[s]] + b_h)
                        k.op("act", lambda e, p=p, i=i: e.activation(rtmp[:, i, :], pmm[p][:], ACT.Relu),
                             reads=[b_pmm[p]], writes=[b_rtmp[i]])
                        k.op("dve", lambda e, i=i, fi=fi: e.tensor_tensor(big[:, fi, :], rtmp[:, i, :], rtmp[:, i, :], ALU.mult),
                             reads=[b_rtmp[i]], writes=[b_big[fi]])
                for j2 in range(8):
                    s = next_block()
                    for i in range(2):
                        j = 2 * j2 + i
                        p = next_pmm()
                        mm_group(p, proj_pairs(s, i, lambda kc, slot=slot: big[:, slot + kc, :]),
                                 reads=[b_wr[s]] + b_big[slot:slot + 16])
                        k.op("dve", lambda e, p=p, j=j: e.scalar_tensor_tensor(
                            xT[:, j, :], pmm[p][:], dvc(DV_GT2, j), xT[:, j, :], ALU.mult, ALU.add),
                            reads=[b_pmm[p], B["dv"], b_x[j]], writes=[b_x[j]])

            rmsnorm_stats()
            for c in range(16):
                q = c % 2
                k.op("dve", lambda e, c=c, q=q: e.scalar_tensor_tensor(
                    ost[:, q, :], xT[:, c, :], ppc(PC_FG, c), rstd[:], ALU.mult, ALU.mult),
                    reads=[b_x[c], B["pp"], B["rstd"]], writes=[b_ost[q]])
                k.dma("sp", "o%d" % q, out_d[c * 128:(c + 1) * 128, t0:t0 + T], ost[:, q, :], reads=[b_ost[q]])

        k.final_wait("sp")
        k.emit()
    return nc


def _t5_bucket_table():
    qi = np.arange(128)[:, None]
    ki = np.arange(256)[None, :]
    rel = qi + 128 - ki
    relc = np.maximum(rel, 0)
    max_exact = 16
    relf = np.maximum(relc, 1).astype(np.float32)
    large = max_exact + (np.log(relf / np.float32(max_exact)) / np.float32(math.log(128 / max_exact))
                         * np.float32(32 - max_exact)).astype(np.int32)
    large = np.minimum(large, 31)
    bucket = np.where(relc < max_exact, relc, large)
    valid = (rel >= 0) & (rel < 128)
    return bucket, valid


def prep_shared(inp):
    W = {
        "w_ada": np.asarray(inp["w_ada"][0]), "w_in": np.asarray(inp["w_in"][0]),
        "w_lru_out": np.asarray(inp["w_lru_out"][0]), "w_attn_out": np.asarray(inp["w_attn_out"][0]),
        "w_out": np.asarray(inp["w_out"][0]), "w_ff1": np.asarray(inp["w_ff1"][0]), "w_ff2": np.asarray(inp["w_ff2"][0]),
    }
    specs = block_specs()
    wblk = np.zeros((NBLK, 128, 4096), np.float32)
    for bi, (name, r0, cols) in enumerate(specs):
        cols = np.asarray(cols)
        sub = np.zeros((2048, 256), np.float32)
        ok = cols >= 0
        sub[:, ok] = W[name][r0:r0 + 2048][:, cols[ok]]
        wblk[bi] = sub.reshape(16, 128, 256).transpose(1, 0, 2).reshape(128, 4096)

    def fm(v):
        return np.asarray(v, np.float32).reshape(-1, 128).T

    lruw = np.concatenate([np.asarray(inp["lru_wa"][0]).transpose(1, 0, 2)[:, :, None, :],
                           np.asarray(inp["lru_wx"][0]).transpose(1, 0, 2)[:, :, None, :]], axis=2)
    lruw = np.ascontiguousarray(lruw.reshape(128, 16 * 256), np.float32)
    bucket, valid = _t5_bucket_table()
    rb = np.asarray(inp["rel_bias"], np.float32)
    biasg = np.ascontiguousarray(rb[bucket].transpose(0, 2, 1).reshape(128, NHEAD * 256))
    maskc = np.where(valid, 0.0, NEG).astype(np.float32)
    ppbase = np.zeros((128, PC_N), np.float32)
    ppbase[:, PC_BADA:PC_BADA + 96] = fm(inp["b_ada"][0])
    ppbase[:, PC_G1:PC_G1 + 16] = fm(inp["norm1_g"][0])
    ppbase[:, PC_G2:PC_G2 + 16] = fm(inp["norm2_g"][0])
    ppbase[:, PC_FG:PC_FG + 16] = fm(inp["final_g"])
    cw = np.asarray(inp["conv_w"][0], np.float32)
    for kk in range(4):
        ppbase[:, PC_CW + kk * 16: PC_CW + kk * 16 + 16] = fm(cw[kk])
    ppbase[:, PC_CB:PC_CB + 16] = fm(inp["conv_b"][0])
    ppbase[:, PC_BA:PC_BA + 16] = fm(inp["lru_ba"][0])
    ppbase[:, PC_BX:PC_BX + 16] = fm(inp["lru_bx"][0])
    ppbase[:, PC_LAM:PC_LAM + 16] = fm(inp["lru_lambda"][0])
    return dict(wblk=wblk, lruw=lruw, biasg=biasg, maskc=maskc, ppbase=ppbase,
                sinks=np.asarray(inp["attn_sinks"], np.float32).reshape(1, NHEAD),
                idn=np.eye(128, dtype=np.float32))


def core_inputs(shared, x_b, c_b):
    pp = shared["ppbase"].copy()
    pp[:, PC_C:PC_C + 16] = np.asarray(c_b, np.float32).reshape(16, 128).T
    return {"xT": np.ascontiguousarray(np.asarray(x_b, np.float32).T), "wblk": shared["wblk"], "pp": pp,
            "lruw": shared["lruw"], "biasg": shared["biasg"], "maskc": shared["maskc"], "sinks": shared["sinks"],
            "idn": shared["idn"]}


_NC_CACHE = {}


def kernel(**inputs):
    x = np.asarray(inputs["x"])
    c = np.asarray(inputs["c"])
    shared = prep_shared(inputs)
    if 8 not in _NC_CACHE:
        _NC_CACHE[8] = build(8)
    nc = _NC_CACHE[8]
    in_maps = [core_inputs(shared, x[b], c[b]) for b in range(8)]
    res = run_bass_kernel_spmd(nc, in_maps, core_ids=list(range(8)))
    out = np.stack([np.asarray(r["outT"]).T for r in res.results], axis=0)
    return np.ascontiguousarray(out.astype(np.float32))
```

```python
import contextlib
import math
import numpy as np
import concourse.bass as bass
import concourse.mybir as mybir
from concourse.bass_utils import run_bass_kernel_spmd

F32 = mybir.dt.float32
BF16 = mybir.dt.bfloat16
ALU = mybir.AluOpType
ACT = mybir.ActivationFunctionType
AX = mybir.AxisListType

D = 2048
S = 4096
T = 512
NCH = 16
NHEAD = 32
EPS = 1e-6
NADA = 48
NADA1 = 16
import os
LAG_D = int(os.environ.get('LAG_D', '1'))
LAG_F = int(os.environ.get('LAG_F', '2'))
NEG = -30000.0
PC_C, PC_BADA, PC_G1, PC_G2, PC_FG, PC_CW, PC_CB, PC_BA, PC_BX, PC_LAM, PC_N = 0, 16, 112, 128, 144, 160, 224, 240, 256, 272, 288
DV_SH1, DV_SC1, DV_GT1, DV_SH2, DV_SC2, DV_GT2, DV_A1, DV_A2, DV_CA, DV_CA2, DV_NBA, DV_NBX, DV_N = (
    0, 16, 32, 48, 64, 80, 96, 112, 128, 144, 160, 176, 192)
OFF_LX, OFF_LG, OFF_Q, OFF_K, OFF_V, OFF_GA, OFF_GB = 0, 2048, 4096, 6144, 6400, 6656, 8704


def block_specs():
    sp = []
    for ob in range(NADA):
        sp.append(("w_ada", 0, list(range(ob * 256, ob * 256 + 256))))
    for c in range(16):
        sp.append(("w_in", 0, list(range(OFF_LX + c * 128, OFF_LX + c * 128 + 128)) +
                   list(range(OFF_LG + c * 128, OFF_LG + c * 128 + 128))))
    for g in range(4):
        sp.append(("w_in", 0, list(range(OFF_Q + g * 512, OFF_Q + g * 512 + 256))))
        sp.append(("w_in", 0, list(range(OFF_Q + g * 512 + 256, OFF_Q + g * 512 + 512))))
        kc = list(range(OFF_K + g * 64, OFF_K + g * 64 + 64))
        sp.append(("w_in", 0, kc + kc + list(range(OFF_V + g * 64, OFF_V + g * 64 + 64)) + [-1] * 64))
    for j2 in range(8):
        sp.append(("w_in", 0, list(range(OFF_GA + j2 * 256, OFF_GA + j2 * 256 + 256))))
        sp.append(("w_lru_out", 0, list(range(j2 * 256, j2 * 256 + 256))))
        sp.append(("w_in", 0, list(range(OFF_GB + j2 * 256, OFF_GB + j2 * 256 + 256))))
        sp.append(("w_attn_out", 0, list(range(j2 * 256, j2 * 256 + 256))))
    for j2 in range(8):
        sp.append(("w_out", 0, list(range(j2 * 256, j2 * 256 + 256))))
    for qf in range(4):
        for b in range(8):
            sp.append(("w_ff1", 0, list(range(qf * 2048 + b * 256, qf * 2048 + b * 256 + 256))))
        for j2 in range(8):
            sp.append(("w_ff2", qf * 2048, list(range(j2 * 256, j2 * 256 + 256))))
    return sp


NBLK = NADA + 132


class Buf:
    __slots__ = ("name", "lw", "rd")

    def __init__(self, name):
        self.name = name
        self.lw = None
        self.rd = {}


class Stream:
    def __init__(self, name):
        self.name = name
        self.ops = []
        self.count = 0
        self.waited = {}


class KB:
    ENGS = ("pe", "act", "dve", "pool", "sp")

    def __init__(self, nc):
        self.nc = nc
        self.st = {n: Stream(n) for n in self.ENGS}
        self.dma_sems = {}
        self.sem_handles = {}

    def _need(self, s, ev, waits):
        if ev is None:
            return
        key, val = ev
        if s.waited.get(key, 0) >= val:
            return
        s.waited[key] = val
        waits.append((key, val))

    def op(self, eng, fn, reads=(), writes=(), signal=True):
        s = self.st[eng]
        waits = []
        for b in reads:
            if b.lw is not None:
                self._need(s, b.lw, waits)
        for b in writes:
            if b.lw is not None and b.lw[0] != eng:
                self._need(s, b.lw, waits)
            for k, v in b.rd.items():
                if k != eng:
                    self._need(s, (k, v), waits)
        keep = []
        for (k, v) in waits:
            if k == eng and v > s.count:
                s.waited[k] = s.count
                continue
            keep.append((k, v))
        nxt = s.count + 1
        ev = (eng, nxt)
        for b in writes:
            b.lw = ev
            b.rd = {}
        for b in reads:
            if b.rd.get(eng, 0) < nxt:
                b.rd[eng] = nxt
        if signal:
            s.count = nxt
        s.ops.append((keep, fn, eng if signal else None, 1))

    def dma(self, eng, slot, out, in_, reads=(), writes=()):
        s = self.st[eng]
        waits = []
        for b in reads:
            self._need(s, b.lw, waits)
        for b in writes:
            self._need(s, b.lw, waits)
            for k, v in b.rd.items():
                self._need(s, (k, v), waits)
        key = "dma_" + slot
        val = self.dma_sems.get(key, 0) + 16
        self.dma_sems[key] = val
        ev = (key, val)
        for b in writes:
            b.lw = ev
            b.rd = {}
        for b in reads:
            b.rd[key] = val

        def fn(e, out=out, in_=in_):
            return e.dma_start(out=out, in_=in_)
        s.ops.append((waits, fn, key, 16))

    def final_wait(self, eng):
        s = self.st[eng]
        waits = []
        for n in self.ENGS:
            if n != eng and self.st[n].count > 0:
                self._need(s, (n, self.st[n].count), waits)
        for k, v in self.dma_sems.items():
            self._need(s, (k, v), waits)
        s.ops.append((waits, None, None, 0))

    def emit(self):
        nc = self.nc
        with contextlib.ExitStack() as es:
            keys = list(self.ENGS) + list(self.dma_sems.keys())
            for k in keys:
                self.sem_handles[k] = es.enter_context(nc.semaphore("s_" + k))
            block = es.enter_context(nc.Block())
            H = self.sem_handles

            def run(stream):
                def body(e):
                    for waits, fn, inc, amt in stream.ops:
                        for k, v in waits:
                            e.wait_ge(H[k], v)
                        if fn is None:
                            continue
                        ins = fn(e)
                        if inc is not None:
                            ins.then_inc(H[inc], amt)
                return body

            block.tensor(run(self.st["pe"]))
            block.scalar(run(self.st["act"]))
            block.vector(run(self.st["dve"]))
            block.gpsimd(run(self.st["pool"]))
            block.sync(run(self.st["sp"]))


def build(NT=8):
    nc = bass.Bass("TRN2", target_bir_lowering=False)
    xT_d = nc.dram_tensor("xT", [D, S], F32, kind="ExternalInput").ap()
    wblk_d = nc.dram_tensor("wblk", [NBLK, 128, 4096], F32, kind="ExternalInput").ap()
    pp_d = nc.dram_tensor("pp", [128, PC_N], F32, kind="ExternalInput").ap()
    lruw_d = nc.dram_tensor("lruw", [128, 16 * 256], F32, kind="ExternalInput").ap()
    biasg_d = nc.dram_tensor("biasg", [128, NHEAD * 256], F32, kind="ExternalInput").ap()
    maskc_d = nc.dram_tensor("maskc", [128, 256], F32, kind="ExternalInput").ap()
    sinks_d = nc.dram_tensor("sinks", [1, NHEAD], F32, kind="ExternalInput").ap()
    idn_d = nc.dram_tensor("idn", [128, 128], F32, kind="ExternalInput").ap()
    out_d = nc.dram_tensor("outT", [D, S], F32, kind="ExternalOutput").ap()
    wscr_d = nc.dram_tensor("wscr", [NBLK, 128, 4096], BF16).ap()

    with contextlib.ExitStack() as es:
        def sb(n, s, d):
            return es.enter_context(nc.sbuf_tensor(n, s, d))

        def ps(n, s, d):
            return es.enter_context(nc.psum_tensor(n, s, d))

        wring = sb("wring", [128, 4, 4096], BF16)
        xT = sb("xTs", [128, 16, 512], F32)
        hT = sb("hTs", [128, 16, 512], BF16)
        big = sb("big", [128, 48, 512], BF16)
        bm = sb("bm", [128, NHEAD, 256], BF16)
        lruw = sb("lruws", [128, 16, 256], BF16)
        pp = sb("pps", [128, PC_N], F32)
        dv = sb("dvs", [128, DV_N], F32)
        cact = sb("cact", [128, 16], BF16)
        ctmp = sb("ctmp", [128, 16], F32)
        ident_b = sb("ident_b", [128, 128], BF16)
        ones_f = sb("ones_f", [128, 128], F32)
        sinks_bc = sb("sinks_bc", [128, NHEAD], F32)
        nsink = sb("nsink", [128, NHEAD], F32)
        maskc = sb("maskcs", [128, 256], F32)
        hist = sb("hist", [128, 16, 4], F32)
        hstate = sb("hstate", [128, 16], F32)
        kbuf = sb("kbuf", [128, 4, 640], BF16)
        vbuf = sb("vbuf", [128, 4, 5, 64], BF16)
        sqt = sb("sqt", [128, 2, 512], F32)
        lnv = sb("lnv", [128, 512], F32)
        rstd = sb("rstd", [128, 512], F32)
        lxb = sb("lxb", [128, 516], F32)
        xc = sb("xc", [128, 512], F32)
        xcb = sb("xcb", [128, 512], BF16)
        tr = sb("tr", [128, 512], F32)
        ta = sb("ta", [128, 512], F32)
        ti = sb("ti", [128, 512], F32)
        rec = sb("rec", [128, 512], F32)
        gxs = sb("gxs", [128, 512], F32)
        gsq = sb("gsq", [128, 512], F32)
        gt = sb("gt", [128, 512], F32)
        sga = sb("sga", [128, 2, 512], F32)
        sgb = sb("sgb", [128, 2, 512], F32)
        ntmp = sgb
        ost = sqt
        rtmp = sga
        pbuf = sb("pbuf", [128, 2, 2, 256], BF16)
        pTs = sb("pTs", [128, 2, 512], BF16)
        mx = sb("mx", [128, 2, 2], F32)
        negm = sb("negm", [128, 2, 8], F32)
        rsum = sb("rsum", [128, 2, 8], F32)
        etmp = sb("etmp", [128, 2, 8], F32)
        rden = sb("rden", [128, 2, 8], F32)
        atok = sb("atok", [128, 512], BF16)

        NPM = int(os.environ.get('NPM', '4'))
        pmm = [ps("pmm%d" % i, [128, 512], F32) for i in range(NPM)]
        pS = [ps("pS%d" % i, [128, 512], F32) for i in range(2)]
        if os.environ.get('PT2', '0') == '1':
            pTb = [ps("pT%d" % i, [128, 1024], BF16) for i in range(2)]
            pTv = lambda par: pTv(par)
        else:
            pTone = ps("pTone", [128, 1024], BF16)
            pTv = lambda par: pTone[:, par * 512:(par + 1) * 512]
        pO = ps("pO", [128, 512], F32)

        def program(k, full):
            req_log = []
            b_wr = [Buf("wr%d" % i) for i in range(4)]
            b_scr = [Buf("scr%d" % i) for i in range(NBLK)]
            b_x = [Buf("x%d" % i) for i in range(16)]
            b_h = [Buf("h%d" % i) for i in range(16)]
            b_big = [Buf("big%d" % i) for i in range(48)]
            b_pmm = [Buf("pmm%d" % i) for i in range(NPM)]
            b_pS = [Buf("pS%d" % i) for i in range(2)]
            _bpt = Buf("pT")
            b_pT = [_bpt, _bpt]
            b_pO = Buf("pO")
            names = ["bm", "lruw", "pp", "dv", "cact", "ctmp", "ident", "ones", "sinks", "nsink", "maskc", "hist",
                     "hstate", "lnv", "rstd", "lxb", "xc", "xcb", "tr", "ta", "ti", "rec", "gxs", "gsq", "gt",
                     "negm", "rsum", "etmp", "rden", "atok"]
            B = {n: Buf(n) for n in names}
            b_kb = [Buf("kb%d" % i) for i in range(4)]
            b_vb = [Buf("vb%d" % i) for i in range(4)]
            b_sqt = [Buf("sqt%d" % i) for i in range(2)]
            b_sga = [Buf("sga%d" % i) for i in range(2)]
            b_sgb = [Buf("sgb%d" % i) for i in range(2)]
            b_ntmp = b_sgb
            b_ost = b_sqt
            b_rtmp = b_sga
            b_pb = [Buf("pb%d" % i) for i in range(2)]
            b_pTs = [Buf("pTs%d" % i) for i in range(2)]
            b_mx = [Buf("mx%d" % i) for i in range(2)]
            b_negm = [[Buf("negm%d_%d" % (ip, i)) for i in range(4)] for ip in range(2)]
            b_rsum = [Buf("rsum%d" % i) for i in range(2)]
            b_etmp = [Buf("etmp%d" % i) for i in range(2)]
            b_rden = [Buf("rden%d" % i) for i in range(2)]

            def dvc(col, c=0, n=1):
                return dv[:, col + c: col + c + n]

            def ppc(col, c=0, n=1):
                return pp[:, col + c: col + c + n]

            state = {"seq": 0, "pm": 0}
            sched = []

            def issue_load(seq, blk, first):
                s = seq % 4
                if first:
                    k.dma("pool", "w%d" % s, wring[:, s, :], wblk_d[blk], writes=[b_wr[s]])
                    if blk >= NADA and NT > 1:
                        k.dma("sp", "ws%d" % s, wscr_d[blk], wring[:, s, :], reads=[b_wr[s]], writes=[b_scr[blk]])
                else:
                    k.dma("sp", "w%d" % s, wring[:, s, :], wscr_d[blk], reads=[b_scr[blk]], writes=[b_wr[s]])

            PRE = 3
            issued = {"n": 0}

            def next_block(blk, first):
                seq = state["seq"]
                req_log.append((blk, first))
                if full is not None:
                    assert full[seq] == (blk, first)
                    while issued["n"] < min(len(full), seq + PRE + 1):
                        n = issued["n"]
                        issue_load(n, full[n][0], full[n][1])
                        issued["n"] += 1
                state["seq"] = seq + 1
                return seq % 4

            def next_pmm():
                i = state["pm"] % NPM
                state["pm"] += 1
                return i

            def mm_group(pi, pairs, reads, ncols=512, col0=0, sig_all=False):
                n = len(pairs)
                for idx, (l, r) in enumerate(pairs):
                    last = idx == n - 1
                    k.op("pe", lambda e, l=l, r=r, idx=idx, last=last: e.matmul(
                        pmm[pi][:, col0:col0 + ncols], l, r, start=(idx == 0), stop=last),
                        reads=reads[idx] if isinstance(reads, dict) else reads,
                        writes=[b_pmm[pi]], signal=(last or sig_all))

            k.dma("sp", "c0", pp[:], pp_d, writes=[B["pp"]])
            k.dma("sp", "c1", maskc[:], maskc_d, writes=[B["maskc"]])
            k.dma("sp", "c2", sinks_bc[:], sinks_d.broadcast_to([128, NHEAD]), writes=[B["sinks"]])
            k.dma("pool", "c3", ident_b[:], idn_d, writes=[B["ident"]])
            k.dma("pool", "c4", lruw[:].rearrange("p a b -> p (a b)"), lruw_d, writes=[B["lruw"]])
            xflat = xT[:].rearrange("p a b -> p (a b)")
            k.dma("sp", "c5", xflat, biasg_d, writes=b_x)
            k.op("dve", lambda e: e.memset(ones_f[:], 1.0), writes=[B["ones"]])
            k.op("dve", lambda e: e.memset(hist[:].rearrange("p a b -> p (a b)"), 0.0), writes=[B["hist"]])
            k.op("dve", lambda e: e.memset(hstate[:], 0.0), writes=[B["hstate"]])
            k.op("dve", lambda e: e.memset(kbuf[:].rearrange("p a b -> p (a b)"), 0.0), writes=b_kb)
            k.op("dve", lambda e: e.memset(vbuf[:].rearrange("p a b c -> p (a b c)"), 0.0), writes=b_vb)
            for h in range(NHEAD):
                k.op("dve", lambda e, h=h: e.tensor_tensor(bm[:, h, :], xflat[:, h * 256:(h + 1) * 256], maskc[:], ALU.add),
                     reads=b_x + [B["maskc"]], writes=[B["bm"]])
            k.op("dve", lambda e: e.tensor_scalar(nsink[:], sinks_bc[:], -1.0, None, ALU.mult),
                 reads=[B["sinks"]], writes=[B["nsink"]])
            k.op("act", lambda e: e.activation(ctmp[:], ppc(PC_C, 0, 16), ACT.Exp, scale=-1.0),
                 reads=[B["pp"]], writes=[B["ctmp"]])
            k.op("dve", lambda e: e.tensor_scalar(ctmp[:], ctmp[:], 1.0, None, ALU.add), reads=[B["ctmp"]], writes=[B["ctmp"]])
            k.op("dve", lambda e: e.reciprocal(ctmp[:], ctmp[:]), reads=[B["ctmp"]], writes=[B["ctmp"]])
            k.op("dve", lambda e: e.tensor_tensor(cact[:], ctmp[:], ppc(PC_C, 0, 16), ALU.mult),
                 reads=[B["ctmp"], B["pp"]], writes=[B["cact"]])
            def ada_blocks(ob0, ob1):
                for ob in range(ob0, ob1):
                    s = next_block(ob, True)
                    for i in range(2):
                        oc = ob * 2 + i
                        for kc in range(16):
                            k.op("pe", lambda e, s=s, i=i, kc=kc, oc=oc: e.matmul(
                                pO[:, oc:oc + 1], wring[:, s, kc * 256 + i * 128: kc * 256 + i * 128 + 128],
                                cact[:, kc:kc + 1], start=(kc == 0), stop=(kc == 15)),
                                reads=[b_wr[s], B["cact"]], writes=[b_pO], signal=(kc == 15))
                c0, c1 = ob0 * 2, ob1 * 2
                k.op("dve", lambda e: e.tensor_tensor(dv[:, c0:c1], pO[:, c0:c1], ppc(PC_BADA, c0, c1 - c0), ALU.add),
                     reads=[b_pO, B["pp"]], writes=[B["dv"]])

            ada_blocks(0, NADA1)
            k.op("dve", lambda e: e.scalar_tensor_tensor(dvc(DV_A1, 0, 16), dvc(DV_SC1, 0, 16), 1.0, ppc(PC_G1, 0, 16),
                                                         ALU.add, ALU.mult), reads=[B["dv"], B["pp"]], writes=[B["dv"]])
            k.op("act", lambda e: e.activation(ctmp[:], ppc(PC_LAM, 0, 16), ACT.Exp, scale=-1.0),
                 reads=[B["pp"], B["cact"]], writes=[B["ctmp"]])
            k.op("act", lambda e: e.activation(ctmp[:], ctmp[:], ACT.Ln, bias=1.0), reads=[B["ctmp"]], writes=[B["ctmp"]])
            k.op("dve", lambda e: e.tensor_scalar(dvc(DV_CA, 0, 16), ctmp[:], -8.0, None, ALU.mult),
                 reads=[B["ctmp"]], writes=[B["dv"]])
            k.op("dve", lambda e: e.tensor_scalar(dvc(DV_CA2, 0, 16), ctmp[:], -16.0, None, ALU.mult),
                 reads=[B["ctmp"]], writes=[B["dv"]])
            k.op("dve", lambda e: e.tensor_scalar(dvc(DV_NBA, 0, 16), ppc(PC_BA, 0, 16), -1.0, None, ALU.mult),
                 reads=[B["pp"]], writes=[B["dv"]])
            k.op("dve", lambda e: e.tensor_scalar(dvc(DV_NBX, 0, 16), ppc(PC_BX, 0, 16), -1.0, None, ALU.mult),
                 reads=[B["pp"]], writes=[B["dv"]])

            def rmsnorm_stats():
                pi = next_pmm()
                for c in range(16):
                    q = c % 2
                    k.op("act", lambda e, c=c, q=q: e.activation(sqt[:, q, :], xT[:, c, :], ACT.Square),
                         reads=[b_x[c]], writes=[b_sqt[q]])
                    k.op("pe", lambda e, c=c, q=q, pi=pi: e.matmul(pmm[pi][:], ones_f[:], sqt[:, q, :],
                                                                  start=(c == 0), stop=(c == 15)),
                         reads=[b_sqt[q], B["ones"]], writes=[b_pmm[pi]], signal=True)
                k.op("act", lambda e, pi=pi: e.activation(lnv[:], pmm[pi][:], ACT.Ln, bias=EPS, scale=1.0 / D),
                     reads=[b_pmm[pi]], writes=[B["lnv"]])
                k.op("act", lambda e: e.activation(rstd[:], lnv[:], ACT.Exp, scale=-0.5),
                     reads=[B["lnv"]], writes=[B["rstd"]])

            def modulate(acol, scol):
                for c in range(16):
                    q = c % 2
                    k.op("dve", lambda e, c=c, q=q: e.tensor_tensor(ntmp[:, q, :], xT[:, c, :], rstd[:], ALU.mult),
                         reads=[b_x[c], B["rstd"]], writes=[b_ntmp[q]])
                    k.op("act", lambda e, c=c, q=q: e.activation(hT[:, c, :], ntmp[:, q, :], ACT.Identity,
                                                                 bias=dvc(scol, c), scale=dvc(acol, c)),
                         reads=[b_ntmp[q], B["dv"]], writes=[b_h[c]])

            def proj_pairs(s, i, src, srcbufs=None):
                return [(wring[:, s, kc * 256 + i * 128: kc * 256 + i * 128 + 128], src(kc)) for kc in range(16)]

            C0 = math.sqrt(2.0 / math.pi)
            C1 = 0.044715

            for tt in range(NT):
                t0 = tt * T
                for c in range(16):
                    k.dma("sp", "x%d" % c, xT[:, c, :], xT_d[c * 128:(c + 1) * 128, t0:t0 + T], writes=[b_x[c]])
                rmsnorm_stats()
                modulate(DV_A1, DV_SH1)

                def lru_chunk(c):
                    s = next_block(NADA + c, tt == 0)
                    p_lx = next_pmm()
                    mm_group(p_lx, proj_pairs(s, 0, lambda kc: hT[:, kc, :]), reads=[b_wr[s]] + b_h)
                    p_lg = next_pmm()
                    mm_group(p_lg, proj_pairs(s, 1, lambda kc: hT[:, kc, :]), reads=[b_wr[s]] + b_h)
                    yield
                    k.op("dve", lambda e, c=c: e.tensor_copy(lxb[:, 0:3], hist[:, c, 0:3]),
                         reads=[B["hist"]], writes=[B["lxb"]])
                    k.op("act", lambda e, p=p_lx: e.activation(lxb[:, 3:515], pmm[p][:], ACT.Copy),
                         reads=[b_pmm[p_lx]], writes=[B["lxb"]])
                    k.op("act", lambda e, p=p_lg: e.activation(gxs[:], pmm[p][:], ACT.Copy), reads=[b_pmm[p_lg]], writes=[B["gxs"]])
                    k.op("act", lambda e, p=p_lg: e.activation(gsq[:], pmm[p][:], ACT.Square), reads=[b_pmm[p_lg]], writes=[B["gsq"]])
                    yield
                    k.op("dve", lambda e, c=c: e.tensor_copy(hist[:, c, 0:3], lxb[:, 512:515]),
                         reads=[B["lxb"]], writes=[B["hist"]])
                    k.op("dve", lambda e, c=c: e.tensor_scalar(xc[:], lxb[:, 3:515], ppc(PC_CW, 3 * 16 + c), ppc(PC_CB, c),
                                                              ALU.mult, ALU.add),
                         reads=[B["lxb"], B["pp"]], writes=[B["xc"]])
                    k.op("dve", lambda e: e.tensor_scalar(gt[:], gsq[:], C1, 1.0, ALU.mult, ALU.add), reads=[B["gsq"]], writes=[B["gt"]])
                    k.op("dve", lambda e, c=c: e.scalar_tensor_tensor(
                        xc[:], lxb[:, 2:514], ppc(PC_CW, 2 * 16 + c), xc[:], ALU.mult, ALU.add),
                        reads=[B["lxb"], B["pp"], B["xc"]], writes=[B["xc"]])
                    yield
                    k.op("dve", lambda e: e.tensor_tensor(gt[:], gt[:], gxs[:], ALU.mult), reads=[B["gt"], B["gxs"]], writes=[B["gt"]])
                    for kk in (1, 0):
                        k.op("dve", lambda e, c=c, kk=kk: e.scalar_tensor_tensor(
                            xc[:], lxb[:, kk:kk + 512], ppc(PC_CW, kk * 16 + c), xc[:], ALU.mult, ALU.add),
                            reads=[B["lxb"], B["pp"], B["xc"]], writes=[B["xc"]])
                    yield
                    k.op("act", lambda e: e.activation(gt[:], gt[:], ACT.Exp, scale=-2.0 * C0), reads=[B["gt"]], writes=[B["gt"]])
                    k.op("act", lambda e: e.activation(xcb[:], xc[:], ACT.Copy), reads=[B["xc"]], writes=[B["xcb"]])
                    k.op("act", lambda e: e.activation(gt[:], gt[:], ACT.Ln, bias=1.0), reads=[B["gt"]], writes=[B["gt"]])
                    yield
                    p_r = next_pmm()
                    k.op("pe", lambda e, c=c, p=p_r: e.matmul(pmm[p][:], lruw[:, c, 0:128], xcb[:], start=True, stop=True),
                         reads=[B["lruw"], B["xcb"]], writes=[b_pmm[p_r]])
                    p_i = next_pmm()
                    k.op("pe", lambda e, c=c, p=p_i: e.matmul(pmm[p][:], lruw[:, c, 128:256], xcb[:], start=True, stop=True),
                         reads=[B["lruw"], B["xcb"]], writes=[b_pmm[p_i]])
                    yield
                    k.op("act", lambda e, c=c, p=p_r: e.activation(tr[:], pmm[p][:], ACT.Exp, bias=dvc(DV_NBA, c), scale=-1.0),
                         reads=[b_pmm[p_r], B["dv"]], writes=[B["tr"]])
                    k.op("act", lambda e, c=c, p=p_i: e.activation(ti[:], pmm[p][:], ACT.Exp, bias=dvc(DV_NBX, c), scale=-1.0),
                         reads=[b_pmm[p_i], B["dv"]], writes=[B["ti"]])
                    k.op("act", lambda e: e.activation(gt[:], gt[:], ACT.Exp, scale=-1.0), reads=[B["gt"]], writes=[B["gt"]])
                    yield
                    k.op("act", lambda e: e.activation(tr[:], tr[:], ACT.Ln, bias=1.0), reads=[B["tr"]], writes=[B["tr"]])
                    k.op("act", lambda e: e.activation(ti[:], ti[:], ACT.Ln, bias=1.0), reads=[B["ti"]], writes=[B["ti"]])
                    k.op("dve", lambda e: e.tensor_tensor(gt[:], gt[:], gxs[:], ALU.mult), reads=[B["gt"], B["gxs"]], writes=[B["gt"]])
                    yield
                    k.op("act", lambda e: e.activation(tr[:], tr[:], ACT.Exp, scale=-1.0), reads=[B["tr"]], writes=[B["tr"]])
                    k.op("act", lambda e: e.activation(ti[:], ti[:], ACT.Exp, scale=-1.0), reads=[B["ti"]], writes=[B["ti"]])
                    yield
                    k.op("act", lambda e, c=c: e.activation(ta[:], tr[:], ACT.Exp, scale=dvc(DV_CA, c)),
                         reads=[B["tr"], B["dv"]], writes=[B["ta"]])
                    k.op("act", lambda e, c=c: e.activation(tr[:], tr[:], ACT.Exp, scale=dvc(DV_CA2, c)),
                         reads=[B["tr"], B["dv"]], writes=[B["tr"]])
                    k.op("dve", lambda e: e.tensor_tensor(ti[:], ti[:], xc[:], ALU.mult), reads=[B["ti"], B["xc"]], writes=[B["ti"]])
                    yield
                    k.op("act", lambda e: e.activation(tr[:], tr[:], ACT.Ln, bias=1.0000002, scale=-1.0),
                         reads=[B["tr"]], writes=[B["tr"]])
                    yield
                    k.op("act", lambda e: e.activation(tr[:], tr[:], ACT.Exp, scale=0.5), reads=[B["tr"]], writes=[B["tr"]])
                    yield
                    if tt == 0:
                        k.op("dve", lambda e: e.memset(tr[:, 0:1], 1.0), reads=[B["tr"]], writes=[B["tr"]])
                    k.op("dve", lambda e: e.tensor_tensor(ti[:], ti[:], tr[:], ALU.mult), reads=[B["ti"], B["tr"]], writes=[B["ti"]])
                    k.op("dve", lambda e, c=c: e.tensor_tensor_scan(rec[:], ta[:], ti[:], hstate[:, c:c + 1], ALU.mult, ALU.add),
                         reads=[B["ta"], B["ti"], B["hstate"]], writes=[B["rec"]])
                    yield
                    k.op("dve", lambda e, c=c: e.tensor_copy(hstate[:, c:c + 1], rec[:, 511:512]),
                         reads=[B["rec"]], writes=[B["hstate"]])
                    k.op("dve", lambda e, c=c: e.tensor_tensor(big[:, c, :], gt[:], rec[:], ALU.mult),
                         reads=[B["gt"], B["rec"]], writes=[b_big[c]])
                    yield

                def attn_proj(g):
                    for qb in range(2):
                        s = next_block(NADA + 16 + 3 * g + qb, tt == 0)
                        for i in range(2):
                            ci = 32 + 4 * g + 2 * qb + i
                            p = next_pmm()
                            mm_group(p, proj_pairs(s, i, lambda kc: hT[:, kc, :]), reads=[b_wr[s]] + b_h)
                            k.op("act", lambda e, p=p, ci=ci: e.activation(big[:, ci, :], pmm[p][:], ACT.Copy, scale=0.125),
                                 reads=[b_pmm[p]], writes=[b_big[ci]])
                        yield
                    s = next_block(NADA + 16 + 3 * g + 2, tt == 0)
                    p = next_pmm()
                    mm_group(p, proj_pairs(s, 0, lambda kc: hT[:, kc, :]), reads=[b_wr[s]] + b_h)
                    k.op("dve", lambda e, p=p, g=g: e.tensor_copy(kbuf[:, g, 128:640], pmm[p][:]),
                         reads=[b_pmm[p]], writes=[b_kb[g]])
                    p = next_pmm()
                    for tb in range(4):
                        for kc in range(16):
                            k.op("pe", lambda e, p=p, s=s, tb=tb, kc=kc: e.matmul(
                                pmm[p][:, tb * 64:(tb + 1) * 64], hT[:, kc, tb * 128:(tb + 1) * 128],
                                wring[:, s, kc * 256 + 128: kc * 256 + 192], start=(kc == 0), stop=(kc == 15)),
                                reads=[b_wr[s]] + b_h, writes=[b_pmm[p]], signal=(kc == 15))
                    k.op("act", lambda e, p=p, g=g: e.activation(
                        vbuf[:, g, 1:5, :].rearrange("p a b -> p (a b)"), pmm[p][:, 0:256], ACT.Copy),
                        reads=[b_pmm[p]], writes=[b_vb[g]])
                    yield

                def attn_hist(g):
                    if tt < NT - 1:
                        k.op("dve", lambda e, g=g: e.tensor_copy(kbuf[:, g, 0:128], kbuf[:, g, 512:640]),
                             reads=[b_kb[g]], writes=[b_kb[g]])
                        k.op("dve", lambda e, g=g: e.tensor_copy(vbuf[:, g, 0, :], vbuf[:, g, 4, :]),
                             reads=[b_vb[g]], writes=[b_vb[g]])

                rounds = [(g, n, rnd) for g in range(4) for n in range(4) for rnd in range(4)]

                def rinfo(i):
                    g, n, rnd = rounds[i]
                    first_blk = (tt == 0 and n == 0)
                    return dict(g=g, n=n, rnd=rnd, par=i % 2, ip=(i // 4) % 2, hp0=2 * rnd, first=first_blk,
                                kw=128 if first_blk else 256, kcol0=n * 128 + (128 if first_blk else 0),
                                bcol0=128 if first_blk else 0, nkb=1 if first_blk else 2)

                def st_A(i):
                    r = rinfo(i)
                    g, n, par, kw, kcol0, bcol0 = r["g"], r["n"], r["par"], r["kw"], r["kcol0"], r["bcol0"]
                    for hh in range(2):
                        j = r["hp0"] + hh
                        h = 8 * g + j
                        ci = 32 + 4 * g + j // 2
                        half = (j % 2) * 64
                        k.op("pe", lambda e, par=par, hh=hh, ci=ci, half=half, n=n, g=g, kcol0=kcol0, kw=kw: e.matmul(
                            pS[par][:, hh * 256: hh * 256 + kw], big[half:half + 64, ci, n * 128:(n + 1) * 128],
                            kbuf[half:half + 64, g, kcol0:kcol0 + kw], start=True, stop=False),
                            reads=[b_big[ci], b_kb[g]], writes=[b_pS[par]], signal=False)
                        k.op("pe", lambda e, par=par, hh=hh, h=h, bcol0=bcol0, kw=kw: e.matmul(
                            pS[par][:, hh * 256: hh * 256 + kw], ident_b[:], bm[:, h, bcol0:bcol0 + kw],
                            start=False, stop=True),
                            reads=[B["ident"], B["bm"]], writes=[b_pS[par]], signal=True)

                def st_B(i):
                    r = rinfo(i)
                    g, par, kw, hp0, ip, rnd = r["g"], r["par"], r["kw"], r["hp0"], r["ip"], r["rnd"]
                    k.op("dve", lambda e, par=par, kw=kw: e.tensor_reduce(
                        mx[:, par, :], pS[par][:].rearrange("p (a b) -> p a b", a=2)[:, :, 0:kw], AX.X, ALU.max),
                        reads=[b_pS[par]], writes=[b_mx[par]])
                    k.op("dve", lambda e, par=par, hp0=hp0, g=g, ip=ip: e.scalar_tensor_tensor(
                        negm[:, ip, hp0:hp0 + 2], mx[:, par, :], -1.0, nsink[:, 8 * g + hp0: 8 * g + hp0 + 2],
                        ALU.mult, ALU.min),
                        reads=[b_mx[par], B["nsink"]], writes=[b_negm[ip][rnd]])

                def st_C(i):
                    r = rinfo(i)
                    g, par, kw, hp0, ip, rnd = r["g"], r["par"], r["kw"], r["hp0"], r["ip"], r["rnd"]
                    for hh in range(2):
                        j = hp0 + hh
                        k.op("act", lambda e, par=par, hh=hh, j=j, kw=kw, ip=ip: e.activation(
                            pbuf[:, par, hh, 0:kw], pS[par][:, hh * 256: hh * 256 + kw], ACT.Exp,
                            bias=negm[:, ip, j:j + 1], scale=1.0, accum_out=rsum[:, ip, j:j + 1]),
                            reads=[b_pS[par], b_negm[ip][rnd]], writes=[b_pb[par], b_rsum[ip]])
                    if rnd == 3:
                        k.op("dve", lambda e, g=g, ip=ip: e.tensor_tensor(
                            etmp[:, ip, :], sinks_bc[:, 8 * g:8 * g + 8], negm[:, ip, :], ALU.add),
                            reads=[B["sinks"]] + b_negm[ip], writes=[b_etmp[ip]])
                        k.op("act", lambda e, ip=ip: e.activation(etmp[:, ip, :], etmp[:, ip, :], ACT.Exp),
                             reads=[b_etmp[ip]], writes=[b_etmp[ip]])
                        k.op("dve", lambda e, ip=ip: e.tensor_tensor(rden[:, ip, :], rsum[:, ip, :], etmp[:, ip, :], ALU.add),
                             reads=[b_rsum[ip], b_etmp[ip]], writes=[b_rden[ip]])
                        k.op("dve", lambda e, ip=ip: e.reciprocal(rden[:, ip, :], rden[:, ip, :]),
                             reads=[b_rden[ip]], writes=[b_rden[ip]])

                def st_D(i):
                    r = rinfo(i)
                    par, nkb = r["par"], r["nkb"]
                    for hh in range(2):
                        for kb in range(nkb):
                            col = (hh * 2 + kb) * 128
                            k.op("pe", lambda e, par=par, hh=hh, kb=kb, col=col: e.transpose(
                                pTv(par)[:, col: col + 128],
                                pbuf[:, par, hh, kb * 128:(kb + 1) * 128], ident_b[:]),
                                reads=[b_pb[par], B["ident"]], writes=[b_pT[par]],
                                signal=(hh == 1 and kb == nkb - 1))

                def st_E(i):
                    par = rinfo(i)["par"]
                    k.op("dve", lambda e, par=par: e.tensor_copy(pTs[:, par, :], pTv(par)),
                         reads=[b_pT[par]], writes=[b_pTs[par]])

                def st_F(i):
                    r = rinfo(i)
                    g, n, par, nkb, hp0, first_blk = r["g"], r["n"], r["par"], r["nkb"], r["hp0"], r["first"]
                    for hh in range(2):
                        j = hp0 + hh
                        for kb in range(nkb):
                            col = (hh * 2 + kb) * 128
                            vblk = n + kb + (1 if first_blk else 0)
                            k.op("pe", lambda e, par=par, j=j, kb=kb, col=col, vblk=vblk, g=g, nkb=nkb: e.matmul(
                                pO[:, j * 64:(j + 1) * 64], pTs[:, par, col:col + 128], vbuf[:, g, vblk, :],
                                start=(kb == 0), stop=(kb == nkb - 1)),
                                reads=[b_pTs[par], b_vb[g]], writes=[b_pO],
                                signal=(kb == nkb - 1))

                def st_G(i):
                    r = rinfo(i)
                    g, n, ip, par = r["g"], r["n"], r["ip"], r["par"]
                    pT2 = pTv(par)
                    b_pT2 = b_pT[par]
                    k.op("dve", lambda e, ip=ip: e.tensor_tensor(
                        atok[:].rearrange("p (a b) -> p a b", a=8), pO[:].rearrange("p (a b) -> p a b", a=8),
                        rden[:, ip, :].unsqueeze(2).broadcast_to([128, 8, 64]), ALU.mult),
                        reads=[b_pO, b_rden[ip]], writes=[B["atok"]])
                    yield
                    for q4 in range(4):
                        k.op("pe", lambda e, q4=q4: e.transpose(
                            pT2[:, q4 * 128:(q4 + 1) * 128], atok[:, q4 * 128:(q4 + 1) * 128], ident_b[:]),
                            reads=[B["atok"], B["ident"]], writes=[b_pT2], signal=(q4 == 3))
                    yield
                    k.op("act", lambda e, g=g, n=n: e.activation(
                        big[:, 16 + 4 * g:16 + 4 * g + 4, n * 128:(n + 1) * 128],
                        pT2.rearrange("p (a b) -> p a b", a=4), ACT.Copy),
                        reads=[b_pT2], writes=[b_big[16 + 4 * g + q4] for q4 in range(4)])
                    if n == 3:
                        attn_hist(g)
                    yield

                def attn_stream():
                    NR = len(rounds)
                    for i in range(NR + LAG_F):
                        if i < NR:
                            st_A(i)
                            yield
                            st_B(i)
                            yield
                            st_C(i)
                            yield
                        if 0 <= i - LAG_D < NR:
                            st_D(i - LAG_D)
                            yield
                            st_E(i - LAG_D)
                            yield
                        if 0 <= i - LAG_F < NR:
                            st_F(i - LAG_F)
                            yield
                            if rounds[i - LAG_F][2] == 3:
                                yield from st_G(i - LAG_F)

                def proj_stream():
                    for g in range(4):
                        yield from attn_proj(g)

                def lru_stream():
                    for c in range(16):
                        yield from lru_chunk(c)

                def drain(gen):
                    for _ in gen:
                        pass

                pj = proj_stream()
                for _ in range(3):
                    next(pj)
                gens = [lru_stream(), attn_stream()]
                live = [True, True]
                step = 0
                pj_live = True
                while any(live):
                    for gi, gg in enumerate(gens):
                        if live[gi]:
                            try:
                                next(gg)
                            except StopIteration:
                                live[gi] = False
                    step += 1
                    if pj_live and step % 12 == 0:
                        try:
                            next(pj)
                        except StopIteration:
                            pj_live = False
                drain(pj)

                if tt == 0:
                    ada_blocks(NADA1, NADA)
                    k.op("dve", lambda e: e.scalar_tensor_tensor(dvc(DV_A2, 0, 16), dvc(DV_SC2, 0, 16), 1.0, ppc(PC_G2, 0, 16),
                                                                 ALU.add, ALU.mult), reads=[B["dv"], B["pp"]], writes=[B["dv"]])

                for j2 in range(8):
                    s = next_block(NADA + 28 + 4 * j2 + 0, tt == 0)
                    for i in range(2):
                        p = next_pmm()
                        mm_group(p, proj_pairs(s, i, lambda kc: hT[:, kc, :]), reads=[b_wr[s]] + b_h)
                        k.op("act", lambda e, p=p, i=i: e.activation(sga[:, i, :], pmm[p][:], ACT.Exp, scale=-1.0),
                             reads=[b_pmm[p]], writes=[b_sga[i]])
                        k.op("act", lambda e, i=i: e.activation(sga[:, i, :], sga[:, i, :], ACT.Ln, bias=1.0),
                             reads=[b_sga[i]], writes=[b_sga[i]])
                        k.op("act", lambda e, i=i: e.activation(sga[:, i, :], sga[:, i, :], ACT.Exp, scale=-1.0),
                             reads=[b_sga[i]], writes=[b_sga[i]])
                    s = next_block(NADA + 28 + 4 * j2 + 1, tt == 0)
                    for i in range(2):
                        p = next_pmm()
                        mm_group(p, proj_pairs(s, i, lambda kc: big[:, kc, :]), reads=[b_wr[s]] + b_big[0:16])
                        k.op("dve", lambda e, p=p, i=i: e.tensor_tensor(sga[:, i, :], sga[:, i, :], pmm[p][:], ALU.mult),
                             reads=[b_sga[i], b_pmm[p]], writes=[b_sga[i]])
                    s = next_block(NADA + 28 + 4 * j2 + 2, tt == 0)
                    for i in range(2):
                        p = next_pmm()
                        mm_group(p, proj_pairs(s, i, lambda kc: hT[:, kc, :]), reads=[b_wr[s]] + b_h)
                        k.op("act", lambda e, p=p, i=i: e.activation(sgb[:, i, :], pmm[p][:], ACT.Exp, scale=-1.0),
                             reads=[b_pmm[p]], writes=[b_sgb[i]])
                        k.op("act", lambda e, i=i: e.activation(sgb[:, i, :], sgb[:, i, :], ACT.Ln, bias=1.0),
                             reads=[b_sgb[i]], writes=[b_sgb[i]])
                        k.op("act", lambda e, i=i: e.activation(sgb[:, i, :], sgb[:, i, :], ACT.Exp, scale=-1.0),
                             reads=[b_sgb[i]], writes=[b_sgb[i]])
                    s = next_block(NADA + 28 + 4 * j2 + 3, tt == 0)
                    for i in range(2):
                        j = 2 * j2 + i
                        p = next_pmm()
                        mm_group(p, proj_pairs(s, i, lambda kc: big[:, 16 + kc, :]), reads=[b_wr[s]] + b_big[16:32])
                        k.op("dve", lambda e, p=p, i=i: e.tensor_tensor(sgb[:, i, :], sgb[:, i, :], pmm[p][:], ALU.mult),
                             reads=[b_sgb[i], b_pmm[p]], writes=[b_sgb[i]])
                        k.op("dve", lambda e, i=i, j=j: e.tensor_tensor(big[:, 32 + j, :], sga[:, i, :], sgb[:, i, :], ALU.add),
                             reads=[b_sga[i], b_sgb[i]], writes=[b_big[32 + j]])

                for j2 in range(8):
                    s = next_block(NADA + 60 + j2, tt == 0)
                    for i in range(2):
                        j = 2 * j2 + i
                        p = next_pmm()
                        mm_group(p, proj_pairs(s, i, lambda kc: big[:, 32 + kc, :]), reads=[b_wr[s]] + b_big[32:48])
                        k.op("dve", lambda e, p=p, j=j: e.scalar_tensor_tensor(
                            xT[:, j, :], pmm[p][:], dvc(DV_GT1, j), xT[:, j, :], ALU.mult, ALU.add),
                            reads=[b_pmm[p], B["dv"], b_x[j]], writes=[b_x[j]])

                rmsnorm_stats()
                modulate(DV_A2, DV_SH2)

                for qf in range(4):
                    slot = (qf % 3) * 16
                    for b8 in range(8):
                        s = next_block(NADA + 68 + qf * 16 + b8, tt == 0)
                        for i in range(2):
                            fi = slot + 2 * b8 + i
                            p = next_pmm()
                            mm_group(p, proj_pairs(s, i, lambda kc: hT[:, kc, :]), reads=[b_wr[s]] + b_h)
                            k.op("act", lambda e, p=p, i=i: e.activation(rtmp[:, i, :], pmm[p][:], ACT.Relu),
                                 reads=[b_pmm[p]], writes=[b_rtmp[i]])
                            k.op("dve", lambda e, i=i, fi=fi: e.tensor_tensor(big[:, fi, :], rtmp[:, i, :], rtmp[:, i, :], ALU.mult),
                                 reads=[b_rtmp[i]], writes=[b_big[fi]])
                    for j2 in range(8):
                        s = next_block(NADA + 68 + qf * 16 + 8 + j2, tt == 0)
                        for i in range(2):
                            j = 2 * j2 + i
                            p = next_pmm()
                            mm_group(p, proj_pairs(s, i, lambda kc, slot=slot: big[:, slot + kc, :]),
                                     reads=[b_wr[s]] + b_big[slot:slot + 16])
                            k.op("dve", lambda e, p=p, j=j: e.scalar_tensor_tensor(
                                xT[:, j, :], pmm[p][:], dvc(DV_GT2, j), xT[:, j, :], ALU.mult, ALU.add),
                                reads=[b_pmm[p], B["dv"], b_x[j]], writes=[b_x[j]])

                rmsnorm_stats()
                for c in range(16):
                    q = c % 2
                    k.op("dve", lambda e, c=c, q=q: e.scalar_tensor_tensor(
                        ost[:, q, :], xT[:, c, :], ppc(PC_FG, c), rstd[:], ALU.mult, ALU.mult),
                        reads=[b_x[c], B["pp"], B["rstd"]], writes=[b_ost[q]])
                    k.dma("sp", "o%d" % q, out_d[c * 128:(c + 1) * 128, t0:t0 + T], ost[:, q, :], reads=[b_ost[q]])

            return req_log

        k1 = KB(nc)
        order = program(k1, None)
        k = KB(nc)
        order2 = program(k, order)
        assert order2 == order
        k.final_wait("sp")
        k.emit()
    return nc


def _t5_bucket_table():
    qi = np.arange(128)[:, None]
    ki = np.arange(256)[None, :]
    rel = qi + 128 - ki
    relc = np.maximum(rel, 0)
    max_exact = 16
    relf = np.maximum(relc, 1).astype(np.float32)
    large = max_exact + (np.log(relf / np.float32(max_exact)) / np.float32(math.log(128 / max_exact))
                         * np.float32(32 - max_exact)).astype(np.int32)
    large = np.minimum(large, 31)
    bucket = np.where(relc < max_exact, relc, large)
    valid = (rel >= 0) & (rel < 128)
    return bucket, valid


def prep_shared(inp):
    W = {
        "w_ada": np.asarray(inp["w_ada"][0]), "w_in": np.asarray(inp["w_in"][0]),
        "w_lru_out": np.asarray(inp["w_lru_out"][0]), "w_attn_out": np.asarray(inp["w_attn_out"][0]),
        "w_out": np.asarray(inp["w_out"][0]), "w_ff1": np.asarray(inp["w_ff1"][0]), "w_ff2": np.asarray(inp["w_ff2"][0]),
    }
    specs = block_specs()
    wblk = np.zeros((NBLK, 128, 4096), np.float32)
    for bi, (name, r0, cols) in enumerate(specs):
        cols = np.asarray(cols)
        sub = np.zeros((2048, 256), np.float32)
        ok = cols >= 0
        sub[:, ok] = W[name][r0:r0 + 2048][:, cols[ok]]
        wblk[bi] = sub.reshape(16, 128, 256).transpose(1, 0, 2).reshape(128, 4096)

    def fm(v):
        return np.asarray(v, np.float32).reshape(-1, 128).T

    lruw = np.concatenate([np.asarray(inp["lru_wa"][0]).transpose(1, 0, 2)[:, :, None, :],
                           np.asarray(inp["lru_wx"][0]).transpose(1, 0, 2)[:, :, None, :]], axis=2)
    lruw = np.ascontiguousarray(lruw.reshape(128, 16 * 256), np.float32)
    bucket, valid = _t5_bucket_table()
    rb = np.asarray(inp["rel_bias"], np.float32)
    biasg = np.ascontiguousarray(rb[bucket].transpose(0, 2, 1).reshape(128, NHEAD * 256))
    maskc = np.where(valid, 0.0, NEG).astype(np.float32)
    ppbase = np.zeros((128, PC_N), np.float32)
    ppbase[:, PC_BADA:PC_BADA + 96] = fm(inp["b_ada"][0])
    ppbase[:, PC_G1:PC_G1 + 16] = fm(inp["norm1_g"][0])
    ppbase[:, PC_G2:PC_G2 + 16] = fm(inp["norm2_g"][0])
    ppbase[:, PC_FG:PC_FG + 16] = fm(inp["final_g"])
    cw = np.asarray(inp["conv_w"][0], np.float32)
    for kk in range(4):
        ppbase[:, PC_CW + kk * 16: PC_CW + kk * 16 + 16] = fm(cw[kk])
    ppbase[:, PC_CB:PC_CB + 16] = fm(inp["conv_b"][0])
    ppbase[:, PC_BA:PC_BA + 16] = fm(inp["lru_ba"][0])
    ppbase[:, PC_BX:PC_BX + 16] = fm(inp["lru_bx"][0])
    ppbase[:, PC_LAM:PC_LAM + 16] = fm(inp["lru_lambda"][0])
    return dict(wblk=wblk, lruw=lruw, biasg=biasg, maskc=maskc, ppbase=ppbase,
                sinks=np.asarray(inp["attn_sinks"], np.float32).reshape(1, NHEAD),
                idn=np.eye(128, dtype=np.float32))


def core_inputs(shared, x_b, c_b):
    pp = shared["ppbase"].copy()
    pp[:, PC_C:PC_C + 16] = np.asarray(c_b, np.float32).reshape(16, 128).T
    return {"xT": np.ascontiguousarray(np.asarray(x_b, np.float32).T), "wblk": shared["wblk"], "pp": pp,
            "lruw": shared["lruw"], "biasg": shared["biasg"], "maskc": shared["maskc"], "sinks": shared["sinks"],
            "idn": shared["idn"]}


_NC_CACHE = {}


def kernel(**inputs):
    x = np.asarray(inputs["x"])
    c = np.asarray(inputs["c"])
    shared = prep_shared(inputs)
    if 8 not in _NC_CACHE:
        _NC_CACHE[8] = build(8)
    nc = _NC_CACHE[8]
    in_maps = [core_inputs(shared, x[b], c[b]) for b in range(8)]
    res = run_bass_kernel_spmd(nc, in_maps, core_ids=list(range(8)))
    out = np.stack([np.asarray(r["outT"]).T for r in res.results], axis=0)
    return np.ascontiguousarray(out.astype(np.float32))
```

```python
import contextlib
import math
import os
import numpy as np
import concourse.bass as bass
import concourse.mybir as mybir
from concourse.bass_utils import run_bass_kernel_spmd

F32 = mybir.dt.float32
BF16 = mybir.dt.bfloat16
ALU = mybir.AluOpType
ACT = mybir.ActivationFunctionType
AX = mybir.AxisListType

D = 2048
S = 4096
T = 512
NCH = 16
NHEAD = 32
EPS = 1e-6
NADA = 48
NADA1 = 16
import os
LAG_D = int(os.environ.get('LAG_D', '1'))
LAG_F = int(os.environ.get('LAG_F', '2'))
NEG = -30000.0
PC_C, PC_BADA, PC_G1, PC_G2, PC_FG, PC_CW, PC_CB, PC_BA, PC_BX, PC_LAM, PC_N = 0, 16, 112, 128, 144, 160, 224, 240, 256, 272, 288
DV_SH1, DV_SC1, DV_GT1, DV_SH2, DV_SC2, DV_GT2, DV_A1, DV_A2, DV_CA, DV_CA2, DV_NBA, DV_NBX, DV_N = (
    0, 16, 32, 48, 64, 80, 96, 112, 128, 144, 160, 176, 192)
OFF_LX, OFF_LG, OFF_Q, OFF_K, OFF_V, OFF_GA, OFF_GB = 0, 2048, 4096, 6144, 6400, 6656, 8704


def block_specs():
    sp = []
    for ob in range(NADA):
        sp.append(("w_ada", 0, list(range(ob * 256, ob * 256 + 256))))
    for c in range(16):
        sp.append(("w_in", 0, list(range(OFF_LX + c * 128, OFF_LX + c * 128 + 128)) +
                   list(range(OFF_LG + c * 128, OFF_LG + c * 128 + 128))))
    for g in range(4):
        sp.append(("w_in", 0, list(range(OFF_Q + g * 512, OFF_Q + g * 512 + 256))))
        sp.append(("w_in", 0, list(range(OFF_Q + g * 512 + 256, OFF_Q + g * 512 + 512))))
        kc = list(range(OFF_K + g * 64, OFF_K + g * 64 + 64))
        sp.append(("w_in", 0, kc + kc + list(range(OFF_V + g * 64, OFF_V + g * 64 + 64)) + [-1] * 64))
    for j2 in range(8):
        sp.append(("w_in", 0, list(range(OFF_GA + j2 * 256, OFF_GA + j2 * 256 + 256))))
        sp.append(("w_lru_out", 0, list(range(j2 * 256, j2 * 256 + 256))))
        sp.append(("w_in", 0, list(range(OFF_GB + j2 * 256, OFF_GB + j2 * 256 + 256))))
        sp.append(("w_attn_out", 0, list(range(j2 * 256, j2 * 256 + 256))))
    for j2 in range(8):
        sp.append(("w_out", 0, list(range(j2 * 256, j2 * 256 + 256))))
    for qf in range(4):
        for b in range(8):
            sp.append(("w_ff1", 0, list(range(qf * 2048 + b * 256, qf * 2048 + b * 256 + 256))))
        for j2 in range(8):
            sp.append(("w_ff2", qf * 2048, list(range(j2 * 256, j2 * 256 + 256))))
    return sp


NBLK = NADA + 132


class Buf:
    __slots__ = ("name", "lw", "rd")

    def __init__(self, name):
        self.name = name
        self.lw = None
        self.rd = {}


class Stream:
    def __init__(self, name):
        self.name = name
        self.ops = []
        self.count = 0
        self.waited = {}


class KB:
    ENGS = ("pe", "act", "dve", "pool", "sp")

    def __init__(self, nc):
        self.nc = nc
        self.st = {n: Stream(n) for n in self.ENGS}
        self.dma_sems = {}
        self.sem_handles = {}

    def _need(self, s, ev, waits):
        if ev is None:
            return
        key, val = ev
        if s.waited.get(key, 0) >= val:
            return
        s.waited[key] = val
        waits.append((key, val))

    DEF_DUR = {"pe": 0.22, "act": 0.62, "dve": 0.62, "pool": 1.2, "sp": 0.1}

    def begin_region(self):
        self.rec = []

    def end_region(self):
        rec, self.rec = self.rec, None
        if not rec:
            return
        import heapq
        units = []
        cur = None
        for o in rec:
            if o["kind"] == "dma":
                assert cur is None
                units.append([o])
                continue
            if cur is not None and cur[-1]["eng"] != o["eng"]:
                raise AssertionError("unsignaled op run interrupted by another engine")
            if cur is None:
                cur = []
            cur.append(o)
            if o["signal"]:
                units.append(cur)
                cur = None
        assert cur is None
        n = len(units)
        lastw, readers = {}, {}
        preds = [set() for _ in range(n)]
        for ui, u in enumerate(units):
            for o in u:
                for b in o["reads"]:
                    w = lastw.get(id(b))
                    if w is not None and w != ui:
                        preds[ui].add(w)
                for b in o["writes"]:
                    w = lastw.get(id(b))
                    if w is not None and w != ui:
                        preds[ui].add(w)
                    for r in readers.get(id(b), ()):
                        if r != ui:
                            preds[ui].add(r)
            for o in u:
                for b in o["writes"]:
                    lastw[id(b)] = ui
                    readers[id(b)] = []
            for o in u:
                for b in o["reads"]:
                    if lastw.get(id(b)) != ui:
                        readers.setdefault(id(b), []).append(ui)
        succs = [[] for _ in range(n)]
        indeg = [len(p) for p in preds]
        for ui, p in enumerate(preds):
            for q in p:
                succs[q].append(ui)
        LAT = 0.25
        DMA_LAT = float(os.environ.get("DMA_LAT", "5.0"))
        free = {e: 0.0 for e in self.ENGS}
        fin = [0.0] * n
        rdy = [0.0] * n
        heaps = {e: [] for e in self.ENGS}
        for ui in range(n):
            if indeg[ui] == 0:
                heapq.heappush(heaps[units[ui][0]["eng"]], (0.0, ui))
        order = []
        done = 0
        while done < n:
            best = None
            for e in self.ENGS:
                h = heaps[e]
                if not h:
                    continue
                t_free = free[e]
                cand = h[0]
                st = max(t_free, cand[0])
                if best is None or (st, cand[1]) < (best[0], best[2]):
                    best = (st, e, cand[1])
            st, e, ui = best
            heapq.heappop(heaps[e])
            u = units[ui]
            dur = sum(o["dur"] for o in u)
            free[e] = st + dur
            if u[0]["kind"] == "dma":
                fin[ui] = st + dur + DMA_LAT
            else:
                fin[ui] = st + dur
            order.append((st, ui))
            done += 1
            for v in succs[ui]:
                lat = 0.0 if units[v][0]["eng"] == e and u[0]["kind"] != "dma" else LAT
                rdy[v] = max(rdy[v], fin[ui] + lat)
                indeg[v] -= 1
                if indeg[v] == 0:
                    heapq.heappush(heaps[units[v][0]["eng"]], (rdy[v], v))
        order.sort()
        self.last_region_span = order[-1][0] if order else 0.0
        for _, ui in order:
            for o in units[ui]:
                if o["kind"] == "dma":
                    self.dma(o["eng"], o["slot"], o["out"], o["in_"], reads=o["reads"], writes=o["writes"])
                else:
                    self.op(o["eng"], o["fn"], reads=o["reads"], writes=o["writes"], signal=o["signal"])

    def op(self, eng, fn, reads=(), writes=(), signal=True, dur=None):
        if getattr(self, "rec", None) is not None:
            self.rec.append(dict(kind="op", eng=eng, fn=fn, reads=list(reads), writes=list(writes), signal=signal,
                                 dur=self.DEF_DUR[eng] if dur is None else dur))
            return
        s = self.st[eng]
        waits = []
        for b in reads:
            if b.lw is not None:
                self._need(s, b.lw, waits)
        for b in writes:
            if b.lw is not None and b.lw[0] != eng:
                self._need(s, b.lw, waits)
            for k, v in b.rd.items():
                if k != eng:
                    self._need(s, (k, v), waits)
        keep = []
        for (k, v) in waits:
            if k == eng and v > s.count:
                s.waited[k] = s.count
                continue
            keep.append((k, v))
        nxt = s.count + 1
        ev = (eng, nxt)
        for b in writes:
            b.lw = ev
            b.rd = {}
        for b in reads:
            if b.rd.get(eng, 0) < nxt:
                b.rd[eng] = nxt
        if signal:
            s.count = nxt
        s.ops.append((keep, fn, eng if signal else None, 1))

    def dma(self, eng, slot, out, in_, reads=(), writes=()):
        if getattr(self, "rec", None) is not None:
            self.rec.append(dict(kind="dma", eng=eng, slot=slot, out=out, in_=in_, reads=list(reads),
                                 writes=list(writes), signal=True, dur=0.15))
            return
        s = self.st[eng]
        waits = []
        for b in reads:
            self._need(s, b.lw, waits)
        for b in writes:
            self._need(s, b.lw, waits)
            for k, v in b.rd.items():
                self._need(s, (k, v), waits)
        key = "dma_" + slot
        val = self.dma_sems.get(key, 0) + 16
        self.dma_sems[key] = val
        ev = (key, val)
        for b in writes:
            b.lw = ev
            b.rd = {}
        for b in reads:
            b.rd[key] = val

        def fn(e, out=out, in_=in_):
            return e.dma_start(out=out, in_=in_)
        s.ops.append((waits, fn, key, 16))

    def final_wait(self, eng):
        s = self.st[eng]
        waits = []
        for n in self.ENGS:
            if n != eng and self.st[n].count > 0:
                self._need(s, (n, self.st[n].count), waits)
        for k, v in self.dma_sems.items():
            self._need(s, (k, v), waits)
        s.ops.append((waits, None, None, 0))

    def emit(self):
        nc = self.nc
        with contextlib.ExitStack() as es:
            keys = list(self.ENGS) + list(self.dma_sems.keys())
            for k in keys:
                self.sem_handles[k] = es.enter_context(nc.semaphore("s_" + k))
            block = es.enter_context(nc.Block())
            H = self.sem_handles

            def run(stream):
                def body(e):
                    for waits, fn, inc, amt in stream.ops:
                        for k, v in waits:
                            e.wait_ge(H[k], v)
                        if fn is None:
                            continue
                        ins = fn(e)
                        if inc is not None:
                            ins.then_inc(H[inc], amt)
                return body

            block.tensor(run(self.st["pe"]))
            block.scalar(run(self.st["act"]))
            block.vector(run(self.st["dve"]))
            block.gpsimd(run(self.st["pool"]))
            block.sync(run(self.st["sp"]))


def build(NT=8):
    nc = bass.Bass("TRN2", target_bir_lowering=False)
    xT_d = nc.dram_tensor("xT", [D, S], F32, kind="ExternalInput").ap()
    wblk_d = nc.dram_tensor("wblk", [NBLK, 128, 4096], F32, kind="ExternalInput").ap()
    pp_d = nc.dram_tensor("pp", [128, PC_N], F32, kind="ExternalInput").ap()
    lruw_d = nc.dram_tensor("lruw", [128, 16 * 256], F32, kind="ExternalInput").ap()
    biasg_d = nc.dram_tensor("biasg", [128, NHEAD * 256], F32, kind="ExternalInput").ap()
    maskc_d = nc.dram_tensor("maskc", [128, 256], F32, kind="ExternalInput").ap()
    sinks_d = nc.dram_tensor("sinks", [1, NHEAD], F32, kind="ExternalInput").ap()
    idn_d = nc.dram_tensor("idn", [128, 128], F32, kind="ExternalInput").ap()
    out_d = nc.dram_tensor("outT", [D, S], F32, kind="ExternalOutput").ap()
    wscr_d = nc.dram_tensor("wscr", [NBLK, 128, 4096], BF16).ap()

    with contextlib.ExitStack() as es:
        def sb(n, s, d):
            return es.enter_context(nc.sbuf_tensor(n, s, d))

        def ps(n, s, d):
            return es.enter_context(nc.psum_tensor(n, s, d))

        NRING = 3
        wring = sb("wring", [128, NRING, 4096], BF16)
        xT = sb("xTs", [128, 16, 512], F32)
        hT = sb("hTs", [128, 16, 512], BF16)
        big = sb("big", [128, 48, 512], BF16)
        bm = sb("bm", [128, NHEAD, 256], BF16)
        lruw = sb("lruws", [128, 16, 256], BF16)
        pp = sb("pps", [128, PC_N], F32)
        dv = sb("dvs", [128, DV_N], F32)
        cact = sb("cact", [128, 16], BF16)
        ctmp = sb("ctmp", [128, 16], F32)
        ident_b = sb("ident_b", [128, 128], BF16)
        ones_f = sb("ones_f", [128, 128], F32)
        sinks_bc = sb("sinks_bc", [128, NHEAD], F32)
        nsink = sb("nsink", [128, NHEAD], F32)
        maskc = sb("maskcs", [128, 256], F32)
        hist = sb("hist", [128, 16, 4], F32)
        hstate = sb("hstate", [128, 16], F32)
        kbuf = sb("kbuf", [128, 2, 4, 640], BF16)
        Sn = sb("Sn", [128, 2, 2, 256], F32)
        vbuf = sb("vbuf", [128, 4, 5, 64], BF16)
        sqt = sb("sqt", [128, 2, 512], F32)
        rstd = sb("rstd", [128, 512], F32)
        lxb2 = [sb("lxb", [128, 516], F32)] * 2
        xc2 = [sb("xc", [128, 512], F32)] * 2
        xcb2 = [sb("xcb", [128, 512], BF16)] * 2
        tr2 = [sb("tr", [128, 512], F32)] * 2
        ta2 = [sb("ta", [128, 512], F32)] * 2
        ti2 = [sb("ti", [128, 512], F32)] * 2
        rec = sb("rec", [128, 512], F32)
        gxs = sb("gxs", [128, 512], F32)
        gsq = sb("gsq", [128, 512], F32)
        gt = sb("gt", [128, 512], F32)
        lnv = gt
        sga = sb("sga", [128, 2, 512], F32)
        sgb = sb("sgb", [128, 2, 512], F32)
        ntmp = sgb
        ost = sqt
        rtmp = sga
        pbuf = sb("pbuf", [128, 2, 2, 256], BF16)
        pTs = sb("pTs", [128, 2, 512], BF16)
        mx = sb("mx", [128, 2, 2], F32)
        negm = sb("negm", [128, 2, 8], F32)
        rsum = sb("rsum", [128, 2, 8], F32)
        etmp = sb("etmp", [128, 2, 8], F32)
        rden = sb("rden", [128, 2, 8], F32)
        atok = sb("atok", [128, 512], BF16)

        NPM = int(os.environ.get('NPM', '4'))
        pmm = [ps("pmm%d" % i, [128, 512], F32) for i in range(NPM)]
        pS = [ps("pS%d" % i, [128, 512], F32) for i in range(2)]
        if os.environ.get('PT2', '0') == '1':
            pTb = [ps("pT%d" % i, [128, 1024], BF16) for i in range(2)]
            pTv = lambda par: pTv(par)
        else:
            pTone = ps("pTone", [128, 1024], BF16)
            pTv = lambda par: pTone[:, par * 512:(par + 1) * 512]
        pO = ps("pO", [128, 512], F32)

        def program(k, full):
            req_log = []
            b_wr = [Buf("wr%d" % i) for i in range(NRING)]
            b_scr = [Buf("scr%d" % i) for i in range(NBLK)]
            b_x = [Buf("x%d" % i) for i in range(16)]
            b_h = [Buf("h%d" % i) for i in range(16)]
            b_big = [Buf("big%d" % i) for i in range(48)]
            b_pmm = [Buf("pmm%d" % i) for i in range(NPM)]
            b_pS = [Buf("pS%d" % i) for i in range(2)]
            _bpt = Buf("pT")
            b_pT = [_bpt, _bpt]
            b_pO = Buf("pO")
            names = ["bm", "lruw", "pp", "dv", "cact", "ctmp", "ident", "ones", "sinks", "nsink", "maskc", "hist",
                     "hstate", "lnv", "rstd", "lxb", "xc", "xcb", "tr", "ta", "ti", "rec", "gxs", "gsq", "gt",
                     "negm", "rsum", "etmp", "rden", "atok"]
            B = {n: Buf(n) for n in names}
            B2 = [{n: Buf(n) for n in ("lxb", "xc", "xcb", "tr", "ta", "ti")}] * 2
            b_kb = [Buf("kb%d" % i) for i in range(4)]
            b_vb = [Buf("vb%d" % i) for i in range(4)]
            b_sqt = [Buf("sqt%d" % i) for i in range(2)]
            b_sga = [Buf("sga%d" % i) for i in range(2)]
            b_sgb = [Buf("sgb%d" % i) for i in range(2)]
            b_ntmp = b_sgb
            b_ost = b_sqt
            b_rtmp = b_sga
            b_pb = [Buf("pb%d" % i) for i in range(2)]
            b_pTs = [Buf("pTs%d" % i) for i in range(2)]
            b_mx = [Buf("mx%d" % i) for i in range(2)]
            b_Sn = [Buf("Sn%d" % i) for i in range(2)]
            b_negm = [[Buf("negm%d_%d" % (ip, i)) for i in range(4)] for ip in range(2)]
            b_rsum = [Buf("rsum%d" % i) for i in range(2)]
            b_etmp = [Buf("etmp%d" % i) for i in range(2)]
            b_rden = [Buf("rden%d" % i) for i in range(2)]

            def dvc(col, c=0, n=1):
                return dv[:, col + c: col + c + n]

            def ppc(col, c=0, n=1):
                return pp[:, col + c: col + c + n]

            state = {"seq": 0, "pm": 0}
            sched = []

            def issue_load(seq, blk, tt_):
                s = seq % NRING
                if blk < NADA:
                    mode = "c"
                elif tt_ == 0:
                    mode = "cs" if blk % 2 == 0 else "c"
                elif tt_ == 1:
                    mode = "s" if blk % 2 == 0 else "cs"
                else:
                    mode = "s"
                if mode == "cs" and NT <= tt_ + 1:
                    mode = "c"
                if mode in ("c", "cs"):
                    k.dma("pool", "w%d" % s, wring[:, s, :], wblk_d[blk], writes=[b_wr[s]])
                    if mode == "cs":
                        k.dma("sp", "ws%d" % s, wscr_d[blk], wring[:, s, :], reads=[b_wr[s]], writes=[b_scr[blk]])
                else:
                    k.dma("sp", "w%d" % s, wring[:, s, :], wscr_d[blk], reads=[b_scr[blk]], writes=[b_wr[s]])

            PRE = NRING - 1
            issued = {"n": 0}

            def next_block(blk, first):
                seq = state["seq"]
                req_log.append((blk, first))
                if full is not None:
                    assert full[seq] == (blk, first)
                    while issued["n"] < min(len(full), seq + PRE + 1):
                        n = issued["n"]
                        issue_load(n, full[n][0], full[n][1])
                        issued["n"] += 1
                state["seq"] = seq + 1
                return seq % NRING

            held = set()

            def next_pmm(hold=False):
                while True:
                    i = state["pm"] % NPM
                    state["pm"] += 1
                    if i not in held:
                        break
                if hold:
                    held.add(i)
                return i

            def mm_group(pi, pairs, reads, ncols=512, col0=0, sig_all=False):
                n = len(pairs)
                for idx, (l, r) in enumerate(pairs):
                    last = idx == n - 1
                    k.op("pe", lambda e, l=l, r=r, idx=idx, last=last: e.matmul(
                        pmm[pi][:, col0:col0 + ncols], l, r, start=(idx == 0), stop=last),
                        reads=reads[idx] if isinstance(reads, dict) else reads,
                        writes=[b_pmm[pi]], signal=(last or sig_all))

            k.dma("sp", "c0", pp[:], pp_d, writes=[B["pp"]])
            k.dma("sp", "c1", maskc[:], maskc_d, writes=[B["maskc"]])
            k.dma("sp", "c2", sinks_bc[:], sinks_d.broadcast_to([128, NHEAD]), writes=[B["sinks"]])
            k.dma("pool", "c3", ident_b[:], idn_d, writes=[B["ident"]])
            k.dma("pool", "c4", lruw[:].rearrange("p a b -> p (a b)"), lruw_d, writes=[B["lruw"]])
            xflat = xT[:].rearrange("p a b -> p (a b)")
            k.dma("sp", "c5", xflat, biasg_d, writes=b_x)
            k.op("dve", lambda e: e.memset(ones_f[:], 1.0), writes=[B["ones"]])
            k.op("dve", lambda e: e.memset(hist[:].rearrange("p a b -> p (a b)"), 0.0), writes=[B["hist"]])
            k.op("dve", lambda e: e.memset(hstate[:], 0.0), writes=[B["hstate"]])
            k.op("dve", lambda e: e.memset(kbuf[:].rearrange("p a b c -> p (a b c)"), 0.0), writes=b_kb)
            k.op("dve", lambda e: e.memset(vbuf[:].rearrange("p a b c -> p (a b c)"), 0.0), writes=b_vb)
            for h in range(NHEAD):
                k.op("dve", lambda e, h=h: e.tensor_tensor(bm[:, h, :], xflat[:, h * 256:(h + 1) * 256], maskc[:], ALU.add),
                     reads=b_x + [B["maskc"]], writes=[B["bm"]])
            k.op("dve", lambda e: e.tensor_scalar(nsink[:], sinks_bc[:], -1.0, None, ALU.mult),
                 reads=[B["sinks"]], writes=[B["nsink"]])
            k.op("act", lambda e: e.activation(ctmp[:], ppc(PC_C, 0, 16), ACT.Exp, scale=-1.0),
                 reads=[B["pp"]], writes=[B["ctmp"]])
            k.op("dve", lambda e: e.tensor_scalar(ctmp[:], ctmp[:], 1.0, None, ALU.add), reads=[B["ctmp"]], writes=[B["ctmp"]])
            k.op("dve", lambda e: e.reciprocal(ctmp[:], ctmp[:]), reads=[B["ctmp"]], writes=[B["ctmp"]])
            k.op("dve", lambda e: e.tensor_tensor(cact[:], ctmp[:], ppc(PC_C, 0, 16), ALU.mult),
                 reads=[B["ctmp"], B["pp"]], writes=[B["cact"]])
            def ada_blocks(ob0, ob1):
                for ob in range(ob0, ob1):
                    s = next_block(ob, -1)
                    for i in range(2):
                        oc = ob * 2 + i
                        for kc in range(16):
                            k.op("pe", lambda e, s=s, i=i, kc=kc, oc=oc: e.matmul(
                                pO[:, oc:oc + 1], wring[:, s, kc * 256 + i * 128: kc * 256 + i * 128 + 128],
                                cact[:, kc:kc + 1], start=(kc == 0), stop=(kc == 15)),
                                reads=[b_wr[s], B["cact"]], writes=[b_pO], signal=(kc == 15))
                c0, c1 = ob0 * 2, ob1 * 2
                k.op("dve", lambda e: e.tensor_tensor(dv[:, c0:c1], pO[:, c0:c1], ppc(PC_BADA, c0, c1 - c0), ALU.add),
                     reads=[b_pO, B["pp"]], writes=[B["dv"]])

            ada_blocks(0, NADA1)
            k.op("dve", lambda e: e.scalar_tensor_tensor(dvc(DV_A1, 0, 16), dvc(DV_SC1, 0, 16), 1.0, ppc(PC_G1, 0, 16),
                                                         ALU.add, ALU.mult), reads=[B["dv"], B["pp"]], writes=[B["dv"]])
            k.op("act", lambda e: e.activation(ctmp[:], ppc(PC_LAM, 0, 16), ACT.Exp, scale=-1.0),
                 reads=[B["pp"], B["cact"]], writes=[B["ctmp"]])
            k.op("act", lambda e: e.activation(ctmp[:], ctmp[:], ACT.Ln, bias=1.0), reads=[B["ctmp"]], writes=[B["ctmp"]])
            k.op("dve", lambda e: e.tensor_scalar(dvc(DV_CA, 0, 16), ctmp[:], -8.0, None, ALU.mult),
                 reads=[B["ctmp"]], writes=[B["dv"]])
            k.op("dve", lambda e: e.tensor_scalar(dvc(DV_CA2, 0, 16), ctmp[:], -16.0, None, ALU.mult),
                 reads=[B["ctmp"]], writes=[B["dv"]])
            k.op("dve", lambda e: e.tensor_scalar(dvc(DV_NBA, 0, 16), ppc(PC_BA, 0, 16), -1.0, None, ALU.mult),
                 reads=[B["pp"]], writes=[B["dv"]])
            k.op("dve", lambda e: e.tensor_scalar(dvc(DV_NBX, 0, 16), ppc(PC_BX, 0, 16), -1.0, None, ALU.mult),
                 reads=[B["pp"]], writes=[B["dv"]])

            def rmsnorm_stats():
                pi = next_pmm()
                for c in range(16):
                    q = c % 2
                    k.op("act", lambda e, c=c, q=q: e.activation(sqt[:, q, :], xT[:, c, :], ACT.Square),
                         reads=[b_x[c]], writes=[b_sqt[q]])
                    k.op("pe", lambda e, c=c, q=q, pi=pi: e.matmul(pmm[pi][:], ones_f[:], sqt[:, q, :],
                                                                  start=(c == 0), stop=(c == 15)),
                         reads=[b_sqt[q], B["ones"]], writes=[b_pmm[pi]], signal=True)
                k.op("act", lambda e, pi=pi: e.activation(lnv[:], pmm[pi][:], ACT.Ln, bias=EPS, scale=1.0 / D),
                     reads=[b_pmm[pi]], writes=[B["gt"]])
                k.op("act", lambda e: e.activation(rstd[:], lnv[:], ACT.Exp, scale=-0.5),
                     reads=[B["gt"]], writes=[B["rstd"]])

            def modulate(acol, scol):
                for c in range(16):
                    q = c % 2
                    k.op("dve", lambda e, c=c, q=q: e.tensor_tensor(ntmp[:, q, :], xT[:, c, :], rstd[:], ALU.mult),
                         reads=[b_x[c], B["rstd"]], writes=[b_ntmp[q]])
                    k.op("act", lambda e, c=c, q=q: e.activation(hT[:, c, :], ntmp[:, q, :], ACT.Identity,
                                                                 bias=dvc(scol, c), scale=dvc(acol, c)),
                         reads=[b_ntmp[q], B["dv"]], writes=[b_h[c]])

            def proj_pairs(s, i, src, srcbufs=None):
                return [(wring[:, s, kc * 256 + i * 128: kc * 256 + i * 128 + 128], src(kc)) for kc in range(16)]

            C0 = math.sqrt(2.0 / math.pi)
            C1 = 0.044715

            k.begin_region()
            for tt in range(NT):
                t0 = tt * T
                for c in range(16):
                    k.dma("sp", "x%d" % c, xT[:, c, :], xT_d[c * 128:(c + 1) * 128, t0:t0 + T], writes=[b_x[c]])
                rmsnorm_stats()
                modulate(DV_A1, DV_SH1)

                def lru_chunk(c):
                    cp = c % 2
                    lxb, xc, xcb, tr, ta, ti = lxb2[cp], xc2[cp], xcb2[cp], tr2[cp], ta2[cp], ti2[cp]
                    BL = B2[cp]
                    s = next_block(NADA + c, tt)
                    p_lx = next_pmm(hold=True)
                    mm_group(p_lx, proj_pairs(s, 0, lambda kc: hT[:, kc, :]), reads={kc: [b_wr[s], b_h[kc]] for kc in range(16)})
                    p_lg = next_pmm(hold=True)
                    mm_group(p_lg, proj_pairs(s, 1, lambda kc: hT[:, kc, :]), reads={kc: [b_wr[s], b_h[kc]] for kc in range(16)})
                    yield
                    k.op("dve", lambda e, c=c: e.tensor_copy(lxb[:, 0:3], hist[:, c, 0:3]),
                         reads=[B["hist"]], writes=[BL["lxb"]], dur=0.12)
                    k.op("act", lambda e, p=p_lx: e.activation(lxb[:, 3:515], pmm[p][:], ACT.Copy),
                         reads=[b_pmm[p_lx]], writes=[BL["lxb"]])
                    k.op("act", lambda e, p=p_lg: e.activation(gxs[:], pmm[p][:], ACT.Copy), reads=[b_pmm[p_lg]], writes=[B["gxs"]])
                    k.op("act", lambda e, p=p_lg: e.activation(gsq[:], pmm[p][:], ACT.Square), reads=[b_pmm[p_lg]], writes=[B["gsq"]])
                    held.discard(p_lx)
                    held.discard(p_lg)
                    yield
                    k.op("dve", lambda e, c=c: e.tensor_copy(hist[:, c, 0:3], lxb[:, 512:515]),
                         reads=[BL["lxb"]], writes=[B["hist"]], dur=0.12)
                    k.op("dve", lambda e, c=c: e.tensor_scalar(xc[:], lxb[:, 3:515], ppc(PC_CW, 3 * 16 + c), ppc(PC_CB, c),
                                                              ALU.mult, ALU.add),
                         reads=[BL["lxb"], B["pp"]], writes=[BL["xc"]])
                    k.op("dve", lambda e: e.tensor_scalar(gt[:], gsq[:], C1, 1.0, ALU.mult, ALU.add), reads=[B["gsq"]], writes=[B["gt"]])
                    k.op("dve", lambda e, c=c: e.scalar_tensor_tensor(
                        xc[:], lxb[:, 2:514], ppc(PC_CW, 2 * 16 + c), xc[:], ALU.mult, ALU.add),
                        reads=[BL["lxb"], B["pp"], BL["xc"]], writes=[BL["xc"]])
                    yield
                    k.op("dve", lambda e: e.tensor_tensor(gt[:], gt[:], gxs[:], ALU.mult), reads=[B["gt"], B["gxs"]], writes=[B["gt"]])
                    for kk in (1, 0):
                        k.op("dve", lambda e, c=c, kk=kk: e.scalar_tensor_tensor(
                            xc[:], lxb[:, kk:kk + 512], ppc(PC_CW, kk * 16 + c), xc[:], ALU.mult, ALU.add),
                            reads=[BL["lxb"], B["pp"], BL["xc"]], writes=[BL["xc"]])
                    yield
                    k.op("act", lambda e: e.activation(gt[:], gt[:], ACT.Exp, scale=-2.0 * C0), reads=[B["gt"]], writes=[B["gt"]])
                    k.op("act", lambda e: e.activation(xcb[:], xc[:], ACT.Copy), reads=[BL["xc"]], writes=[BL["xcb"]])
                    k.op("act", lambda e: e.activation(gt[:], gt[:], ACT.Ln, bias=1.0), reads=[B["gt"]], writes=[B["gt"]])
                    yield
                    p_r = next_pmm(hold=True)
                    k.op("pe", lambda e, c=c, p=p_r: e.matmul(pmm[p][:], lruw[:, c, 0:128], xcb[:], start=True, stop=True),
                         reads=[B["lruw"], BL["xcb"]], writes=[b_pmm[p_r]])
                    p_i = next_pmm(hold=True)
                    k.op("pe", lambda e, c=c, p=p_i: e.matmul(pmm[p][:], lruw[:, c, 128:256], xcb[:], start=True, stop=True),
                         reads=[B["lruw"], BL["xcb"]], writes=[b_pmm[p_i]])
                    yield
                    k.op("act", lambda e, c=c, p=p_r: e.activation(tr[:], pmm[p][:], ACT.Exp, bias=dvc(DV_NBA, c), scale=-1.0),
                         reads=[b_pmm[p_r], B["dv"]], writes=[BL["tr"]])
                    k.op("act", lambda e, c=c, p=p_i: e.activation(ti[:], pmm[p][:], ACT.Exp, bias=dvc(DV_NBX, c), scale=-1.0),
                         reads=[b_pmm[p_i], B["dv"]], writes=[BL["ti"]])
                    held.discard(p_r)
                    held.discard(p_i)
                    k.op("act", lambda e: e.activation(gt[:], gt[:], ACT.Exp, scale=-1.0), reads=[B["gt"]], writes=[B["gt"]])
                    yield
                    k.op("act", lambda e: e.activation(tr[:], tr[:], ACT.Ln, bias=1.0), reads=[BL["tr"]], writes=[BL["tr"]])
                    k.op("act", lambda e: e.activation(ti[:], ti[:], ACT.Ln, bias=1.0), reads=[BL["ti"]], writes=[BL["ti"]])
                    k.op("dve", lambda e: e.tensor_tensor(gt[:], gt[:], gxs[:], ALU.mult), reads=[B["gt"], B["gxs"]], writes=[B["gt"]])
                    yield
                    k.op("act", lambda e: e.activation(tr[:], tr[:], ACT.Exp, scale=-1.0), reads=[BL["tr"]], writes=[BL["tr"]])
                    k.op("act", lambda e: e.activation(ti[:], ti[:], ACT.Exp, scale=-1.0), reads=[BL["ti"]], writes=[BL["ti"]])
                    yield
                    k.op("act", lambda e, c=c: e.activation(ta[:], tr[:], ACT.Exp, scale=dvc(DV_CA, c)),
                         reads=[BL["tr"], B["dv"]], writes=[BL["ta"]])
                    k.op("act", lambda e, c=c: e.activation(tr[:], tr[:], ACT.Exp, scale=dvc(DV_CA2, c)),
                         reads=[BL["tr"], B["dv"]], writes=[BL["tr"]])
                    k.op("dve", lambda e: e.tensor_tensor(ti[:], ti[:], xc[:], ALU.mult), reads=[BL["ti"], BL["xc"]], writes=[BL["ti"]])
                    yield
                    k.op("act", lambda e: e.activation(tr[:], tr[:], ACT.Ln, bias=1.0000002, scale=-1.0),
                         reads=[BL["tr"]], writes=[BL["tr"]])
                    yield
                    k.op("act", lambda e: e.activation(tr[:], tr[:], ACT.Exp, scale=0.5), reads=[BL["tr"]], writes=[BL["tr"]])
                    yield
                    if tt == 0:
                        k.op("dve", lambda e: e.memset(tr[:, 0:1], 1.0), reads=[BL["tr"]], writes=[BL["tr"]])
                    k.op("dve", lambda e: e.tensor_tensor(ti[:], ti[:], tr[:], ALU.mult), reads=[BL["ti"], BL["tr"]], writes=[BL["ti"]])
                    k.op("dve", lambda e, c=c: e.tensor_tensor_scan(rec[:], ta[:], ti[:], hstate[:, c:c + 1], ALU.mult, ALU.add),
                         reads=[BL["ta"], BL["ti"], B["hstate"]], writes=[B["rec"]], dur=1.15)
                    yield
                    k.op("dve", lambda e, c=c: e.tensor_copy(hstate[:, c:c + 1], rec[:, 511:512]),
                         reads=[B["rec"]], writes=[B["hstate"]], dur=0.12)
                    k.op("dve", lambda e, c=c: e.tensor_tensor(big[:, c, :], gt[:], rec[:], ALU.mult),
                         reads=[B["gt"], B["rec"]], writes=[b_big[c]])
                    yield

                def attn_proj(g):
                    for qb in range(2):
                        s = next_block(NADA + 16 + 3 * g + qb, tt)
                        for i in range(2):
                            ci = 32 + 4 * g + 2 * qb + i
                            p = next_pmm()
                            mm_group(p, proj_pairs(s, i, lambda kc: hT[:, kc, :]), reads={kc: [b_wr[s], b_h[kc]] for kc in range(16)})
                            k.op("act", lambda e, p=p, ci=ci: e.activation(big[:, ci, :], pmm[p][:], ACT.Copy, scale=0.125),
                                 reads=[b_pmm[p]], writes=[b_big[ci]])
                        yield
                    s = next_block(NADA + 16 + 3 * g + 2, tt)
                    p = next_pmm()
                    mm_group(p, proj_pairs(s, 0, lambda kc: hT[:, kc, :]), reads={kc: [b_wr[s], b_h[kc]] for kc in range(16)})
                    k.op("dve", lambda e, p=p, g=g: e.tensor_copy(kbuf[0:64, 0, g, 128:640], pmm[p][0:64, :]),
                         reads=[b_pmm[p]], writes=[b_kb[g]])
                    k.op("act", lambda e, p=p, g=g: e.activation(kbuf[64:128, 1, g, 128:640], pmm[p][64:128, :], ACT.Copy),
                         reads=[b_pmm[p]], writes=[b_kb[g]])
                    p = next_pmm()
                    for tb in range(4):
                        for kc in range(16):
                            k.op("pe", lambda e, p=p, s=s, tb=tb, kc=kc: e.matmul(
                                pmm[p][:, tb * 64:(tb + 1) * 64], hT[:, kc, tb * 128:(tb + 1) * 128],
                                wring[:, s, kc * 256 + 128: kc * 256 + 192], start=(kc == 0), stop=(kc == 15)),
                                reads=[b_wr[s]] + b_h, writes=[b_pmm[p]], signal=(kc == 15), dur=0.05)
                    k.op("act", lambda e, p=p, g=g: e.activation(
                        vbuf[:, g, 1:5, :].rearrange("p a b -> p (a b)"), pmm[p][:, 0:256], ACT.Copy),
                        reads=[b_pmm[p]], writes=[b_vb[g]])
                    yield

                def attn_hist(g):
                    if tt < NT - 1:
                        k.op("dve", lambda e, g=g: e.tensor_copy(kbuf[0:64, 0, g, 0:128], kbuf[0:64, 0, g, 512:640]),
                             reads=[b_kb[g]], writes=[b_kb[g]])
                        k.op("dve", lambda e, g=g: e.tensor_copy(kbuf[64:128, 1, g, 0:128], kbuf[64:128, 1, g, 512:640]),
                             reads=[b_kb[g]], writes=[b_kb[g]])
                        k.op("dve", lambda e, g=g: e.tensor_copy(vbuf[:, g, 0, :], vbuf[:, g, 4, :]),
                             reads=[b_vb[g]], writes=[b_vb[g]])

                rounds = [(g, n, rnd) for g in range(4) for n in range(4) for rnd in range(4)]

                def rinfo(i):
                    g, n, rnd = rounds[i]
                    first_blk = (tt == 0 and n == 0)
                    return dict(g=g, n=n, rnd=rnd, par=i % 2, ip=(i // 4) % 2, hp0=2 * rnd, first=first_blk,
                                kw=128 if first_blk else 256, kcol0=n * 128 + (128 if first_blk else 0),
                                bcol0=128 if first_blk else 0, nkb=1 if first_blk else 2)

                def st_A(i):
                    r = rinfo(i)
                    while pj_state["n"] < 3 * (r["g"] + 1):
                        next(pj)
                        pj_state["n"] += 1
                    g, n, par, kw, kcol0 = r["g"], r["n"], r["par"], r["kw"], r["kcol0"]
                    for hh in range(2):
                        j = r["hp0"] + hh
                        ci = 32 + 4 * g + j // 2
                        k.op("pe", lambda e, par=par, hh=hh, ci=ci, j=j, n=n, g=g, kcol0=kcol0, kw=kw: e.matmul(
                            pS[par][:, hh * 256: hh * 256 + kw], big[:, ci, n * 128:(n + 1) * 128],
                            kbuf[:, j % 2, g, kcol0:kcol0 + kw], start=True, stop=True),
                            reads=[b_big[ci], b_kb[g]], writes=[b_pS[par]], signal=(hh == 1), dur=0.17)

                def st_B(i):
                    r = rinfo(i)
                    g, par, kw, hp0, ip, rnd, bcol0 = r["g"], r["par"], r["kw"], r["hp0"], r["ip"], r["rnd"], r["bcol0"]
                    h0 = 8 * g + hp0
                    k.op("dve", lambda e, par=par, kw=kw, h0=h0, bcol0=bcol0: e.tensor_tensor(
                        Sn[:, par, :, 0:kw], pS[par][:].rearrange("p (a b) -> p a b", a=2)[:, :, 0:kw],
                        bm[:, h0:h0 + 2, bcol0:bcol0 + kw], ALU.add),
                        reads=[b_pS[par], B["bm"]], writes=[b_Sn[par]])
                    k.op("dve", lambda e, par=par, kw=kw: e.tensor_reduce(
                        mx[:, par, :], Sn[:, par, :, 0:kw], AX.X, ALU.max),
                        reads=[b_Sn[par]], writes=[b_mx[par]])
                    k.op("dve", lambda e, par=par, hp0=hp0, g=g, ip=ip: e.scalar_tensor_tensor(
                        negm[:, ip, hp0:hp0 + 2], mx[:, par, :], -1.0, nsink[:, 8 * g + hp0: 8 * g + hp0 + 2],
                        ALU.mult, ALU.min),
                        reads=[b_mx[par], B["nsink"]], writes=[b_negm[ip][rnd]], dur=0.12)

                def st_C(i):
                    r = rinfo(i)
                    g, par, kw, hp0, ip, rnd = r["g"], r["par"], r["kw"], r["hp0"], r["ip"], r["rnd"]
                    for hh in range(2):
                        j = hp0 + hh
                        k.op("act", lambda e, par=par, hh=hh, j=j, kw=kw, ip=ip: e.activation(
                            pbuf[:, par, hh, 0:kw], Sn[:, par, hh, 0:kw], ACT.Exp,
                            bias=negm[:, ip, j:j + 1], scale=1.0, accum_out=rsum[:, ip, j:j + 1]),
                            reads=[b_Sn[par], b_negm[ip][rnd]], writes=[b_pb[par], b_rsum[ip]], dur=0.5)
                    if rnd == 3:
                        k.op("dve", lambda e, g=g, ip=ip: e.tensor_tensor(
                            etmp[:, ip, :], sinks_bc[:, 8 * g:8 * g + 8], negm[:, ip, :], ALU.add),
                            reads=[B["sinks"]] + b_negm[ip], writes=[b_etmp[ip]], dur=0.12)
                        k.op("act", lambda e, ip=ip: e.activation(etmp[:, ip, :], etmp[:, ip, :], ACT.Exp),
                             reads=[b_etmp[ip]], writes=[b_etmp[ip]], dur=0.12)
                        k.op("dve", lambda e, ip=ip: e.tensor_tensor(rden[:, ip, :], rsum[:, ip, :], etmp[:, ip, :], ALU.add),
                             reads=[b_rsum[ip], b_etmp[ip]], writes=[b_rden[ip]], dur=0.12)
                        k.op("dve", lambda e, ip=ip: e.reciprocal(rden[:, ip, :], rden[:, ip, :]),
                             reads=[b_rden[ip]], writes=[b_rden[ip]], dur=0.12)

                def st_D(i):
                    r = rinfo(i)
                    par, nkb = r["par"], r["nkb"]
                    for hh in range(2):
                        for kb in range(nkb):
                            col = (hh * 2 + kb) * 128
                            k.op("pe", lambda e, par=par, hh=hh, kb=kb, col=col: e.transpose(
                                pTv(par)[:, col: col + 128],
                                pbuf[:, par, hh, kb * 128:(kb + 1) * 128], ident_b[:]),
                                reads=[b_pb[par], B["ident"]], writes=[b_pT[par]],
                                signal=(hh == 1 and kb == nkb - 1), dur=0.12)

                def st_E(i):
                    par = rinfo(i)["par"]
                    k.op("dve", lambda e, par=par: e.tensor_copy(pTs[:, par, :], pTv(par)),
                         reads=[b_pT[par]], writes=[b_pTs[par]])

                def st_F(i):
                    r = rinfo(i)
                    g, n, par, nkb, hp0, first_blk = r["g"], r["n"], r["par"], r["nkb"], r["hp0"], r["first"]
                    for hh in range(2):
                        j = hp0 + hh
                        for kb in range(nkb):
                            col = (hh * 2 + kb) * 128
                            vblk = n + kb + (1 if first_blk else 0)
                            k.op("pe", lambda e, par=par, j=j, kb=kb, col=col, vblk=vblk, g=g, nkb=nkb: e.matmul(
                                pO[:, j * 64:(j + 1) * 64], pTs[:, par, col:col + 128], vbuf[:, g, vblk, :],
                                start=(kb == 0), stop=(kb == nkb - 1)),
                                reads=[b_pTs[par], b_vb[g]], writes=[b_pO],
                                signal=(kb == nkb - 1), dur=0.05)

                def st_G(i):
                    r = rinfo(i)
                    g, n, ip, par = r["g"], r["n"], r["ip"], r["par"]
                    pT2 = pTv(par)
                    b_pT2 = b_pT[par]
                    k.op("dve", lambda e, ip=ip: e.tensor_tensor(
                        atok[:].rearrange("p (a b) -> p a b", a=8), pO[:].rearrange("p (a b) -> p a b", a=8),
                        rden[:, ip, :].unsqueeze(2).broadcast_to([128, 8, 64]), ALU.mult),
                        reads=[b_pO, b_rden[ip]], writes=[B["atok"]])
                    yield
                    for q4 in range(4):
                        k.op("pe", lambda e, q4=q4: e.transpose(
                            pT2[:, q4 * 128:(q4 + 1) * 128], atok[:, q4 * 128:(q4 + 1) * 128], ident_b[:]),
                            reads=[B["atok"], B["ident"]], writes=[b_pT2], signal=(q4 == 3), dur=0.12)
                    yield
                    k.op("act", lambda e, g=g, n=n: e.activation(
                        big[:, 16 + 4 * g:16 + 4 * g + 4, n * 128:(n + 1) * 128],
                        pT2.rearrange("p (a b) -> p a b", a=4), ACT.Copy),
                        reads=[b_pT2], writes=[b_big[16 + 4 * g + q4] for q4 in range(4)])
                    if n == 3:
                        attn_hist(g)
                    yield

                def attn_stream():
                    NR = len(rounds)
                    for i in range(NR + LAG_F):
                        if i < NR:
                            st_A(i)
                            yield
                            st_B(i)
                            yield
                            st_C(i)
                            yield
                        if 0 <= i - LAG_D < NR:
                            st_D(i - LAG_D)
                            yield
                            st_E(i - LAG_D)
                            yield
                        if 0 <= i - LAG_F < NR:
                            st_F(i - LAG_F)
                            yield
                            if rounds[i - LAG_F][2] == 3:
                                yield from st_G(i - LAG_F)

                def proj_stream():
                    for g in range(4):
                        yield from attn_proj(g)

                def lru_stream():
                    for c in range(16):
                        yield from lru_chunk(c)

                def drain(gen):
                    for _ in gen:
                        pass

                pj = proj_stream()
                pj_state = {"n": 0}
                gens = [lru_stream(), attn_stream()]
                live = [True, True]
                step = 0
                pj_live = True
                reps = [int(v) for v in os.environ.get('ILV', '1:1').split(':')]
                if reps[1] == 0:
                    st_ = 0
                    for _ in gens[0]:
                        st_ += 1
                        if st_ % 12 == 0:
                            for _ in range(1):
                                try:
                                    next(pj)
                                    pj_state["n"] += 1
                                except StopIteration:
                                    pass
                    live[0] = False
                    reps = [1, 1]
                while any(live):
                    for gi, gg in enumerate(gens):
                        for _ in range(reps[gi]):
                            if live[gi]:
                                try:
                                    next(gg)
                                except StopIteration:
                                    live[gi] = False
                    step += 1
                    if pj_live and step % 12 == 0:
                        try:
                            next(pj)
                            pj_state["n"] += 1
                        except StopIteration:
                            pj_live = False
                drain(pj)

                if tt == 0:
                    ada_blocks(NADA1, NADA)
                    k.op("dve", lambda e: e.scalar_tensor_tensor(dvc(DV_A2, 0, 16), dvc(DV_SC2, 0, 16), 1.0, ppc(PC_G2, 0, 16),
                                                                 ALU.add, ALU.mult), reads=[B["dv"], B["pp"]], writes=[B["dv"]])

                for j2 in range(8):
                    s = next_block(NADA + 28 + 4 * j2 + 0, tt)
                    for i in range(2):
                        p = next_pmm()
                        mm_group(p, proj_pairs(s, i, lambda kc: hT[:, kc, :]), reads={kc: [b_wr[s], b_h[kc]] for kc in range(16)})
                        k.op("act", lambda e, p=p, i=i: e.activation(sga[:, i, :], pmm[p][:], ACT.Exp, scale=-1.0),
                             reads=[b_pmm[p]], writes=[b_sga[i]])
                        k.op("act", lambda e, i=i: e.activation(sga[:, i, :], sga[:, i, :], ACT.Ln, bias=1.0),
                             reads=[b_sga[i]], writes=[b_sga[i]])
                        k.op("act", lambda e, i=i: e.activation(sga[:, i, :], sga[:, i, :], ACT.Exp, scale=-1.0),
                             reads=[b_sga[i]], writes=[b_sga[i]])
                    s = next_block(NADA + 28 + 4 * j2 + 1, tt)
                    for i in range(2):
                        p = next_pmm()
                        mm_group(p, proj_pairs(s, i, lambda kc: big[:, kc, :]), reads={kc: [b_wr[s], b_big[kc]] for kc in range(16)})
                        k.op("dve", lambda e, p=p, i=i: e.tensor_tensor(sga[:, i, :], sga[:, i, :], pmm[p][:], ALU.mult),
                             reads=[b_sga[i], b_pmm[p]], writes=[b_sga[i]])
                    s = next_block(NADA + 28 + 4 * j2 + 2, tt)
                    for i in range(2):
                        p = next_pmm()
                        mm_group(p, proj_pairs(s, i, lambda kc: hT[:, kc, :]), reads={kc: [b_wr[s], b_h[kc]] for kc in range(16)})
                        k.op("act", lambda e, p=p, i=i: e.activation(sgb[:, i, :], pmm[p][:], ACT.Exp, scale=-1.0),
                             reads=[b_pmm[p]], writes=[b_sgb[i]])
                        k.op("act", lambda e, i=i: e.activation(sgb[:, i, :], sgb[:, i, :], ACT.Ln, bias=1.0),
                             reads=[b_sgb[i]], writes=[b_sgb[i]])
                        k.op("act", lambda e, i=i: e.activation(sgb[:, i, :], sgb[:, i, :], ACT.Exp, scale=-1.0),
                             reads=[b_sgb[i]], writes=[b_sgb[i]])
                    s = next_block(NADA + 28 + 4 * j2 + 3, tt)
                    for i in range(2):
                        j = 2 * j2 + i
                        p = next_pmm()
                        mm_group(p, proj_pairs(s, i, lambda kc: big[:, 16 + kc, :]), reads={kc: [b_wr[s], b_big[16 + kc]] for kc in range(16)})
                        k.op("dve", lambda e, p=p, i=i: e.tensor_tensor(sgb[:, i, :], sgb[:, i, :], pmm[p][:], ALU.mult),
                             reads=[b_sgb[i], b_pmm[p]], writes=[b_sgb[i]])
                        k.op("dve", lambda e, i=i, j=j: e.tensor_tensor(big[:, 32 + j, :], sga[:, i, :], sgb[:, i, :], ALU.add),
                             reads=[b_sga[i], b_sgb[i]], writes=[b_big[32 + j]])

                for j2 in range(8):
                    s = next_block(NADA + 60 + j2, tt)
                    for i in range(2):
                        j = 2 * j2 + i
                        p = next_pmm()
                        mm_group(p, proj_pairs(s, i, lambda kc: big[:, 32 + kc, :]), reads={kc: [b_wr[s], b_big[32 + kc]] for kc in range(16)})
                        k.op("dve", lambda e, p=p, j=j: e.scalar_tensor_tensor(
                            xT[:, j, :], pmm[p][:], dvc(DV_GT1, j), xT[:, j, :], ALU.mult, ALU.add),
                            reads=[b_pmm[p], B["dv"], b_x[j]], writes=[b_x[j]])

                rmsnorm_stats()
                modulate(DV_A2, DV_SH2)

                for qf in range(4):
                    slot = (qf % 3) * 16
                    for b8 in range(8):
                        s = next_block(NADA + 68 + qf * 16 + b8, tt)
                        for i in range(2):
                            fi = slot + 2 * b8 + i
                            p = next_pmm()
                            mm_group(p, proj_pairs(s, i, lambda kc: hT[:, kc, :]), reads={kc: [b_wr[s], b_h[kc]] for kc in range(16)})
                            k.op("act", lambda e, p=p, i=i: e.activation(rtmp[:, i, :], pmm[p][:], ACT.Relu),
                                 reads=[b_pmm[p]], writes=[b_rtmp[i]])
                            k.op("dve", lambda e, i=i, fi=fi: e.tensor_tensor(big[:, fi, :], rtmp[:, i, :], rtmp[:, i, :], ALU.mult),
                                 reads=[b_rtmp[i]], writes=[b_big[fi]])
                    for j2 in range(8):
                        s = next_block(NADA + 68 + qf * 16 + 8 + j2, tt)
                        for i in range(2):
                            j = 2 * j2 + i
                            p = next_pmm()
                            mm_group(p, proj_pairs(s, i, lambda kc, slot=slot: big[:, slot + kc, :]),
                                     reads={kc: [b_wr[s], b_big[slot + kc]] for kc in range(16)})
                            k.op("dve", lambda e, p=p, j=j: e.scalar_tensor_tensor(
                                xT[:, j, :], pmm[p][:], dvc(DV_GT2, j), xT[:, j, :], ALU.mult, ALU.add),
                                reads=[b_pmm[p], B["dv"], b_x[j]], writes=[b_x[j]])

                rmsnorm_stats()
                for c in range(16):
                    q = c % 2
                    k.op("dve", lambda e, c=c, q=q: e.scalar_tensor_tensor(
                        ost[:, q, :], xT[:, c, :], ppc(PC_FG, c), rstd[:], ALU.mult, ALU.mult),
                        reads=[b_x[c], B["pp"], B["rstd"]], writes=[b_ost[q]])
                    k.dma("sp", "o%d" % q, out_d[c * 128:(c + 1) * 128, t0:t0 + T], ost[:, q, :], reads=[b_ost[q]])

            k.end_region()
            return req_log

        k1 = KB(nc)
        order = program(k1, None)
        k = KB(nc)
        order2 = program(k, order)
        assert order2 == order
        k.final_wait("sp")
        k.emit()
    return nc


def _t5_bucket_table():
    qi = np.arange(128)[:, None]
    ki = np.arange(256)[None, :]
    rel = qi + 128 - ki
    relc = np.maximum(rel, 0)
    max_exact = 16
    relf = np.maximum(relc, 1).astype(np.float32)
    large = max_exact + (np.log(relf / np.float32(max_exact)) / np.float32(math.log(128 / max_exact))
                         * np.float32(32 - max_exact)).astype(np.int32)
    large = np.minimum(large, 31)
    bucket = np.where(relc < max_exact, relc, large)
    valid = (rel >= 0) & (rel < 128)
    return bucket, valid


def prep_shared(inp):
    W = {
        "w_ada": np.asarray(inp["w_ada"][0]), "w_in": np.asarray(inp["w_in"][0]),
        "w_lru_out": np.asarray(inp["w_lru_out"][0]), "w_attn_out": np.asarray(inp["w_attn_out"][0]),
        "w_out": np.asarray(inp["w_out"][0]), "w_ff1": np.asarray(inp["w_ff1"][0]), "w_ff2": np.asarray(inp["w_ff2"][0]),
    }
    specs = block_specs()
    wblk = np.zeros((NBLK, 128, 4096), np.float32)
    for bi, (name, r0, cols) in enumerate(specs):
        cols = np.asarray(cols)
        sub = np.zeros((2048, 256), np.float32)
        ok = cols >= 0
        sub[:, ok] = W[name][r0:r0 + 2048][:, cols[ok]]
        wblk[bi] = sub.reshape(16, 128, 256).transpose(1, 0, 2).reshape(128, 4096)

    def fm(v):
        return np.asarray(v, np.float32).reshape(-1, 128).T

    lruw = np.concatenate([np.asarray(inp["lru_wa"][0]).transpose(1, 0, 2)[:, :, None, :],
                           np.asarray(inp["lru_wx"][0]).transpose(1, 0, 2)[:, :, None, :]], axis=2)
    lruw = np.ascontiguousarray(lruw.reshape(128, 16 * 256), np.float32)
    bucket, valid = _t5_bucket_table()
    rb = np.asarray(inp["rel_bias"], np.float32)
    biasg = np.ascontiguousarray(rb[bucket].transpose(0, 2, 1).reshape(128, NHEAD * 256))
    maskc = np.where(valid, 0.0, NEG).astype(np.float32)
    ppbase = np.zeros((128, PC_N), np.float32)
    ppbase[:, PC_BADA:PC_BADA + 96] = fm(inp["b_ada"][0])
    ppbase[:, PC_G1:PC_G1 + 16] = fm(inp["norm1_g"][0])
    ppbase[:, PC_G2:PC_G2 + 16] = fm(inp["norm2_g"][0])
    ppbase[:, PC_FG:PC_FG + 16] = fm(inp["final_g"])
    cw = np.asarray(inp["conv_w"][0], np.float32)
    for kk in range(4):
        ppbase[:, PC_CW + kk * 16: PC_CW + kk * 16 + 16] = fm(cw[kk])
    ppbase[:, PC_CB:PC_CB + 16] = fm(inp["conv_b"][0])
    ppbase[:, PC_BA:PC_BA + 16] = fm(inp["lru_ba"][0])
    ppbase[:, PC_BX:PC_BX + 16] = fm(inp["lru_bx"][0])
    ppbase[:, PC_LAM:PC_LAM + 16] = fm(inp["lru_lambda"][0])
    return dict(wblk=wblk, lruw=lruw, biasg=biasg, maskc=maskc, ppbase=ppbase,
                sinks=np.asarray(inp["attn_sinks"], np.float32).reshape(1, NHEAD),
                idn=np.eye(128, dtype=np.float32))


def core_inputs(shared, x_b, c_b):
    pp = shared["ppbase"].copy()
    pp[:, PC_C:PC_C + 16] = np.asarray(c_b, np.float32).reshape(16, 128).T
    return {"xT": np.ascontiguousarray(np.asarray(x_b, np.float32).T), "wblk": shared["wblk"], "pp": pp,
            "lruw": shared["lruw"], "biasg": shared["biasg"], "maskc": shared["maskc"], "sinks": shared["sinks"],
            "idn": shared["idn"]}


_NC_CACHE = {}


def kernel(**inputs):
    x = np.asarray(inputs["x"])
    c = np.asarray(inputs["c"])
    shared = prep_shared(inputs)
    if 8 not in _NC_CACHE:
        _NC_CACHE[8] = build(8)
    nc = _NC_CACHE[8]
    in_maps = [core_inputs(shared, x[b], c[b]) for b in range(8)]
    res = run_bass_kernel_spmd(nc, in_maps, core_ids=list(range(8)))
    out = np.stack([np.asarray(r["outT"]).T for r in res.results], axis=0)
    return np.ascontiguousarray(out.astype(np.float32))
```

```python
import contextlib
import math
import os
import numpy as np
import concourse.bass as bass
import concourse.mybir as mybir
from concourse.bass_utils import run_bass_kernel_spmd

F32 = mybir.dt.float32
BF16 = mybir.dt.bfloat16
ALU = mybir.AluOpType
ACT = mybir.ActivationFunctionType
AX = mybir.AxisListType

D = 2048
S = 4096
T = 512
NCH = 16
NHEAD = 32
EPS = 1e-6
NADA = 48
NADA1 = 16
SAME_ENG_SYNC = True
import os
LAG_D = int(os.environ.get('LAG_D', '1'))
LAG_F = int(os.environ.get('LAG_F', '2'))
NEG = -30000.0
PC_C, PC_BADA, PC_G1, PC_G2, PC_FG, PC_CW, PC_CB, PC_BA, PC_BX, PC_LAM, PC_N = 0, 16, 112, 128, 144, 160, 224, 240, 256, 272, 288
DV_SH1, DV_SC1, DV_GT1, DV_SH2, DV_SC2, DV_GT2, DV_A1, DV_A2, DV_CA, DV_CA2, DV_NBA, DV_NBX, DV_N = (
    0, 16, 32, 48, 64, 80, 96, 112, 128, 144, 160, 176, 192)
OFF_LX, OFF_LG, OFF_Q, OFF_K, OFF_V, OFF_GA, OFF_GB = 0, 2048, 4096, 6144, 6400, 6656, 8704


def block_specs():
    sp = []
    for ob in range(NADA):
        sp.append(("w_ada", 0, list(range(ob * 256, ob * 256 + 256))))
    for c in range(16):
        sp.append(("w_in", 0, list(range(OFF_LX + c * 128, OFF_LX + c * 128 + 128)) +
                   list(range(OFF_LG + c * 128, OFF_LG + c * 128 + 128))))
    for g in range(4):
        sp.append(("w_in", 0, list(range(OFF_Q + g * 512, OFF_Q + g * 512 + 256))))
        sp.append(("w_in", 0, list(range(OFF_Q + g * 512 + 256, OFF_Q + g * 512 + 512))))
        kc = list(range(OFF_K + g * 64, OFF_K + g * 64 + 64))
        sp.append(("w_in", 0, kc + kc + list(range(OFF_V + g * 64, OFF_V + g * 64 + 64)) + [-1] * 64))
    for j2 in range(8):
        sp.append(("w_in", 0, list(range(OFF_GA + j2 * 256, OFF_GA + j2 * 256 + 256))))
        sp.append(("w_lru_out", 0, list(range(j2 * 256, j2 * 256 + 256))))
        sp.append(("w_in", 0, list(range(OFF_GB + j2 * 256, OFF_GB + j2 * 256 + 256))))
        sp.append(("w_attn_out", 0, list(range(j2 * 256, j2 * 256 + 256))))
    for j2 in range(8):
        sp.append(("w_out", 0, list(range(j2 * 256, j2 * 256 + 256))))
    for qf in range(4):
        for b in range(8):
            sp.append(("w_ff1", 0, list(range(qf * 2048 + b * 256, qf * 2048 + b * 256 + 256))))
        for j2 in range(8):
            sp.append(("w_ff2", qf * 2048, list(range(j2 * 256, j2 * 256 + 256))))
    return sp


NBLK = NADA + 132


class Buf:
    __slots__ = ("name", "lw", "rd")

    def __init__(self, name):
        self.name = name
        self.lw = None
        self.rd = {}


class Stream:
    def __init__(self, name):
        self.name = name
        self.ops = []
        self.count = 0
        self.waited = {}


class KB:
    ENGS = ("pe", "act", "dve", "pool", "sp")

    def __init__(self, nc):
        self.nc = nc
        self.st = {n: Stream(n) for n in self.ENGS}
        self.dma_sems = {}
        self.sem_handles = {}

    def _need(self, s, ev, waits):
        if ev is None:
            return
        key, val = ev
        if s.waited.get(key, 0) >= val:
            return
        s.waited[key] = val
        waits.append((key, val))

    DEF_DUR = {"pe": 0.22, "act": 0.62, "dve": 0.62, "pool": 1.2, "sp": 0.1}

    def begin_region(self):
        self.rec = []

    def end_region(self):
        rec, self.rec = self.rec, None
        if not rec:
            return
        import heapq
        units = []
        cur = None
        for o in rec:
            if o["kind"] == "dma":
                assert cur is None
                units.append([o])
                continue
            if cur is not None and cur[-1]["eng"] != o["eng"]:
                raise AssertionError("unsignaled op run interrupted by another engine")
            if cur is None:
                cur = []
            cur.append(o)
            if o["signal"]:
                units.append(cur)
                cur = None
        assert cur is None
        n = len(units)
        lastw, readers = {}, {}
        preds = [set() for _ in range(n)]
        for ui, u in enumerate(units):
            for o in u:
                for b in o["reads"]:
                    w = lastw.get(id(b))
                    if w is not None and w != ui:
                        preds[ui].add(w)
                for b in o["writes"]:
                    w = lastw.get(id(b))
                    if w is not None and w != ui:
                        preds[ui].add(w)
                    for r in readers.get(id(b), ()):
                        if r != ui:
                            preds[ui].add(r)
            for o in u:
                for b in o["writes"]:
                    lastw[id(b)] = ui
                    readers[id(b)] = []
            for o in u:
                for b in o["reads"]:
                    if lastw.get(id(b)) != ui:
                        readers.setdefault(id(b), []).append(ui)
        succs = [[] for _ in range(n)]
        indeg = [len(p) for p in preds]
        for ui, p in enumerate(preds):
            for q in p:
                succs[q].append(ui)
        LAT = 0.25
        DMA_LAT = float(os.environ.get("DMA_LAT", "5.0"))
        free = {e: 0.0 for e in self.ENGS}
        fin = [0.0] * n
        rdy = [0.0] * n
        heaps = {e: [] for e in self.ENGS}
        for ui in range(n):
            if indeg[ui] == 0:
                heapq.heappush(heaps[units[ui][0]["eng"]], (0.0, ui))
        order = []
        done = 0
        while done < n:
            best = None
            for e in self.ENGS:
                h = heaps[e]
                if not h:
                    continue
                t_free = free[e]
                cand = h[0]
                st = max(t_free, cand[0])
                if best is None or (st, cand[1]) < (best[0], best[2]):
                    best = (st, e, cand[1])
            st, e, ui = best
            heapq.heappop(heaps[e])
            u = units[ui]
            dur = sum(o["dur"] for o in u)
            free[e] = st + dur
            if u[0]["kind"] == "dma":
                fin[ui] = st + dur + DMA_LAT
            else:
                fin[ui] = st + dur
            order.append((st, ui))
            done += 1
            for v in succs[ui]:
                lat = 0.0 if units[v][0]["eng"] == e and u[0]["kind"] != "dma" else LAT
                rdy[v] = max(rdy[v], fin[ui] + lat)
                indeg[v] -= 1
                if indeg[v] == 0:
                    heapq.heappush(heaps[units[v][0]["eng"]], (rdy[v], v))
        order.sort()
        self.last_region_span = order[-1][0] if order else 0.0
        for _, ui in order:
            for o in units[ui]:
                if o["kind"] == "dma":
                    self.dma(o["eng"], o["slot"], o["out"], o["in_"], reads=o["reads"], writes=o["writes"])
                else:
                    self.op(o["eng"], o["fn"], reads=o["reads"], writes=o["writes"], signal=o["signal"])

    def op(self, eng, fn, reads=(), writes=(), signal=True, dur=None):
        if getattr(self, "rec", None) is not None:
            self.rec.append(dict(kind="op", eng=eng, fn=fn, reads=list(reads), writes=list(writes), signal=signal,
                                 dur=self.DEF_DUR[eng] if dur is None else dur))
            return
        s = self.st[eng]
        waits = []
        for b in reads:
            if b.lw is not None:
                self._need(s, b.lw, waits)
        for b in writes:
            if b.lw is not None and (SAME_ENG_SYNC or b.lw[0] != eng):
                self._need(s, b.lw, waits)
            for k, v in b.rd.items():
                if SAME_ENG_SYNC or k != eng:
                    self._need(s, (k, v), waits)
        keep = []
        for (k, v) in waits:
            if k == eng and v > s.count:
                s.waited[k] = s.count
                continue
            keep.append((k, v))
        nxt = s.count + 1
        ev = (eng, nxt)
        for b in writes:
            b.lw = ev
            b.rd = {}
        for b in reads:
            if b.rd.get(eng, 0) < nxt:
                b.rd[eng] = nxt
        if signal:
            s.count = nxt
        s.ops.append((keep, fn, eng if signal else None, 1))

    def dma(self, eng, slot, out, in_, reads=(), writes=()):
        if getattr(self, "rec", None) is not None:
            self.rec.append(dict(kind="dma", eng=eng, slot=slot, out=out, in_=in_, reads=list(reads),
                                 writes=list(writes), signal=True, dur=0.15))
            return
        s = self.st[eng]
        waits = []
        for b in reads:
            self._need(s, b.lw, waits)
        for b in writes:
            self._need(s, b.lw, waits)
            for k, v in b.rd.items():
                self._need(s, (k, v), waits)
        key = "dma_" + slot
        val = self.dma_sems.get(key, 0) + 16
        self.dma_sems[key] = val
        ev = (key, val)
        for b in writes:
            b.lw = ev
            b.rd = {}
        for b in reads:
            b.rd[key] = val

        def fn(e, out=out, in_=in_):
            return e.dma_start(out=out, in_=in_)
        s.ops.append((waits, fn, key, 16))

    def final_wait(self, eng):
        s = self.st[eng]
        waits = []
        for n in self.ENGS:
            if n != eng and self.st[n].count > 0:
                self._need(s, (n, self.st[n].count), waits)
        for k, v in self.dma_sems.items():
            self._need(s, (k, v), waits)
        s.ops.append((waits, None, None, 0))

    def emit(self):
        nc = self.nc
        with contextlib.ExitStack() as es:
            keys = list(self.ENGS) + list(self.dma_sems.keys())
            for k in keys:
                self.sem_handles[k] = es.enter_context(nc.semaphore("s_" + k))
            block = es.enter_context(nc.Block())
            H = self.sem_handles

            def run(stream):
                def body(e):
                    for waits, fn, inc, amt in stream.ops:
                        for k, v in waits:
                            e.wait_ge(H[k], v)
                        if fn is None:
                            continue
                        ins = fn(e)
                        if inc is not None:
                            ins.then_inc(H[inc], amt)
                return body

            block.tensor(run(self.st["pe"]))
            block.scalar(run(self.st["act"]))
            block.vector(run(self.st["dve"]))
            block.gpsimd(run(self.st["pool"]))
            block.sync(run(self.st["sp"]))


def build(NT=8):
    nc = bass.Bass("TRN2", target_bir_lowering=False)
    xT_d = nc.dram_tensor("xT", [D, S], F32, kind="ExternalInput").ap()
    wblk_d = nc.dram_tensor("wblk", [NBLK, 128, 4096], F32, kind="ExternalInput").ap()
    pp_d = nc.dram_tensor("pp", [128, PC_N], F32, kind="ExternalInput").ap()
    lruw_d = nc.dram_tensor("lruw", [128, 16 * 256], F32, kind="ExternalInput").ap()
    biasg_d = nc.dram_tensor("biasg", [128, NHEAD * 256], F32, kind="ExternalInput").ap()
    maskc_d = nc.dram_tensor("maskc", [128, 256], F32, kind="ExternalInput").ap()
    sinks_d = nc.dram_tensor("sinks", [1, NHEAD], F32, kind="ExternalInput").ap()
    idn_d = nc.dram_tensor("idn", [128, 128], F32, kind="ExternalInput").ap()
    out_d = nc.dram_tensor("outT", [D, S], F32, kind="ExternalOutput").ap()
    wscr_d = nc.dram_tensor("wscr", [NBLK, 128, 4096], BF16).ap()

    with contextlib.ExitStack() as es:
        def sb(n, s, d):
            return es.enter_context(nc.sbuf_tensor(n, s, d))

        def ps(n, s, d):
            return es.enter_context(nc.psum_tensor(n, s, d))

        NRING = 3
        wring = sb("wring", [128, NRING, 4096], BF16)
        xT = sb("xTs", [128, 16, 512], F32)
        hT = sb("hTs", [128, 16, 512], BF16)
        big = sb("big", [128, 48, 512], BF16)
        bm = sb("bm", [128, NHEAD, 256], BF16)
        lruw = sb("lruws", [128, 16, 256], BF16)
        pp = sb("pps", [128, PC_N], F32)
        dv = sb("dvs", [128, DV_N], F32)
        cact = sb("cact", [128, 16], BF16)
        ctmp = sb("ctmp", [128, 16], F32)
        ident_b = sb("ident_b", [128, 128], BF16)
        ones_f = sb("ones_f", [128, 128], F32)
        sinks_bc = sb("sinks_bc", [128, NHEAD], F32)
        nsink = sb("nsink", [128, NHEAD], F32)
        maskc = sb("maskcs", [128, 256], F32)
        hist = sb("hist", [128, 16, 4], F32)
        hstate = sb("hstate", [128, 16], F32)
        kbuf = sb("kbuf", [128, 2, 4, 640], BF16)
        Sn = sb("Sn", [128, 2, 2, 256], F32)
        vbuf = sb("vbuf", [128, 4, 5, 64], BF16)
        sqt = sb("sqt", [128, 2, 512], F32)
        rstd = sb("rstd", [128, 512], F32)
        lxb2 = [sb("lxb", [128, 516], F32)] * 2
        xc2 = [sb("xc", [128, 512], F32)] * 2
        xcb2 = [sb("xcb", [128, 512], BF16)] * 2
        tr2 = [sb("tr", [128, 512], F32)] * 2
        ta2 = [sb("ta", [128, 512], F32)] * 2
        ti2 = [sb("ti", [128, 512], F32)] * 2
        rec = sb("rec", [128, 512], F32)
        gxs = sb("gxs", [128, 512], F32)
        gsq = sb("gsq", [128, 512], F32)
        gt = sb("gt", [128, 512], F32)
        lnv = gt
        sga = sb("sga", [128, 2, 512], F32)
        sgb = sb("sgb", [128, 2, 512], F32)
        ntmp = sgb
        ost = sqt
        rtmp = sga
        pbuf = sb("pbuf", [128, 2, 2, 256], BF16)
        pTs = sb("pTs", [128, 2, 512], BF16)
        mx = sb("mx", [128, 2, 2], F32)
        negm = sb("negm", [128, 2, 8], F32)
        rsum = sb("rsum", [128, 2, 8], F32)
        etmp = sb("etmp", [128, 2, 8], F32)
        rden = sb("rden", [128, 2, 8], F32)
        atok = sb("atok", [128, 512], BF16)

        NPM = int(os.environ.get('NPM', '4'))
        pmm = [ps("pmm%d" % i, [128, 512], F32) for i in range(NPM)]
        pS = [ps("pS%d" % i, [128, 512], F32) for i in range(2)]
        if os.environ.get('PT2', '0') == '1':
            pTb = [ps("pT%d" % i, [128, 1024], BF16) for i in range(2)]
            pTv = lambda par: pTv(par)
        else:
            pTone = ps("pTone", [128, 1024], BF16)
            pTv = lambda par: pTone[:, par * 512:(par + 1) * 512]
        pO = ps("pO", [128, 512], F32)

        def program(k, full):
            req_log = []
            b_wr = [Buf("wr%d" % i) for i in range(NRING)]
            b_scr = [Buf("scr%d" % i) for i in range(NBLK)]
            b_x = [Buf("x%d" % i) for i in range(16)]
            b_h = [Buf("h%d" % i) for i in range(16)]
            b_big = [Buf("big%d" % i) for i in range(48)]
            b_pmm = [Buf("pmm%d" % i) for i in range(NPM)]
            b_pS = [Buf("pS%d" % i) for i in range(2)]
            _bpt = Buf("pT")
            b_pT = [_bpt, _bpt]
            b_pO = Buf("pO")
            names = ["bm", "lruw", "pp", "dv", "cact", "ctmp", "ident", "ones", "sinks", "nsink", "maskc", "hist",
                     "hstate", "lnv", "rstd", "lxb", "xc", "xcb", "tr", "ta", "ti", "rec", "gxs", "gsq", "gt",
                     "negm", "rsum", "etmp", "rden", "atok"]
            B = {n: Buf(n) for n in names}
            B2 = [{n: Buf(n) for n in ("lxb", "xc", "xcb", "tr", "ta", "ti")}] * 2
            b_kb = [Buf("kb%d" % i) for i in range(4)]
            b_vb = [Buf("vb%d" % i) for i in range(4)]
            b_sqt = [Buf("sqt%d" % i) for i in range(2)]
            b_sga = [Buf("sga%d" % i) for i in range(2)]
            b_sgb = [Buf("sgb%d" % i) for i in range(2)]
            b_ntmp = b_sgb
            b_ost = b_sqt
            b_rtmp = b_sga
            b_pb = [Buf("pb%d" % i) for i in range(2)]
            b_pTs = [Buf("pTs%d" % i) for i in range(2)]
            b_mx = [Buf("mx%d" % i) for i in range(2)]
            b_Sn = [Buf("Sn%d" % i) for i in range(2)]
            b_negm = [[Buf("negm%d_%d" % (ip, i)) for i in range(4)] for ip in range(2)]
            b_rsum = [Buf("rsum%d" % i) for i in range(2)]
            b_etmp = [Buf("etmp%d" % i) for i in range(2)]
            b_rden = [Buf("rden%d" % i) for i in range(2)]

            def dvc(col, c=0, n=1):
                return dv[:, col + c: col + c + n]

            def ppc(col, c=0, n=1):
                return pp[:, col + c: col + c + n]

            state = {"seq": 0, "pm": 0}
            sched = []

            def issue_load(seq, blk, tt_):
                s = seq % NRING
                if blk < NADA:
                    mode = "c"
                elif tt_ == 0:
                    mode = "cs" if blk % 2 == 0 else "c"
                elif tt_ == 1:
                    mode = "s" if blk % 2 == 0 else "cs"
                else:
                    mode = "s"
                if mode == "cs" and NT <= tt_ + 1:
                    mode = "c"
                if mode in ("c", "cs"):
                    k.dma("pool", "wc%d" % s, wring[:, s, :], wblk_d[blk], writes=[b_wr[s]])
                    if mode == "cs":
                        k.dma("sp", "ws%d" % s, wscr_d[blk], wring[:, s, :], reads=[b_wr[s]], writes=[b_scr[blk]])
                else:
                    k.dma("sp", "w%d" % s, wring[:, s, :], wscr_d[blk], reads=[b_scr[blk]], writes=[b_wr[s]])

            PRE = NRING - 1
            issued = {"n": 0}

            def next_block(blk, first):
                seq = state["seq"]
                req_log.append((blk, first))
                if full is not None:
                    assert full[seq] == (blk, first)
                    while issued["n"] < min(len(full), seq + PRE + 1):
                        n = issued["n"]
                        issue_load(n, full[n][0], full[n][1])
                        issued["n"] += 1
                state["seq"] = seq + 1
                return seq % NRING

            held = set()

            def next_pmm(hold=False):
                while True:
                    i = state["pm"] % NPM
                    state["pm"] += 1
                    if i not in held:
                        break
                if hold:
                    held.add(i)
                return i

            def mm_group(pi, pairs, reads, ncols=512, col0=0, sig_all=False):
                n = len(pairs)
                for idx, (l, r) in enumerate(pairs):
                    last = idx == n - 1
                    k.op("pe", lambda e, l=l, r=r, idx=idx, last=last: e.matmul(
                        pmm[pi][:, col0:col0 + ncols], l, r, start=(idx == 0), stop=last),
                        reads=reads[idx] if isinstance(reads, dict) else reads,
                        writes=[b_pmm[pi]], signal=(last or sig_all))

            k.dma("sp", "c0", pp[:], pp_d, writes=[B["pp"]])
            k.dma("sp", "c1", maskc[:], maskc_d, writes=[B["maskc"]])
            k.dma("sp", "c2", sinks_bc[:], sinks_d.broadcast_to([128, NHEAD]), writes=[B["sinks"]])
            k.dma("pool", "c3", ident_b[:], idn_d, writes=[B["ident"]])
            k.dma("pool", "c4", lruw[:].rearrange("p a b -> p (a b)"), lruw_d, writes=[B["lruw"]])
            xflat = xT[:].rearrange("p a b -> p (a b)")
            k.dma("sp", "c5", xflat, biasg_d, writes=b_x)
            k.op("dve", lambda e: e.memset(ones_f[:], 1.0), writes=[B["ones"]])
            k.op("dve", lambda e: e.memset(hist[:].rearrange("p a b -> p (a b)"), 0.0), writes=[B["hist"]])
            k.op("dve", lambda e: e.memset(hstate[:], 0.0), writes=[B["hstate"]])
            k.op("dve", lambda e: e.memset(kbuf[:].rearrange("p a b c -> p (a b c)"), 0.0), writes=b_kb)
            k.op("dve", lambda e: e.memset(vbuf[:].rearrange("p a b c -> p (a b c)"), 0.0), writes=b_vb)
            for h in range(NHEAD):
                k.op("dve", lambda e, h=h: e.tensor_tensor(bm[:, h, :], xflat[:, h * 256:(h + 1) * 256], maskc[:], ALU.add),
                     reads=b_x + [B["maskc"]], writes=[B["bm"]])
            k.op("dve", lambda e: e.tensor_scalar(nsink[:], sinks_bc[:], -1.0, None, ALU.mult),
                 reads=[B["sinks"]], writes=[B["nsink"]])
            k.op("act", lambda e: e.activation(ctmp[:], ppc(PC_C, 0, 16), ACT.Exp, scale=-1.0),
                 reads=[B["pp"]], writes=[B["ctmp"]])
            k.op("dve", lambda e: e.tensor_scalar(ctmp[:], ctmp[:], 1.0, None, ALU.add), reads=[B["ctmp"]], writes=[B["ctmp"]])
            k.op("dve", lambda e: e.reciprocal(ctmp[:], ctmp[:]), reads=[B["ctmp"]], writes=[B["ctmp"]])
            k.op("dve", lambda e: e.tensor_tensor(cact[:], ctmp[:], ppc(PC_C, 0, 16), ALU.mult),
                 reads=[B["ctmp"], B["pp"]], writes=[B["cact"]])
            def ada_blocks(ob0, ob1):
                for ob in range(ob0, ob1):
                    s = next_block(ob, -1)
                    for i in range(2):
                        oc = ob * 2 + i
                        for kc in range(16):
                            k.op("pe", lambda e, s=s, i=i, kc=kc, oc=oc: e.matmul(
                                pO[:, oc:oc + 1], wring[:, s, kc * 256 + i * 128: kc * 256 + i * 128 + 128],
                                cact[:, kc:kc + 1], start=(kc == 0), stop=(kc == 15)),
                                reads=[b_wr[s], B["cact"]], writes=[b_pO], signal=(kc == 15))
                c0, c1 = ob0 * 2, ob1 * 2
                k.op("dve", lambda e: e.tensor_tensor(dv[:, c0:c1], pO[:, c0:c1], ppc(PC_BADA, c0, c1 - c0), ALU.add),
                     reads=[b_pO, B["pp"]], writes=[B["dv"]])

            ada_blocks(0, NADA1)
            k.op("dve", lambda e: e.scalar_tensor_tensor(dvc(DV_A1, 0, 16), dvc(DV_SC1, 0, 16), 1.0, ppc(PC_G1, 0, 16),
                                                         ALU.add, ALU.mult), reads=[B["dv"], B["pp"]], writes=[B["dv"]])
            k.op("act", lambda e: e.activation(ctmp[:], ppc(PC_LAM, 0, 16), ACT.Exp, scale=-1.0),
                 reads=[B["pp"], B["cact"]], writes=[B["ctmp"]])
            k.op("act", lambda e: e.activation(ctmp[:], ctmp[:], ACT.Ln, bias=1.0), reads=[B["ctmp"]], writes=[B["ctmp"]])
            k.op("dve", lambda e: e.tensor_scalar(dvc(DV_CA, 0, 16), ctmp[:], -8.0, None, ALU.mult),
                 reads=[B["ctmp"]], writes=[B["dv"]])
            k.op("dve", lambda e: e.tensor_scalar(dvc(DV_CA2, 0, 16), ctmp[:], -16.0, None, ALU.mult),
                 reads=[B["ctmp"]], writes=[B["dv"]])
            k.op("dve", lambda e: e.tensor_scalar(dvc(DV_NBA, 0, 16), ppc(PC_BA, 0, 16), -1.0, None, ALU.mult),
                 reads=[B["pp"]], writes=[B["dv"]])
            k.op("dve", lambda e: e.tensor_scalar(dvc(DV_NBX, 0, 16), ppc(PC_BX, 0, 16), -1.0, None, ALU.mult),
                 reads=[B["pp"]], writes=[B["dv"]])

            def rmsnorm_stats():
                pi = next_pmm()
                for c in range(16):
                    q = c % 2
                    k.op("act", lambda e, c=c, q=q: e.activation(sqt[:, q, :], xT[:, c, :], ACT.Square),
                         reads=[b_x[c]], writes=[b_sqt[q]])
                    k.op("pe", lambda e, c=c, q=q, pi=pi: e.matmul(pmm[pi][:], ones_f[:], sqt[:, q, :],
                                                                  start=(c == 0), stop=(c == 15)),
                         reads=[b_sqt[q], B["ones"]], writes=[b_pmm[pi]], signal=True)
                k.op("act", lambda e, pi=pi: e.activation(lnv[:], pmm[pi][:], ACT.Ln, bias=EPS, scale=1.0 / D),
                     reads=[b_pmm[pi]], writes=[B["gt"]])
                k.op("act", lambda e: e.activation(rstd[:], lnv[:], ACT.Exp, scale=-0.5),
                     reads=[B["gt"]], writes=[B["rstd"]])

            def modulate(acol, scol):
                for c in range(16):
                    q = c % 2
                    k.op("dve", lambda e, c=c, q=q: e.tensor_tensor(ntmp[:, q, :], xT[:, c, :], rstd[:], ALU.mult),
                         reads=[b_x[c], B["rstd"]], writes=[b_ntmp[q]])
                    k.op("act", lambda e, c=c, q=q: e.activation(hT[:, c, :], ntmp[:, q, :], ACT.Identity,
                                                                 bias=dvc(scol, c), scale=dvc(acol, c)),
                         reads=[b_ntmp[q], B["dv"]], writes=[b_h[c]])

            def proj_pairs(s, i, src, srcbufs=None):
                return [(wring[:, s, kc * 256 + i * 128: kc * 256 + i * 128 + 128], src(kc)) for kc in range(16)]

            C0 = math.sqrt(2.0 / math.pi)
            C1 = 0.044715

            k.begin_region()
            for tt in range(NT):
                t0 = tt * T
                for c in range(16):
                    k.dma("sp", "x%d" % c, xT[:, c, :], xT_d[c * 128:(c + 1) * 128, t0:t0 + T], writes=[b_x[c]])
                rmsnorm_stats()
                modulate(DV_A1, DV_SH1)

                def lru_chunk(c):
                    cp = c % 2
                    lxb, xc, xcb, tr, ta, ti = lxb2[cp], xc2[cp], xcb2[cp], tr2[cp], ta2[cp], ti2[cp]
                    BL = B2[cp]
                    s = next_block(NADA + c, tt)
                    p_lx = next_pmm(hold=True)
                    mm_group(p_lx, proj_pairs(s, 0, lambda kc: hT[:, kc, :]), reads={kc: [b_wr[s], b_h[kc]] for kc in range(16)})
                    p_lg = next_pmm(hold=True)
                    mm_group(p_lg, proj_pairs(s, 1, lambda kc: hT[:, kc, :]), reads={kc: [b_wr[s], b_h[kc]] for kc in range(16)})
                    yield
                    k.op("dve", lambda e, c=c: e.tensor_copy(lxb[:, 0:3], hist[:, c, 0:3]),
                         reads=[B["hist"]], writes=[BL["lxb"]], dur=0.12)
                    k.op("act", lambda e, p=p_lx: e.activation(lxb[:, 3:515], pmm[p][:], ACT.Copy),
                         reads=[b_pmm[p_lx]], writes=[BL["lxb"]])
                    k.op("act", lambda e, p=p_lg: e.activation(gxs[:], pmm[p][:], ACT.Copy), reads=[b_pmm[p_lg]], writes=[B["gxs"]])
                    k.op("act", lambda e, p=p_lg: e.activation(gsq[:], pmm[p][:], ACT.Square), reads=[b_pmm[p_lg]], writes=[B["gsq"]])
                    held.discard(p_lx)
                    held.discard(p_lg)
                    yield
                    k.op("dve", lambda e, c=c: e.tensor_copy(hist[:, c, 0:3], lxb[:, 512:515]),
                         reads=[BL["lxb"]], writes=[B["hist"]], dur=0.12)
                    k.op("dve", lambda e, c=c: e.tensor_scalar(xc[:], lxb[:, 3:515], ppc(PC_CW, 3 * 16 + c), ppc(PC_CB, c),
                                                              ALU.mult, ALU.add),
                         reads=[BL["lxb"], B["pp"]], writes=[BL["xc"]])
                    k.op("dve", lambda e: e.tensor_scalar(gt[:], gsq[:], C1, 1.0, ALU.mult, ALU.add), reads=[B["gsq"]], writes=[B["gt"]])
                    k.op("dve", lambda e, c=c: e.scalar_tensor_tensor(
                        xc[:], lxb[:, 2:514], ppc(PC_CW, 2 * 16 + c), xc[:], ALU.mult, ALU.add),
                        reads=[BL["lxb"], B["pp"], BL["xc"]], writes=[BL["xc"]])
                    yield
                    k.op("dve", lambda e: e.tensor_tensor(gt[:], gt[:], gxs[:], ALU.mult), reads=[B["gt"], B["gxs"]], writes=[B["gt"]])
                    for kk in (1, 0):
                        k.op("dve", lambda e, c=c, kk=kk: e.scalar_tensor_tensor(
                            xc[:], lxb[:, kk:kk + 512], ppc(PC_CW, kk * 16 + c), xc[:], ALU.mult, ALU.add),
                            reads=[BL["lxb"], B["pp"], BL["xc"]], writes=[BL["xc"]])
                    yield
                    k.op("act", lambda e: e.activation(gt[:], gt[:], ACT.Exp, scale=-2.0 * C0), reads=[B["gt"]], writes=[B["gt"]])
                    k.op("act", lambda e: e.activation(xcb[:], xc[:], ACT.Copy), reads=[BL["xc"]], writes=[BL["xcb"]])
                    k.op("act", lambda e: e.activation(gt[:], gt[:], ACT.Ln, bias=1.0), reads=[B["gt"]], writes=[B["gt"]])
                    yield
                    p_r = next_pmm(hold=True)
                    k.op("pe", lambda e, c=c, p=p_r: e.matmul(pmm[p][:], lruw[:, c, 0:128], xcb[:], start=True, stop=True),
                         reads=[B["lruw"], BL["xcb"]], writes=[b_pmm[p_r]])
                    p_i = next_pmm(hold=True)
                    k.op("pe", lambda e, c=c, p=p_i: e.matmul(pmm[p][:], lruw[:, c, 128:256], xcb[:], start=True, stop=True),
                         reads=[B["lruw"], BL["xcb"]], writes=[b_pmm[p_i]])
                    yield
                    k.op("act", lambda e, c=c, p=p_r: e.activation(tr[:], pmm[p][:], ACT.Exp, bias=dvc(DV_NBA, c), scale=-1.0),
                         reads=[b_pmm[p_r], B["dv"]], writes=[BL["tr"]])
                    k.op("act", lambda e, c=c, p=p_i: e.activation(ti[:], pmm[p][:], ACT.Exp, bias=dvc(DV_NBX, c), scale=-1.0),
                         reads=[b_pmm[p_i], B["dv"]], writes=[BL["ti"]])
                    held.discard(p_r)
                    held.discard(p_i)
                    k.op("act", lambda e: e.activation(gt[:], gt[:], ACT.Exp, scale=-1.0), reads=[B["gt"]], writes=[B["gt"]])
                    yield
                    k.op("act", lambda e: e.activation(tr[:], tr[:], ACT.Ln, bias=1.0), reads=[BL["tr"]], writes=[BL["tr"]])
                    k.op("act", lambda e: e.activation(ti[:], ti[:], ACT.Ln, bias=1.0), reads=[BL["ti"]], writes=[BL["ti"]])
                    k.op("dve", lambda e: e.tensor_tensor(gt[:], gt[:], gxs[:], ALU.mult), reads=[B["gt"], B["gxs"]], writes=[B["gt"]])
                    yield
                    k.op("act", lambda e: e.activation(tr[:], tr[:], ACT.Exp, scale=-1.0), reads=[BL["tr"]], writes=[BL["tr"]])
                    k.op("act", lambda e: e.activation(ti[:], ti[:], ACT.Exp, scale=-1.0), reads=[BL["ti"]], writes=[BL["ti"]])
                    yield
                    k.op("act", lambda e, c=c: e.activation(ta[:], tr[:], ACT.Exp, scale=dvc(DV_CA, c)),
                         reads=[BL["tr"], B["dv"]], writes=[BL["ta"]])
                    k.op("act", lambda e, c=c: e.activation(tr[:], tr[:], ACT.Exp, scale=dvc(DV_CA2, c)),
                         reads=[BL["tr"], B["dv"]], writes=[BL["tr"]])
                    k.op("dve", lambda e: e.tensor_tensor(ti[:], ti[:], xc[:], ALU.mult), reads=[BL["ti"], BL["xc"]], writes=[BL["ti"]])
                    yield
                    k.op("act", lambda e: e.activation(tr[:], tr[:], ACT.Ln, bias=1.0000002, scale=-1.0),
                         reads=[BL["tr"]], writes=[BL["tr"]])
                    yield
                    k.op("act", lambda e: e.activation(tr[:], tr[:], ACT.Exp, scale=0.5), reads=[BL["tr"]], writes=[BL["tr"]])
                    yield
                    if tt == 0:
                        k.op("dve", lambda e: e.memset(tr[:, 0:1], 1.0), reads=[BL["tr"]], writes=[BL["tr"]])
                    k.op("dve", lambda e: e.tensor_tensor(ti[:], ti[:], tr[:], ALU.mult), reads=[BL["ti"], BL["tr"]], writes=[BL["ti"]])
                    k.op("dve", lambda e, c=c: e.tensor_tensor_scan(rec[:], ta[:], ti[:], hstate[:, c:c + 1], ALU.mult, ALU.add),
                         reads=[BL["ta"], BL["ti"], B["hstate"]], writes=[B["rec"]], dur=1.15)
                    yield
                    k.op("dve", lambda e, c=c: e.tensor_copy(hstate[:, c:c + 1], rec[:, 511:512]),
                         reads=[B["rec"]], writes=[B["hstate"]], dur=0.12)
                    k.op("dve", lambda e, c=c: e.tensor_tensor(big[:, c, :], gt[:], rec[:], ALU.mult),
                         reads=[B["gt"], B["rec"]], writes=[b_big[c]])
                    yield

                def attn_proj(g):
                    for qb in range(2):
                        s = next_block(NADA + 16 + 3 * g + qb, tt)
                        for i in range(2):
                            ci = 32 + 4 * g + 2 * qb + i
                            p = next_pmm()
                            mm_group(p, proj_pairs(s, i, lambda kc: hT[:, kc, :]), reads={kc: [b_wr[s], b_h[kc]] for kc in range(16)})
                            k.op("act", lambda e, p=p, ci=ci: e.activation(big[:, ci, :], pmm[p][:], ACT.Copy, scale=0.125),
                                 reads=[b_pmm[p]], writes=[b_big[ci]])
                        yield
                    s = next_block(NADA + 16 + 3 * g + 2, tt)
                    p = next_pmm()
                    mm_group(p, proj_pairs(s, 0, lambda kc: hT[:, kc, :]), reads={kc: [b_wr[s], b_h[kc]] for kc in range(16)})
                    k.op("dve", lambda e, p=p, g=g: e.tensor_copy(kbuf[0:64, 0, g, 128:640], pmm[p][0:64, :]),
                         reads=[b_pmm[p]], writes=[b_kb[g]])
                    k.op("act", lambda e, p=p, g=g: e.activation(kbuf[64:128, 1, g, 128:640], pmm[p][64:128, :], ACT.Copy),
                         reads=[b_pmm[p]], writes=[b_kb[g]])
                    p = next_pmm()
                    for tb in range(4):
                        for kc in range(16):
                            k.op("pe", lambda e, p=p, s=s, tb=tb, kc=kc: e.matmul(
                                pmm[p][:, tb * 64:(tb + 1) * 64], hT[:, kc, tb * 128:(tb + 1) * 128],
                                wring[:, s, kc * 256 + 128: kc * 256 + 192], start=(kc == 0), stop=(kc == 15)),
                                reads=[b_wr[s]] + b_h, writes=[b_pmm[p]], signal=(kc == 15), dur=0.05)
                    k.op("act", lambda e, p=p, g=g: e.activation(
                        vbuf[:, g, 1:5, :].rearrange("p a b -> p (a b)"), pmm[p][:, 0:256], ACT.Copy),
                        reads=[b_pmm[p]], writes=[b_vb[g]])
                    yield

                def attn_hist(g):
                    if tt < NT - 1:
                        k.op("dve", lambda e, g=g: e.tensor_copy(kbuf[0:64, 0, g, 0:128], kbuf[0:64, 0, g, 512:640]),
                             reads=[b_kb[g]], writes=[b_kb[g]])
                        k.op("dve", lambda e, g=g: e.tensor_copy(kbuf[64:128, 1, g, 0:128], kbuf[64:128, 1, g, 512:640]),
                             reads=[b_kb[g]], writes=[b_kb[g]])
                        k.op("dve", lambda e, g=g: e.tensor_copy(vbuf[:, g, 0, :], vbuf[:, g, 4, :]),
                             reads=[b_vb[g]], writes=[b_vb[g]])

                rounds = [(g, n, rnd) for g in range(4) for n in range(4) for rnd in range(4)]

                def rinfo(i):
                    g, n, rnd = rounds[i]
                    first_blk = (tt == 0 and n == 0)
                    return dict(g=g, n=n, rnd=rnd, par=i % 2, ip=(i // 4) % 2, hp0=2 * rnd, first=first_blk,
                                kw=128 if first_blk else 256, kcol0=n * 128 + (128 if first_blk else 0),
                                bcol0=128 if first_blk else 0, nkb=1 if first_blk else 2)

                def st_A(i):
                    r = rinfo(i)
                    while pj_state["n"] < 3 * (r["g"] + 1):
                        next(pj)
                        pj_state["n"] += 1
                    g, n, par, kw, kcol0 = r["g"], r["n"], r["par"], r["kw"], r["kcol0"]
                    for hh in range(2):
                        j = r["hp0"] + hh
                        ci = 32 + 4 * g + j // 2
                        k.op("pe", lambda e, par=par, hh=hh, ci=ci, j=j, n=n, g=g, kcol0=kcol0, kw=kw: e.matmul(
                            pS[par][:, hh * 256: hh * 256 + kw], big[:, ci, n * 128:(n + 1) * 128],
                            kbuf[:, j % 2, g, kcol0:kcol0 + kw], start=True, stop=True),
                            reads=[b_big[ci], b_kb[g]], writes=[b_pS[par]], signal=(hh == 1), dur=0.17)

                def st_B(i):
                    r = rinfo(i)
                    g, par, kw, hp0, ip, rnd, bcol0 = r["g"], r["par"], r["kw"], r["hp0"], r["ip"], r["rnd"], r["bcol0"]
                    h0 = 8 * g + hp0
                    k.op("dve", lambda e, par=par, kw=kw, h0=h0, bcol0=bcol0: e.tensor_tensor(
                        Sn[:, par, :, 0:kw], pS[par][:].rearrange("p (a b) -> p a b", a=2)[:, :, 0:kw],
                        bm[:, h0:h0 + 2, bcol0:bcol0 + kw], ALU.add),
                        reads=[b_pS[par], B["bm"]], writes=[b_Sn[par]])
                    k.op("dve", lambda e, par=par, kw=kw: e.tensor_reduce(
                        mx[:, par, :], Sn[:, par, :, 0:kw], AX.X, ALU.max),
                        reads=[b_Sn[par]], writes=[b_mx[par]])
                    k.op("dve", lambda e, par=par, hp0=hp0, g=g, ip=ip: e.scalar_tensor_tensor(
                        negm[:, ip, hp0:hp0 + 2], mx[:, par, :], -1.0, nsink[:, 8 * g + hp0: 8 * g + hp0 + 2],
                        ALU.mult, ALU.min),
                        reads=[b_mx[par], B["nsink"]], writes=[b_negm[ip][rnd]], dur=0.12)

                def st_C(i):
                    r = rinfo(i)
                    g, par, kw, hp0, ip, rnd = r["g"], r["par"], r["kw"], r["hp0"], r["ip"], r["rnd"]
                    for hh in range(2):
                        j = hp0 + hh
                        k.op("act", lambda e, par=par, hh=hh, j=j, kw=kw, ip=ip: e.activation(
                            pbuf[:, par, hh, 0:kw], Sn[:, par, hh, 0:kw], ACT.Exp,
                            bias=negm[:, ip, j:j + 1], scale=1.0, accum_out=rsum[:, ip, j:j + 1]),
                            reads=[b_Sn[par], b_negm[ip][rnd]], writes=[b_pb[par], b_rsum[ip]], dur=0.5)
                    if rnd == 3:
                        k.op("dve", lambda e, g=g, ip=ip: e.tensor_tensor(
                            etmp[:, ip, :], sinks_bc[:, 8 * g:8 * g + 8], negm[:, ip, :], ALU.add),
                            reads=[B["sinks"]] + b_negm[ip], writes=[b_etmp[ip]], dur=0.12)
                        k.op("act", lambda e, ip=ip: e.activation(etmp[:, ip, :], etmp[:, ip, :], ACT.Exp),
                             reads=[b_etmp[ip]], writes=[b_etmp[ip]], dur=0.12)
                        k.op("dve", lambda e, ip=ip: e.tensor_tensor(rden[:, ip, :], rsum[:, ip, :], etmp[:, ip, :], ALU.add),
                             reads=[b_rsum[ip], b_etmp[ip]], writes=[b_rden[ip]], dur=0.12)
                        k.op("dve", lambda e, ip=ip: e.reciprocal(rden[:, ip, :], rden[:, ip, :]),
                             reads=[b_rden[ip]], writes=[b_rden[ip]], dur=0.12)

                def st_D(i):
                    r = rinfo(i)
                    par, nkb = r["par"], r["nkb"]
                    for hh in range(2):
                        for kb in range(nkb):
                            col = (hh * 2 + kb) * 128
                            k.op("pe", lambda e, par=par, hh=hh, kb=kb, col=col: e.transpose(
                                pTv(par)[:, col: col + 128],
                                pbuf[:, par, hh, kb * 128:(kb + 1) * 128], ident_b[:]),
                                reads=[b_pb[par], B["ident"]], writes=[b_pT[par]],
                                signal=(hh == 1 and kb == nkb - 1), dur=0.12)

                def st_E(i):
                    par = rinfo(i)["par"]
                    if rinfo(i)["nkb"] == 1:
                        k.op("dve", lambda e, par=par: e.tensor_copy(
                            pTs[:, par, :].rearrange("p (h k q) -> p h k q", h=2, k=2)[:, :, 0, :],
                            pTv(par).rearrange("p (h k q) -> p h k q", h=2, k=2)[:, :, 0, :]),
                            reads=[b_pT[par]], writes=[b_pTs[par]])
                    else:
                        k.op("dve", lambda e, par=par: e.tensor_copy(pTs[:, par, :], pTv(par)),
                             reads=[b_pT[par]], writes=[b_pTs[par]])

                def st_F(i):
                    r = rinfo(i)
                    g, n, par, nkb, hp0, first_blk = r["g"], r["n"], r["par"], r["nkb"], r["hp0"], r["first"]
                    for hh in range(2):
                        j = hp0 + hh
                        for kb in range(nkb):
                            col = (hh * 2 + kb) * 128
                            vblk = n + kb + (1 if first_blk else 0)
                            k.op("pe", lambda e, par=par, j=j, kb=kb, col=col, vblk=vblk, g=g, nkb=nkb: e.matmul(
                                pO[:, j * 64:(j + 1) * 64], pTs[:, par, col:col + 128], vbuf[:, g, vblk, :],
                                start=(kb == 0), stop=(kb == nkb - 1)),
                                reads=[b_pTs[par], b_vb[g]], writes=[b_pO],
                                signal=(kb == nkb - 1), dur=0.05)

                def st_G(i):
                    r = rinfo(i)
                    g, n, ip, par = r["g"], r["n"], r["ip"], r["par"]
                    pT2 = pTv(par)
                    b_pT2 = b_pT[par]
                    k.op("dve", lambda e, ip=ip: e.tensor_tensor(
                        atok[:].rearrange("p (a b) -> p a b", a=8), pO[:].rearrange("p (a b) -> p a b", a=8),
                        rden[:, ip, :].unsqueeze(2).broadcast_to([128, 8, 64]), ALU.mult),
                        reads=[b_pO, b_rden[ip]], writes=[B["atok"]])
                    yield
                    for q4 in range(4):
                        k.op("pe", lambda e, q4=q4: e.transpose(
                            pT2[:, q4 * 128:(q4 + 1) * 128], atok[:, q4 * 128:(q4 + 1) * 128], ident_b[:]),
                            reads=[B["atok"], B["ident"]], writes=[b_pT2], signal=(q4 == 3), dur=0.12)
                    yield
                    k.op("act", lambda e, g=g, n=n: e.activation(
                        big[:, 16 + 4 * g:16 + 4 * g + 4, n * 128:(n + 1) * 128],
                        pT2.rearrange("p (a b) -> p a b", a=4), ACT.Copy),
                        reads=[b_pT2], writes=[b_big[16 + 4 * g + q4] for q4 in range(4)])
                    if n == 3:
                        attn_hist(g)
                    yield

                def attn_stream():
                    NR = len(rounds)
                    for i in range(NR + LAG_F):
                        if i < NR:
                            st_A(i)
                            yield
                            st_B(i)
                            yield
                            st_C(i)
                            yield
                        if 0 <= i - LAG_D < NR:
                            st_D(i - LAG_D)
                            yield
                            st_E(i - LAG_D)
                            yield
                        if 0 <= i - LAG_F < NR:
                            st_F(i - LAG_F)
                            yield
                            if rounds[i - LAG_F][2] == 3:
                                yield from st_G(i - LAG_F)

                def proj_stream():
                    for g in range(4):
                        yield from attn_proj(g)

                def lru_stream():
                    for c in range(16):
                        yield from lru_chunk(c)

                def drain(gen):
                    for _ in gen:
                        pass

                pj = proj_stream()
                pj_state = {"n": 0}
                gens = [lru_stream(), attn_stream()]
                live = [True, True]
                step = 0
                pj_live = True
                reps = [int(v) for v in os.environ.get('ILV', '1:1').split(':')]
                if reps[1] == 0:
                    st_ = 0
                    for _ in gens[0]:
                        st_ += 1
                        if st_ % 12 == 0:
                            for _ in range(1):
                                try:
                                    next(pj)
                                    pj_state["n"] += 1
                                except StopIteration:
                                    pass
                    live[0] = False
                    reps = [1, 1]
                while any(live):
                    for gi, gg in enumerate(gens):
                        for _ in range(reps[gi]):
                            if live[gi]:
                                try:
                                    next(gg)
                                except StopIteration:
                                    live[gi] = False
                    step += 1
                    if pj_live and step % 12 == 0:
                        try:
                            next(pj)
                            pj_state["n"] += 1
                        except StopIteration:
                            pj_live = False
                drain(pj)

                if tt == 0:
                    ada_blocks(NADA1, NADA)
                    k.op("dve", lambda e: e.scalar_tensor_tensor(dvc(DV_A2, 0, 16), dvc(DV_SC2, 0, 16), 1.0, ppc(PC_G2, 0, 16),
                                                                 ALU.add, ALU.mult), reads=[B["dv"], B["pp"]], writes=[B["dv"]])

                for j2 in range(8):
                    s = next_block(NADA + 28 + 4 * j2 + 0, tt)
                    for i in range(2):
                        p = next_pmm()
                        mm_group(p, proj_pairs(s, i, lambda kc: hT[:, kc, :]), reads={kc: [b_wr[s], b_h[kc]] for kc in range(16)})
                        k.op("act", lambda e, p=p, i=i: e.activation(sga[:, i, :], pmm[p][:], ACT.Exp, scale=-1.0),
                             reads=[b_pmm[p]], writes=[b_sga[i]])
                        k.op("act", lambda e, i=i: e.activation(sga[:, i, :], sga[:, i, :], ACT.Ln, bias=1.0),
                             reads=[b_sga[i]], writes=[b_sga[i]])
                        k.op("act", lambda e, i=i: e.activation(sga[:, i, :], sga[:, i, :], ACT.Exp, scale=-1.0),
                             reads=[b_sga[i]], writes=[b_sga[i]])
                    s = next_block(NADA + 28 + 4 * j2 + 1, tt)
                    for i in range(2):
                        p = next_pmm()
                        mm_group(p, proj_pairs(s, i, lambda kc: big[:, kc, :]), reads={kc: [b_wr[s], b_big[kc]] for kc in range(16)})
                        k.op("dve", lambda e, p=p, i=i: e.tensor_tensor(sga[:, i, :], sga[:, i, :], pmm[p][:], ALU.mult),
                             reads=[b_sga[i], b_pmm[p]], writes=[b_sga[i]])
                    s = next_block(NADA + 28 + 4 * j2 + 2, tt)
                    for i in range(2):
                        p = next_pmm()
                        mm_group(p, proj_pairs(s, i, lambda kc: hT[:, kc, :]), reads={kc: [b_wr[s], b_h[kc]] for kc in range(16)})
                        k.op("act", lambda e, p=p, i=i: e.activation(sgb[:, i, :], pmm[p][:], ACT.Exp, scale=-1.0),
                             reads=[b_pmm[p]], writes=[b_sgb[i]])
                        k.op("act", lambda e, i=i: e.activation(sgb[:, i, :], sgb[:, i, :], ACT.Ln, bias=1.0),
                             reads=[b_sgb[i]], writes=[b_sgb[i]])
                        k.op("act", lambda e, i=i: e.activation(sgb[:, i, :], sgb[:, i, :], ACT.Exp, scale=-1.0),
                             reads=[b_sgb[i]], writes=[b_sgb[i]])
                    s = next_block(NADA + 28 + 4 * j2 + 3, tt)
                    for i in range(2):
                        j = 2 * j2 + i
                        p = next_pmm()
                        mm_group(p, proj_pairs(s, i, lambda kc: big[:, 16 + kc, :]), reads={kc: [b_wr[s], b_big[16 + kc]] for kc in range(16)})
                        k.op("dve", lambda e, p=p, i=i: e.tensor_tensor(sgb[:, i, :], sgb[:, i, :], pmm[p][:], ALU.mult),
                             reads=[b_sgb[i], b_pmm[p]], writes=[b_sgb[i]])
                        k.op("dve", lambda e, i=i, j=j: e.tensor_tensor(big[:, 32 + j, :], sga[:, i, :], sgb[:, i, :], ALU.add),
                             reads=[b_sga[i], b_sgb[i]], writes=[b_big[32 + j]])

                for j2 in range(8):
                    s = next_block(NADA + 60 + j2, tt)
                    for i in range(2):
                        j = 2 * j2 + i
                        p = next_pmm()
                        mm_group(p, proj_pairs(s, i, lambda kc: big[:, 32 + kc, :]), reads={kc: [b_wr[s], b_big[32 + kc]] for kc in range(16)})
                        k.op("dve", lambda e, p=p, j=j: e.scalar_tensor_tensor(
                            xT[:, j, :], pmm[p][:], dvc(DV_GT1, j), xT[:, j, :], ALU.mult, ALU.add),
                            reads=[b_pmm[p], B["dv"], b_x[j]], writes=[b_x[j]])

                rmsnorm_stats()
                modulate(DV_A2, DV_SH2)

                for qf in range(4):
                    slot = (qf % 3) * 16
                    for b8 in range(8):
                        s = next_block(NADA + 68 + qf * 16 + b8, tt)
                        for i in range(2):
                            fi = slot + 2 * b8 + i
                            p = next_pmm()
                            mm_group(p, proj_pairs(s, i, lambda kc: hT[:, kc, :]), reads={kc: [b_wr[s], b_h[kc]] for kc in range(16)})
                            k.op("act", lambda e, p=p, i=i: e.activation(rtmp[:, i, :], pmm[p][:], ACT.Relu),
                                 reads=[b_pmm[p]], writes=[b_rtmp[i]])
                            k.op("dve", lambda e, i=i, fi=fi: e.tensor_tensor(big[:, fi, :], rtmp[:, i, :], rtmp[:, i, :], ALU.mult),
                                 reads=[b_rtmp[i]], writes=[b_big[fi]])
                    for j2 in range(8):
                        s = next_block(NADA + 68 + qf * 16 + 8 + j2, tt)
                        for i in range(2):
                            j = 2 * j2 + i
                            p = next_pmm()
                            mm_group(p, proj_pairs(s, i, lambda kc, slot=slot: big[:, slot + kc, :]),
                                     reads={kc: [b_wr[s], b_big[slot + kc]] for kc in range(16)})
                            k.op("dve", lambda e, p=p, j=j: e.scalar_tensor_tensor(
                                xT[:, j, :], pmm[p][:], dvc(DV_GT2, j), xT[:, j, :], ALU.mult, ALU.add),
                                reads=[b_pmm[p], B["dv"], b_x[j]], writes=[b_x[j]])

                rmsnorm_stats()
                for c in range(16):
                    q = c % 2
                    k.op("dve", lambda e, c=c, q=q: e.scalar_tensor_tensor(
                        ost[:, q, :], xT[:, c, :], ppc(PC_FG, c), rstd[:], ALU.mult, ALU.mult),
                        reads=[b_x[c], B["pp"], B["rstd"]], writes=[b_ost[q]])
                    k.dma("sp", "o%d" % q, out_d[c * 128:(c + 1) * 128, t0:t0 + T], ost[:, q, :], reads=[b_ost[q]])

            k.end_region()
            return req_log

        k1 = KB(nc)
        order = program(k1, None)
        k = KB(nc)
        order2 = program(k, order)
        assert order2 == order
        k.final_wait("sp")
        k.emit()
    return nc


def _t5_bucket_table():
    qi = np.arange(128)[:, None]
    ki = np.arange(256)[None, :]
    rel = qi + 128 - ki
    relc = np.maximum(rel, 0)
    max_exact = 16
    relf = np.maximum(relc, 1).astype(np.float32)
    large = max_exact + (np.log(relf / np.float32(max_exact)) / np.float32(math.log(128 / max_exact))
                         * np.float32(32 - max_exact)).astype(np.int32)
    large = np.minimum(large, 31)
    bucket = np.where(relc < max_exact, relc, large)
    valid = (rel >= 0) & (rel < 128)
    return bucket, valid


def prep_shared(inp):
    W = {
        "w_ada": np.asarray(inp["w_ada"][0]), "w_in": np.asarray(inp["w_in"][0]),
        "w_lru_out": np.asarray(inp["w_lru_out"][0]), "w_attn_out": np.asarray(inp["w_attn_out"][0]),
        "w_out": np.asarray(inp["w_out"][0]), "w_ff1": np.asarray(inp["w_ff1"][0]), "w_ff2": np.asarray(inp["w_ff2"][0]),
    }
    specs = block_specs()
    wblk = np.zeros((NBLK, 128, 4096), np.float32)
    for bi, (name, r0, cols) in enumerate(specs):
        cols = np.asarray(cols)
        sub = np.zeros((2048, 256), np.float32)
        ok = cols >= 0
        sub[:, ok] = W[name][r0:r0 + 2048][:, cols[ok]]
        wblk[bi] = sub.reshape(16, 128, 256).transpose(1, 0, 2).reshape(128, 4096)

    def fm(v):
        return np.asarray(v, np.float32).reshape(-1, 128).T

    lruw = np.concatenate([np.asarray(inp["lru_wa"][0]).transpose(1, 0, 2)[:, :, None, :],
                           np.asarray(inp["lru_wx"][0]).transpose(1, 0, 2)[:, :, None, :]], axis=2)
    lruw = np.ascontiguousarray(lruw.reshape(128, 16 * 256), np.float32)
    bucket, valid = _t5_bucket_table()
    rb = np.asarray(inp["rel_bias"], np.float32)
    biasg = np.ascontiguousarray(rb[bucket].transpose(0, 2, 1).reshape(128, NHEAD * 256))
    maskc = np.where(valid, 0.0, NEG).astype(np.float32)
    ppbase = np.zeros((128, PC_N), np.float32)
    ppbase[:, PC_BADA:PC_BADA + 96] = fm(inp["b_ada"][0])
    ppbase[:, PC_G1:PC_G1 + 16] = fm(inp["norm1_g"][0])
    ppbase[:, PC_G2:PC_G2 + 16] = fm(inp["norm2_g"][0])
    ppbase[:, PC_FG:PC_FG + 16] = fm(inp["final_g"])
    cw = np.asarray(inp["conv_w"][0], np.float32)
    for kk in range(4):
        ppbase[:, PC_CW + kk * 16: PC_CW + kk * 16 + 16] = fm(cw[kk])
    ppbase[:, PC_CB:PC_CB + 16] = fm(inp["conv_b"][0])
    ppbase[:, PC_BA:PC_BA + 16] = fm(inp["lru_ba"][0])
    ppbase[:, PC_BX:PC_BX + 16] = fm(inp["lru_bx"][0])
    ppbase[:, PC_LAM:PC_LAM + 16] = fm(inp["lru_lambda"][0])
    return dict(wblk=wblk, lruw=lruw, biasg=biasg, maskc=maskc, ppbase=ppbase,
                sinks=np.asarray(inp["attn_sinks"], np.float32).reshape(1, NHEAD),
                idn=np.eye(128, dtype=np.float32))


def core_inputs(shared, x_b, c_b):
    pp = shared["ppbase"].copy()
    pp[:, PC_C:PC_C + 16] = np.asarray(c_b, np.float32).reshape(16, 128).T
    return {"xT": np.ascontiguousarray(np.asarray(x_b, np.float32).T), "wblk": shared["wblk"], "pp": pp,
            "lruw": shared["lruw"], "biasg": shared["biasg"], "maskc": shared["maskc"], "sinks": shared["sinks"],
            "idn": shared["idn"]}


_NC_CACHE = {}


def kernel(**inputs):
    x = np.asarray(inputs["x"])
    c = np.asarray(inputs["c"])
    shared = prep_shared(inputs)
    if 8 not in _NC_CACHE:
        _NC_CACHE[8] = build(8)
    nc = _NC_CACHE[8]
    in_maps = [core_inputs(shared, x[b], c[b]) for b in range(8)]
    res = run_bass_kernel_spmd(nc, in_maps, core_ids=list(range(8)))
    out = np.stack([np.asarray(r["outT"]).T for r in res.results], axis=0)
    return np.ascontiguousarray(out.astype(np.float32))
```
